# Optimizing a Trainium2 kernel written in Bass

```python
import math
import jax, jax.numpy as jnp
from jax import lax
import numpy as np

D_MODEL = 4096
BATCH = 2
SEQ = 4096
DEPTH = 2

N_MIXERS = 2
N_ATTN = (DEPTH + 1) // 2
N_SSM = DEPTH // 2

N_HEADS = 64
N_KV = 8
GROUP = N_HEADS // N_KV
HEAD_DIM = 64
Q_W = N_HEADS * HEAD_DIM
KV_W = N_KV * HEAD_DIM
WINDOW = 128
BLOCK = 128

NUM_BUCKETS = 32
MAX_EXACT = NUM_BUCKETS // 2
MAX_DISTANCE = 128

SSM_GC = 16
SSM_G = D_MODEL // SSM_GC
SSM_P = 64
DT_MIN = 1e-3
DT_MAX = 1e-1

D_FF = 11008
CONV_W = 3

EPS = 1e-6
NEG = -1e30

kernel_name = "hybrid_swa_sink_s5_convffn"


def rmsnorm(x, g):
    x32 = x.astype(jnp.float32)
    y = x32 * lax.rsqrt(jnp.mean(x32 * x32, axis=-1, keepdims=True) + EPS)
    return (y * g.astype(jnp.float32)).astype(x.dtype)


def t5_buckets(dist):
    n = np.maximum(dist, 0)
    is_small = n < MAX_EXACT
    large = MAX_EXACT + (np.log(np.maximum(n, 1) / MAX_EXACT) / np.log(MAX_DISTANCE / MAX_EXACT)
                         * (NUM_BUCKETS - MAX_EXACT)).astype(np.int32)
    large = np.minimum(large, NUM_BUCKETS - 1)
    return np.where(is_small, n, large).astype(np.int32)


def swa_sink_attention(h, w_qkv, w_o, sinks, rel_bias):
    b, L, _ = h.shape
    nb = L // BLOCK
    dt = h.dtype
    qkv = h @ w_qkv
    q = qkv[..., :Q_W].reshape(b, nb, BLOCK, N_KV, GROUP, HEAD_DIM)
    k = qkv[..., Q_W:Q_W + KV_W].reshape(b, nb, BLOCK, N_KV, HEAD_DIM)
    v = qkv[..., Q_W + KV_W:].reshape(b, nb, BLOCK, N_KV, HEAD_DIM)
    pad = ((0, 0), (1, 0), (0, 0), (0, 0), (0, 0))
    kb = jnp.concatenate([jnp.pad(k, pad)[:, :-1], k], axis=2)
    vb = jnp.concatenate([jnp.pad(v, pad)[:, :-1], v], axis=2)
    s = jnp.einsum('bnqkgd,bnskd->bnkgqs', q, kb).astype(jnp.float32) * (HEAD_DIM ** -0.5)
    qi = np.arange(BLOCK)[:, None] + BLOCK
    kj = np.arange(2 * BLOCK)[None, :]
    dist = qi - kj
    local = (dist >= 0) & (dist < WINDOW)
    bias = rel_bias.astype(jnp.float32)[t5_buckets(dist)]
    bias = jnp.transpose(bias, (2, 0, 1)).reshape(N_KV, GROUP, BLOCK, 2 * BLOCK)
    blk = jnp.arange(nb)[:, None, None]
    valid = jnp.asarray(local)[None] & ((blk > 0) | jnp.asarray(kj >= BLOCK)[None])
    s = jnp.where(valid[None, :, None, None], s + bias, NEG)
    sink = sinks.astype(jnp.float32).reshape(1, 1, N_KV, GROUP, 1, 1)
    m = jnp.maximum(jnp.max(s, axis=-1, keepdims=True), sink)
    p = jnp.exp(s - m)
    p = p / (jnp.sum(p, axis=-1, keepdims=True) + jnp.exp(sink - m))
    o = jnp.einsum('bnkgqs,bnskd->bnqkgd', p.astype(dt), vb)
    return o.reshape(b, L, Q_W) @ w_o


def s5_scan_combine(e1, e2):
    ar1, ai1, br1, bi1 = e1
    ar2, ai2, br2, bi2 = e2
    ar = ar2 * ar1 - ai2 * ai1
    ai = ar2 * ai1 + ai2 * ar1
    br = ar2 * br1 - ai2 * bi1 + br2
    bi = ar2 * bi1 + ai2 * br1 + bi2
    return (ar, ai, br, bi)


def s5_layer(h, lam_re, lam_im, log_step, b_re, b_im, c_re, c_im, d_skip, w_glu):
    b, L, D = h.shape
    dt = h.dtype
    delta = jnp.exp(log_step.astype(jnp.float32))[:, None]
    lr = lam_re.astype(jnp.float32)
    li = lam_im.astype(jnp.float32)
    mag = jnp.exp(lr * delta)
    abar_r = mag * jnp.cos(li * delta)
    abar_i = mag * jnp.sin(li * delta)
    nr = abar_r - 1.0
    ni = abar_i
    den = lr * lr + li * li
    fr = ((nr * lr + ni * li) / den)[..., None]
    fi = ((ni * lr - nr * li) / den)[..., None]
    br32 = b_re.astype(jnp.float32)
    bi32 = b_im.astype(jnp.float32)
    bbar_r = (fr * br32 - fi * bi32).astype(dt)
    bbar_i = (fr * bi32 + fi * br32).astype(dt)
    u = h.reshape(b, L, SSM_G, SSM_GC)
    bu_r = jnp.einsum('blgc,gpc->blgp', u, bbar_r)
    bu_i = jnp.einsum('blgc,gpc->blgp', u, bbar_i)
    a_r = jnp.broadcast_to(abar_r.astype(dt)[None, None], (1, L, SSM_G, SSM_P))
    a_i = jnp.broadcast_to(abar_i.astype(dt)[None, None], (1, L, SSM_G, SSM_P))
    _, _, xr, xi = lax.associative_scan(s5_scan_combine, (a_r, a_i, bu_r, bu_i), axis=1)
    y = jnp.einsum('blgp,gcp->blgc', xr, c_re) - jnp.einsum('blgp,gcp->blgc', xi, c_im)
    y = y.reshape(b, L, D) + d_skip * h
    z = jax.nn.gelu(y) @ w_glu
    return z[..., :D] * jax.nn.sigmoid(z[..., D:])


def conv_gated_mlp(h, w_up, conv_w, conv_b, w_down):
    L = h.shape[1]
    u = h @ w_up
    up = jnp.pad(u, ((0, 0), (CONV_W - 1, 0), (0, 0)))
    c = conv_b + sum(conv_w[j] * up[:, j:j + L] for j in range(CONV_W))
    g = c[..., :D_FF]
    v = c[..., D_FF:]
    return (jax.nn.silu(g) * v) @ w_down


def setup_inputs(seed: int = 0) -> dict:
    key = jax.random.key(seed)
    ks = jax.random.split(key, 24)
    f32 = jnp.float32
    nrm = lambda k, shape, scale: jax.random.normal(k, shape, f32) * scale
    x = jax.random.normal(ks[0], (BATCH, SEQ, D_MODEL), f32)
    attn_norm = 1.0 + nrm(ks[1], (N_ATTN, D_MODEL), 0.02)
    w_qkv = nrm(ks[2], (N_ATTN, D_MODEL, Q_W + 2 * KV_W), D_MODEL ** -0.5)
    w_o = nrm(ks[3], (N_ATTN, Q_W, D_MODEL), Q_W ** -0.5)
    sinks = nrm(ks[4], (N_ATTN, N_HEADS), 0.5)
    rel_bias = nrm(ks[5], (NUM_BUCKETS, N_HEADS), 0.5)
    ssm_norm = 1.0 + nrm(ks[6], (N_SSM, D_MODEL), 0.02)
    n_idx = jnp.arange(SSM_P, dtype=f32)
    lambda_re = -0.5 + nrm(ks[7], (N_SSM, SSM_G, SSM_P), 0.01)
    lambda_im = math.pi * n_idx + nrm(ks[8], (N_SSM, SSM_G, SSM_P), 0.01)
    log_step = jax.random.uniform(ks[9], (N_SSM, SSM_G), f32, math.log(DT_MIN), math.log(DT_MAX))
    b_re = nrm(ks[10], (N_SSM, SSM_G, SSM_P, SSM_GC), (2 * SSM_GC) ** -0.5)
    b_im = nrm(ks[11], (N_SSM, SSM_G, SSM_P, SSM_GC), (2 * SSM_GC) ** -0.5)
    c_re = nrm(ks[12], (N_SSM, SSM_G, SSM_GC, SSM_P), (2 * SSM_P) ** -0.5)
    c_im = nrm(ks[13], (N_SSM, SSM_G, SSM_GC, SSM_P), (2 * SSM_P) ** -0.5)
    d_skip = nrm(ks[14], (N_SSM, D_MODEL), 1.0)
    w_glu = nrm(ks[15], (N_SSM, D_MODEL, 2 * D_MODEL), D_MODEL ** -0.5)
    ffn_norm = 1.0 + nrm(ks[16], (DEPTH, D_MODEL), 0.02)
    w_up = nrm(ks[17], (DEPTH, D_MODEL, 2 * D_FF), D_MODEL ** -0.5)
    conv_w = nrm(ks[18], (DEPTH, CONV_W, 2 * D_FF), CONV_W ** -0.5)
    conv_b = nrm(ks[19], (DEPTH, 2 * D_FF), 0.02)
    w_down = nrm(ks[20], (DEPTH, D_FF, D_MODEL), D_FF ** -0.5)
    final_norm = 1.0 + nrm(ks[21], (D_MODEL,), 0.02)
    return {'x': x, 'attn_norm': attn_norm, 'w_qkv': w_qkv, 'w_o': w_o, 'sinks': sinks,
            'rel_bias': rel_bias, 'ssm_norm': ssm_norm, 'lambda_re': lambda_re,
            'lambda_im': lambda_im, 'log_step': log_step, 'b_re': b_re, 'b_im': b_im,
            'c_re': c_re, 'c_im': c_im, 'd_skip': d_skip, 'w_glu': w_glu,
            'ffn_norm': ffn_norm, 'w_up': w_up, 'conv_w': conv_w, 'conv_b': conv_b,
            'w_down': w_down, 'final_norm': final_norm}


def reference(x, attn_norm, w_qkv, w_o, sinks, rel_bias, ssm_norm, lambda_re, lambda_im,
              log_step, b_re, b_im, c_re, c_im, d_skip, w_glu, ffn_norm, w_up, conv_w,
              conv_b, w_down, final_norm):
    for i in range(DEPTH):
        j = i // N_MIXERS
        if i % N_MIXERS == 0:
            x = x + swa_sink_attention(rmsnorm(x, attn_norm[j]), w_qkv[j], w_o[j], sinks[j], rel_bias)
        else:
            x = x + s5_layer(rmsnorm(x, ssm_norm[j]), lambda_re[j], lambda_im[j], log_step[j],
                             b_re[j], b_im[j], c_re[j], c_im[j], d_skip[j], w_glu[j])
        x = x + conv_gated_mlp(rmsnorm(x, ffn_norm[i]), w_up[i], conv_w[i], conv_b[i], w_down[i])
    return rmsnorm(x, final_norm)
```

```python
import contextlib
import math
import numpy as np
import concourse.bass as bass
import concourse.mybir as mybir
from concourse.bass_utils import run_bass_kernel_spmd

F32 = mybir.dt.float32
BF16 = mybir.dt.bfloat16
ALU = mybir.AluOpType
AF = mybir.ActivationFunctionType

ENGS = ("pe", "act", "dve", "pool", "sp")
NSLOT = 8
SAME_ENGINE_IN_ORDER = ("pe",)
EPS = 1e-6
NEG = -1e30


class Op:
    __slots__ = ("eng", "fn", "deps", "dma", "idx", "needed", "sem", "val", "slot")

    def __init__(self, eng, fn, dma):
        self.eng = eng
        self.fn = fn
        self.dma = dma
        self.deps = []
        self.needed = False
        self.sem = None
        self.val = 0
        self.slot = -1


class V:
    __slots__ = ("ap", "key")

    def __init__(self, ap, key):
        self.ap = ap
        self.key = key

    def v(self, f):
        return V(f(self.ap), self.key)


class Rec:
    def __init__(self):
        self.ops = {e: [] for e in ENGS}
        self.lastw = {}
        self.readers = {}
        self.seen = {e: {p: -1 for p in ENGS} for e in ENGS}
        self.seen_dma = {e: set() for e in ENGS}
        self.slot_last = {}
        self.slot_rr = {e: 0 for e in ENGS}

    def _add_dep(self, op, prod):
        if prod is None or prod is op:
            return
        e = op.eng
        if prod.dma:
            if id(prod) in self.seen_dma[e]:
                return
            self.seen_dma[e].add(id(prod))
        else:
            if prod.eng == e and e in SAME_ENGINE_IN_ORDER:
                return
            if prod.idx <= self.seen[e][prod.eng]:
                return
            self.seen[e][prod.eng] = prod.idx
        prod.needed = True
        op.deps.append(prod)

    def op(self, eng, fn, reads=(), writes=(), dma=False, cc=False):
        o = Op(eng, fn, dma or cc)
        o.idx = len(self.ops[eng])
        for k in reads:
            self._add_dep(o, self.lastw.get(k))
        for k in writes:
            self._add_dep(o, self.lastw.get(k))
            for r in self.readers.get(k, ()):
                self._add_dep(o, r)
        if cc:
            o.slot = "cc"
            prev = self.slot_last.get(("cc", 0))
            if prev is not None:
                self._add_dep(o, prev)
            self.slot_last[("cc", 0)] = o
            o.needed = True
        elif dma:
            s = self.slot_rr[eng]
            self.slot_rr[eng] = (s + 1) % NSLOT
            o.slot = s
            prev = self.slot_last.get((eng, s))
            if prev is not None:
                self._add_dep(o, prev)
            self.slot_last[(eng, s)] = o
            o.needed = True
        for k in reads:
            self.readers.setdefault(k, []).append(o)
        for k in writes:
            self.lastw[k] = o
            self.readers[k] = []
        self.ops[eng].append(o)
        return o

    def dma(self, eng, out, in_, reads=(), writes=(), **kw):
        return self.op(eng, lambda e: e.dma_start(out=out, in_=in_, **kw), reads, writes, dma=True)

    def barrier(self):
        lasts = []
        for e in ENGS:
            for o in reversed(self.ops[e]):
                if not o.dma and o.fn is not None:
                    lasts.append(o)
                    break
        dmas = [o for o in self.slot_last.values()]
        for e in ENGS:
            o = Op(e, None, False)
            o.idx = len(self.ops[e])
            for p in lasts + dmas:
                self._add_dep(o, p)
            self.ops[e].append(o)

    def wait_keys(self, eng, keys):
        o = Op(eng, None, False)
        o.idx = len(self.ops[eng])
        for k in keys:
            self._add_dep(o, self.lastw.get(k))
        self.ops[eng].append(o)

    def emit(self, nc, sems, dsems):
        for e in ENGS:
            cnt = 0
            dcnt = {}
            for o in self.ops[e]:
                if o.dma and o.slot == "cc":
                    k = ("cc", 0)
                    dcnt[k] = dcnt.get(k, 0) + 1
                    o.sem = dsems[k]
                    o.val = dcnt[k]
                elif o.dma:
                    k = (e, o.slot)
                    dcnt[k] = dcnt.get(k, 0) + 16
                    o.sem = dsems[k]
                    o.val = dcnt[k]
                elif o.needed:
                    cnt += 1
                    o.sem = sems[e]
                    o.val = cnt
        engobj = {"pe": "tensor", "act": "scalar", "dve": "vector", "pool": "gpsimd", "sp": "sync"}
        with nc.Block() as block:
            for e in ENGS:
                ops = self.ops[e]
                if not ops:
                    continue

                def body(eng, ops=ops):
                    for o in ops:
                        for p in o.deps:
                            eng.wait_ge(p.sem, p.val)
                        if o.fn is None:
                            continue
                        ins = o.fn(eng)
                        if o.dma and o.slot == "cc":
                            ins.then_inc(o.sem)
                        elif o.dma:
                            ins.then_inc(o.sem, 16)
                        elif o.needed:
                            ins.then_inc(o.sem, 1)

                getattr(block, engobj[e])(body)


class Cfg:
    def __init__(self, D=4096, F=11008, NOWN=1024):
        self.D = D
        self.F = F
        self.NOWN = NOWN
        self.HALO = 16
        self.NT = NOWN + self.HALO
        self.KVH = 256
        self.NKV = NOWN + self.KVH
        self.KT = D // 128
        self.FT = F // 128
        self.NH = D // 64
        self.NKVH = self.NH // 8
        self.QW = D
        self.KVW = self.NKVH * 64
        self.G = D // 16
        self.FC = 8
        self.NQB = self.NKV // 128 - 1
        self.NCH = self.NT // 8


def ntiles(n):
    if n == 1040:
        return [(0, 352), (352, 352), (704, 336)]
    if n == 1280:
        return [(i * 320, 320) for i in range(4)]
    k = (n + 511) // 512
    base = (n + k - 1) // k
    base = (base + 31) // 32 * 32
    out = []
    s = 0
    while s < n:
        m = min(base, n - s)
        out.append((s, m))
        s += m
    return out


class Prog:
    def __init__(self, cfg, phases):
        self.c = cfg
        self.phases = phases
        self.nc = bass.Bass("TRN2", target_bir_lowering=False)
        self.r = Rec()
        self.din = {}
        self.dout = {}

    def inp(self, name, shape, dt=F32):
        if name in self.din:
            return self.din[name]
        t = self.nc.dram_tensor(name, list(shape), dt, kind="ExternalInput").ap()
        self.din[name] = t
        return t

    def outp(self, name, shape, dt=F32):
        t = self.nc.dram_tensor(name, list(shape), dt, kind="ExternalOutput").ap()
        self.dout[name] = t
        return t

    def scratch(self, name, shape, dt=F32):
        return self.nc.dram_tensor(name, list(shape), dt).ap()

    def sb(self, name, shape, dt):
        return self.nc.alloc_sbuf_tensor("sb_" + name, list(shape), dt)

    def setup_common(self):
        c, nc, r = self.c, self.nc, self.r
        self.ps = [nc.alloc_psum_tensor(f"ps{i}", [128, 512], F32) for i in range(8)]
        self.hreg = self.sb("hreg", [128, c.KT * c.NKV], BF16)
        self.GCOLS = 12800
        self.FCOLS = 6400
        self.greg = self.sb("greg", [128, self.GCOLS], BF16)
        self.freg = self.sb("freg", [128, self.FCOLS], F32)
        self.goff = 0
        self.foff = 0
        self.ones_bf = self.sb("ones_bf", [128, 128], BF16)
        self.xbuf = [self.sb(f"xbuf{i}", [128, c.NKV], F32) for i in range(2)]
        self.sq = [self.sb(f"sq{i}", [128, c.NKV], BF16) for i in range(2)]
        self.rstd = self.sb("rstd", [128, c.NKV], F32)
        self.NW = 3
        self.wst = [self.sb(f"wst{i}", [128, c.KT, 256], BF16) for i in range(self.NW)]
        self.wi = 0
        self.gains = self.sb("gains", [128, 5 * c.KT], F32)
        self.flag = self.sb("flag", [128, 1], F32)
        g_in = self.inp("gains", [128, 5 * c.KT])
        f_in = self.inp("flag", [128, 1])
        r.op("pool", lambda e: e.memset(self.ones_bf[:], 1.0), writes=["ones"])
        r.dma("sp", self.gains[:], g_in[:, :], writes=["gains"])
        r.dma("sp", self.flag[:], f_in[:, :], writes=["flag"])
        self.xres = self.scratch("xres", [c.D, c.NT])

    def phase_reset(self):
        self.goff = 0
        self.foff = 0

    def gc(self, cols):
        a = self.greg[:, self.goff:self.goff + cols]
        self.goff += (cols + 15) // 16 * 16
        assert self.goff <= self.GCOLS, ("greg overflow", self.goff)
        return a

    def fc(self, cols):
        a = self.freg[:, self.foff:self.foff + cols]
        self.foff += (cols + 7) // 8 * 8
        assert self.foff <= self.FCOLS, ("freg overflow", self.foff)
        return a

    def defer(self, fn):
        if not hasattr(self, "pending"):
            self.pending = []
        self.pending.append(fn)

    def flush(self):
        for fn in getattr(self, "pending", []):
            fn()
        self.pending = []

    def nextw(self):
        b = self.wi % self.NW
        self.wi += 1
        return b

    def hview(self, ncols):
        c = self.c
        return self.hreg[:, 0:c.KT * ncols].rearrange("p (k n) -> p k n", k=c.KT)

    def rmsnorm(self, src, src_key, col0, ncols, gcol, hT, hkey, use_flag, s_major=False):
        c, r = self.c, self.r
        nts = ntiles(ncols)
        ps = self.ps
        for k in range(c.KT):
            xb = self.xbuf[k % 2]
            sq = self.sq[k % 2]
            r.dma("sp", xb[:, 0:ncols], src[k * 128:(k + 1) * 128, col0:col0 + ncols],
                  reads=[f"{src_key}{k}"], writes=[f"xbuf{k%2}"])
            r.op("act", lambda e, xb=xb, sq=sq: e.activation(out=sq[:, 0:ncols], in_=xb[:, 0:ncols], func=AF.Square),
                 reads=[f"xbuf{k%2}"], writes=[f"sq{k%2}"])
            for i, (s, n) in enumerate(nts):
                r.op("pe", lambda e, i=i, s=s, n=n, sq=sq, k=k: e.matmul(
                    ps[i][:, 0:n], lhsT=self.ones_bf[:, :], rhs=sq[:, s:s + n],
                    start=(k == 0), stop=(k == c.KT - 1)),
                    reads=[f"sq{k%2}", "ones"], writes=[f"ps{i}"])
        for i, (s, n) in enumerate(nts):
            r.op("dve", lambda e, i=i, s=s, n=n: e.tensor_scalar(
                out=self.rstd[:, s:s + n], in0=ps[i][:, 0:n], scalar1=1.0 / c.D, scalar2=EPS,
                op0=ALU.mult, op1=ALU.add), reads=[f"ps{i}"], writes=["rstd"])
        r.op("act", lambda e: e.sqrt(out=self.rstd[:, 0:ncols], in_=self.rstd[:, 0:ncols]),
             reads=["rstd"], writes=["rstd"])
        r.op("dve", lambda e: e.reciprocal(out=self.rstd[:, 0:ncols], in_=self.rstd[:, 0:ncols]),
             reads=["rstd"], writes=["rstd"])
        if use_flag:
            r.op("dve", lambda e: e.tensor_scalar(
                out=self.rstd[:, 0:c.HALO], in0=self.rstd[:, 0:c.HALO], scalar1=self.flag[:, 0:1],
                scalar2=None, op0=ALU.mult), reads=["rstd", "flag"], writes=["rstd"])
        for k in range(c.KT):
            xb = self.xbuf[k % 2]
            r.dma("sp", xb[:, 0:ncols], src[k * 128:(k + 1) * 128, col0:col0 + ncols],
                  reads=[f"{src_key}{k}"], writes=[f"xbuf{k%2}"])
            if s_major:
                hs8 = self.hs8()
                r.op("dve", lambda e, xb=xb, k=k, hs8=hs8: e.scalar_tensor_tensor(
                    out=hs8[:, k, :, 0:c.NCH].rearrange("p s j -> p j s"),
                    in0=xb[:, 0:ncols].rearrange("p (j s) -> p j s", s=8),
                    scalar=self.gains[:, gcol + k:gcol + k + 1],
                    in1=self.rstd[:, 0:ncols].rearrange("p (j s) -> p j s", s=8), op0=ALU.mult, op1=ALU.mult),
                    reads=[f"xbuf{k%2}", "rstd", "gains"], writes=[hkey])
                continue
            r.op("dve", lambda e, xb=xb, k=k: e.scalar_tensor_tensor(
                out=hT[:, k, :], in0=xb[:, 0:ncols], scalar=self.gains[:, gcol + k:gcol + k + 1],
                in1=self.rstd[:, 0:ncols], op0=ALU.mult, op1=ALU.mult),
                reads=[f"xbuf{k%2}", "rstd", "gains"], writes=[hkey])

    def load_w(self, buf_i, w, col0, ncols=128, dstcol=0, eng="pool"):
        wb = self.wst[buf_i]
        self.r.dma(eng, wb[:, :, dstcol:dstcol + ncols],
                   w[:, col0:col0 + ncols].rearrange("(k p) m -> p k m", p=128),
                   writes=[f"wst{buf_i}"])
        self.flush()
        return wb

    def linear_tile(self, wb, wkey, hT, hkey, nts, bank0, col_off=0, wc0=0):
        c, r = self.c, self.r
        for k in range(c.KT):
            for i, (s, n) in enumerate(nts):
                r.op("pe", lambda e, i=i, s=s, n=n, k=k: e.matmul(
                    self.ps[bank0 + i][:, 0:n], lhsT=wb[:, k, wc0:wc0 + 128], rhs=hT[:, k, col_off + s:col_off + s + n],
                    start=(k == 0), stop=(k == c.KT - 1)),
                    reads=[wkey, hkey], writes=[f"ps{bank0+i}"])

    def attention(self):
        c, nc, r = self.c, self.nc, self.r
        ps = self.ps
        xT0 = self.inp("xT0", [c.D, c.NKV])
        w_qkv = self.inp("w_qkv", [c.D, c.QW + 2 * c.KVW])
        w_o = self.inp("w_o", [c.QW, c.D])
        bias_tab = self.inp("bias_tab", [c.NH // 2, 128, 512])
        sinks2_in = self.inp("sinks2", [128, c.NH // 2])
        negmask_in = self.inp("negmask", [128, 128])
        oT_d = self.scratch("oT_d", [c.D, c.NQB * 128], BF16)
        nblk = c.NKV // 128

        self.phase_reset()
        kdup = [self.gc(c.NKV) for i in range(2)]
        vtok = [self.gc(nblk * 64).rearrange("p (a n) -> p a n", a=nblk) for i in range(2)]
        qT = [self.gc(c.NKV) for i in range(2)]
        pbf = [self.gc(512) for i in range(2)]
        otile = [self.gc(c.NQB * 128) for i in range(2)]
        biasT = [self.fc(512) for i in range(2)]
        esb = [self.fc(512) for i in range(2)]
        esink = self.fc(c.NH // 2)
        negmask = self.fc(128)
        rd = self.fc(128)

        r.dma("sp", esink, sinks2_in[:, :], writes=["esink"])
        r.op("act", lambda e: e.activation(out=esink, in_=esink, func=AF.Exp), reads=["esink"], writes=["esink"])
        r.dma("sp", negmask, negmask_in[:, :], writes=["negmask"])

        hT = self.hview(c.NKV)
        self.rmsnorm(xT0, "xT0", 0, c.NKV, 0 * c.KT, hT, "hT", use_flag=False)
        nts = ntiles(c.NKV)
        it_ctr = 0
        for kvh in range(c.NKVH):
            kd = kdup[kvh % 2]
            kk = f"kdup{kvh%2}"
            vt = vtok[kvh % 2]
            vk = f"vtok{kvh%2}"
            b = self.nextw()
            wb = self.wst[b]
            self.load_w(b, w_qkv, c.QW + kvh * 64, 64, 0)
            self.load_w(b, w_qkv, c.QW + kvh * 64, 64, 64)
            self.load_w(b, w_qkv, c.QW + c.KVW + kvh * 64, 64, 128)
            self.linear_tile(wb, f"wst{b}", hT, "hT", nts, 0, wc0=0)
            for i, (s, n) in enumerate(nts):
                r.op("act", lambda e, i=i, s=s, n=n, kd=kd: e.copy(out=kd[:, s:s + n], in_=ps[i][:, 0:n]),
                     reads=[f"ps{i}"], writes=[kk])
            for blk in range(nblk):
                bank = 4 if blk % 2 == 0 else 5
                for k in range(c.KT):
                    r.op("pe", lambda e, blk=blk, k=k, bank=bank, wb=wb: e.matmul(
                        ps[bank][:, 0:64], lhsT=hT[:, k, blk * 128:(blk + 1) * 128], rhs=wb[:, k, 128:192],
                        start=(k == 0), stop=(k == c.KT - 1)), reads=["hT", f"wst{b}"], writes=[f"ps{bank}"])
                r.op("act", lambda e, blk=blk, bank=bank, vt=vt: e.copy(out=vt[:, blk, :], in_=ps[bank][:, 0:64]),
                     reads=[f"ps{bank}"], writes=[vk])
            for jp in range(2):
                bq = self.nextw()
                wq = self.wst[bq]
                j0 = kvh * 4 + jp * 2
                self.load_w(bq, w_qkv, j0 * 128, 256, 0)
                for jj in range(2):
                    j = j0 + jj
                    q = qT[j % 2]
                    qk = f"qT{j%2}"
                    self.linear_tile(wq, f"wst{bq}", hT, "hT", nts, 0, wc0=jj * 128)
                    for i, (s, n) in enumerate(nts):
                        r.op("act", lambda e, i=i, s=s, n=n, q=q: e.copy(out=q[:, s:s + n], in_=ps[i][:, 0:n]),
                             reads=[f"ps{i}"], writes=[qk])
                    bt = biasT[j % 2]
                    r.dma("sp", bt, bias_tab[j, :, :], writes=[f"biasT{j%2}"])
                    ot = otile[j % 2]
                    otk = f"otile{j%2}"
                    for qb in range(1, c.NQB + 1):
                        it = it_ctr % 2
                        it_ctr += 1
                        scs = [ps[6], ps[7]]
                        od = ps[4 + it]
                        odk = f"ps{4+it}"
                        e_sb = esb[it]
                        p_sb = pbf[it]
                        for hh in range(2):
                            for kb in range(2):
                                kblk = qb - 1 + kb
                                r.op("pe", lambda e, hh=hh, kb=kb, kblk=kblk, q=q, qb=qb, kd=kd, scs=scs: e.matmul(
                                    scs[hh][:, kb * 128:(kb + 1) * 128],
                                    lhsT=kd[hh * 64:(hh + 1) * 64, kblk * 128:(kblk + 1) * 128],
                                    rhs=q[hh * 64:(hh + 1) * 64, qb * 128:(qb + 1) * 128],
                                    start=True, stop=True), reads=[kk, qk], writes=[f"ps{6+hh}"])
                        for hh in range(2):
                            r.op("dve", lambda e, e_sb=e_sb, bt=bt, hh=hh, scs=scs: e.scalar_tensor_tensor(
                                out=e_sb[:, hh * 256:(hh + 1) * 256], in0=scs[hh][:, 0:256], scalar=0.125,
                                in1=bt[:, hh * 256:(hh + 1) * 256], op0=ALU.mult, op1=ALU.add),
                                reads=[f"ps{6+hh}", f"biasT{j%2}"], writes=[f"esb{it}"])
                        if qb == 2:
                            for hh in range(2):
                                r.op("dve", lambda e, e_sb=e_sb, hh=hh: e.tensor_tensor(
                                    out=e_sb[:, hh * 256:hh * 256 + 128], in0=e_sb[:, hh * 256:hh * 256 + 128],
                                    in1=negmask, op=ALU.add), reads=[f"esb{it}", "negmask"], writes=[f"esb{it}"])
                        r.op("act", lambda e, e_sb=e_sb, p_sb=p_sb: e.activation(out=p_sb, in_=e_sb, func=AF.Exp),
                             reads=[f"esb{it}"], writes=[f"pbf{it}"])
                        for hh in range(2):
                            for kb in range(2):
                                r.op("pe", lambda e, hh=hh, kb=kb, p_sb=p_sb, od=od: e.matmul(
                                    od[hh * 64:(hh + 1) * 64, 0:128], lhsT=self.ones_bf[:, 0:64],
                                    rhs=p_sb[:, (hh * 2 + kb) * 128:(hh * 2 + kb + 1) * 128],
                                    start=(kb == 0), stop=(kb == 1)), reads=[f"pbf{it}", "ones"], writes=[odk])
                        for hh in range(2):
                            for kb in range(2):
                                kblk = qb - 1 + kb
                                r.op("pe", lambda e, hh=hh, kb=kb, kblk=kblk, p_sb=p_sb, od=od, vt=vt: e.matmul(
                                    od[hh * 64:(hh + 1) * 64, 128:256], lhsT=vt[:, kblk, :],
                                    rhs=p_sb[:, (hh * 2 + kb) * 128:(hh * 2 + kb + 1) * 128],
                                    start=(kb == 0), stop=(kb == 1)), reads=[f"pbf{it}", vk], writes=[odk])
                        r.op("dve", lambda e, j=j, od=od: e.tensor_scalar(
                            out=rd, in0=od[:, 0:128], scalar1=esink[:, j:j + 1], scalar2=None, op0=ALU.add),
                            reads=[odk, "esink"], writes=["rd"])
                        r.op("dve", lambda e: e.reciprocal(out=rd, in_=rd), reads=["rd"], writes=["rd"])
                        r.op("dve", lambda e, ot=ot, qb=qb, od=od: e.tensor_tensor(
                            out=ot[:, (qb - 1) * 128:qb * 128], in0=od[:, 128:256], in1=rd, op=ALU.mult),
                            reads=[odk, "rd"], writes=[otk])
                    r.dma("sp", oT_d[j * 128:(j + 1) * 128, :], ot, reads=[otk], writes=["oT_d"])

        r.barrier()
        oT = self.hview(c.NT)
        ocol0 = c.NQB * 128 - c.NT
        for k in range(c.KT):
            r.dma("sp", oT[:, k, :], oT_d[k * 128:(k + 1) * 128, ocol0:ocol0 + c.NT], reads=["oT_d"], writes=["oT"])
        nt3 = ntiles(c.NT)
        xcol0 = c.NKV - c.NT
        for mp in range(c.KT // 2):
            b = self.nextw()
            self.load_w(b, w_o, mp * 256, 256, 0)
            for half in range(2):
                m = mp * 2 + half
                bank0 = (m % 2) * 3
                self.linear_tile(self.wst[b], f"wst{b}", oT, "oT", nt3, bank0, wc0=half * 128)
                xb = self.xbuf[m % 2]
                r.dma("sp", xb[:, 0:c.NT], xT0[m * 128:(m + 1) * 128, xcol0:xcol0 + c.NT], reads=["xT0"],
                      writes=[f"xbuf{m%2}"])
                for i, (s, n) in enumerate(nt3):
                    r.op("dve", lambda e, i=i, s=s, n=n, xb=xb, bank0=bank0: e.tensor_tensor(
                        out=xb[:, s:s + n], in0=ps[bank0 + i][:, 0:n], in1=xb[:, s:s + n], op=ALU.add),
                        reads=[f"ps{bank0+i}", f"xbuf{m%2}"], writes=[f"xbuf{m%2}"])
                r.dma("sp", self.xres[m * 128:(m + 1) * 128, :], xb[:, 0:c.NT], reads=[f"xbuf{m%2}"], writes=[f"xres{m}"])
        r.barrier()

    def ffn(self, li):
        c, nc, r = self.c, self.nc, self.r
        ps = self.ps
        w_up = self.inp(f"w_up{li}", [c.D, 2 * c.F])
        w_down = self.inp(f"w_down{li}", [c.F, c.D])
        convp_in = self.inp(f"convp{li}", [128, 4 * 2 * c.FT])
        self.phase_reset()
        if not hasattr(self, "convp_sb"):
            self.convp_sb = self.sb("convp", [128, 4 * 2 * c.FT], F32)
        convp = self.convp_sb
        act = [self.gc(c.FC * c.NT).rearrange("p (a n) -> p a n", a=c.FC)] * 2
        wdn = [self.gc(c.FC * 256).rearrange("p (a n) -> p a n", a=c.FC) for i in range(2)]
        u = [self.fc(c.NT) for i in range(2)]
        cc = [self.fc(c.NT) for i in range(2)]
        sg = self.fc(c.NT)
        yst = [self.fc(c.NT)] * 2
        r.dma("sp", convp[:], convp_in[:, :], writes=["convp"])
        cp = convp[:, :].rearrange("p (j f) -> p j f", j=4)

        hT = self.hview(c.NT)
        self.rmsnorm(self.xres, "xres", 0, c.NT, (1 if li == 0 else 3) * c.KT, hT, "hT", use_flag=True)
        nt3 = ntiles(c.NT)
        NT = c.NT
        chunks = [list(range(s, min(s + c.FC, c.FT))) for s in range(0, c.FT, c.FC)]
        evi = 0
        sti = 0
        stg = [(u[0], "u0"), (u[1], "u1"), (cc[0], "cc0"), (cc[1], "cc1"), (sg, "sg"), (yst[0], "yst")]
        for ci, chunk in enumerate(chunks):
            actb = act[0]
            ak = "actb"
            wtiles = {}
            for li_, f in enumerate(chunk):
                for which in range(2):
                    if li_ % 2 == 0:
                        b = self.nextw()
                        ncol = 256 if li_ + 1 < len(chunk) else 128
                        self.load_w(b, w_up, which * c.F + f * 128, ncol, 0)
                        wtiles[which] = b
                    b = wtiles[which]
                    bank0 = which * 3
                    self.linear_tile(self.wst[b], f"wst{b}", hT, "hT", nt3, bank0, wc0=(li_ % 2) * 128)
                    uu = u[which]
                    cw = cc[which]
                    col = which * c.FT + f
                    for i, (s, n) in enumerate(nt3):
                        r.op("act", lambda e, i=i, s=s, n=n, uu=uu, bank0=bank0: e.copy(
                            out=uu[:, s:s + n], in_=ps[bank0 + i][:, 0:n]),
                            reads=[f"ps{bank0+i}"], writes=[f"u{which}"])
                    r.op("act", lambda e, uu=uu, cw=cw, col=col: e.activation(
                        out=cw[:, :], in_=uu[:, :], func=AF.Identity, scale=cp[:, 2, col:col + 1],
                        bias=cp[:, 3, col:col + 1]), reads=[f"u{which}", "convp"], writes=[f"cc{which}"])
                    r.op("dve", lambda e, uu=uu, cw=cw, col=col: e.scalar_tensor_tensor(
                        out=cw[:, 1:NT], in0=uu[:, 0:NT - 1], scalar=cp[:, 1, col:col + 1], in1=cw[:, 1:NT],
                        op0=ALU.mult, op1=ALU.add), reads=[f"u{which}", f"cc{which}", "convp"], writes=[f"cc{which}"])
                    r.op("dve", lambda e, uu=uu, cw=cw, col=col: e.scalar_tensor_tensor(
                        out=cw[:, 2:NT], in0=uu[:, 0:NT - 2], scalar=cp[:, 0, col:col + 1], in1=cw[:, 2:NT],
                        op0=ALU.mult, op1=ALU.add), reads=[f"u{which}", f"cc{which}", "convp"], writes=[f"cc{which}"])
                r.op("act", lambda e: e.activation(out=sg[:, :], in_=cc[0][:, :], func=AF.Silu),
                     reads=["cc0"], writes=["sg"])
                r.op("dve", lambda e, actb=actb, li_=li_: e.tensor_tensor(
                    out=actb[:, li_, :], in0=sg[:, :], in1=cc[1][:, :], op=ALU.mult),
                    reads=["sg", "cc1"], writes=[ak])
            nf = len(chunk)
            f0 = chunk[0]
            for mp in range(c.KT // 2):
                wd = wdn[mp % 2]
                wk = f"wdn{mp%2}"
                r.dma("pool", wd[:, 0:nf, :],
                      w_down[f0 * 128:(f0 + nf) * 128, mp * 256:(mp + 1) * 256].rearrange("(k p) m -> p k m", p=128),
                      writes=[wk])
                self.flush()
                for half in range(2):
                    m = mp * 2 + half
                    bank0 = (m % 2) * 3
                    for q in range(nf):
                        for i, (s, n) in enumerate(nt3):
                            r.op("pe", lambda e, i=i, s=s, n=n, q=q, wd=wd, half=half, bank0=bank0, actb=actb, nf=nf: e.matmul(
                                ps[bank0 + i][:, 0:n], lhsT=wd[:, q, half * 128:(half + 1) * 128],
                                rhs=actb[:, q, s:s + n], start=(q == 0), stop=(q == nf - 1)),
                                reads=[wk, ak], writes=[f"ps{bank0+i}"])
                    st, stk = stg[sti % len(stg)]
                    sti += 1
                    for i, (s, n) in enumerate(nt3):
                        eng = "act" if (evi % 2 == 0) else "dve"
                        evi += 1
                        if eng == "act":
                            r.op("act", lambda e, i=i, s=s, n=n, st=st, bank0=bank0: e.copy(
                                out=st[:, s:s + n], in_=ps[bank0 + i][:, 0:n]),
                                reads=[f"ps{bank0+i}"], writes=[stk])
                        else:
                            r.op("dve", lambda e, i=i, s=s, n=n, st=st, bank0=bank0: e.tensor_copy(
                                out=st[:, s:s + n], in_=ps[bank0 + i][:, 0:n]),
                                reads=[f"ps{bank0+i}"], writes=[stk])
                    self.defer(lambda m=m, st=st, stk=stk: r.dma(
                        "pool", self.xres[m * 128:(m + 1) * 128, :], st[:, :], reads=[stk],
                        writes=[f"xres{m}"], accum_op=ALU.add))
        self.flush()
        r.barrier()

    JP = 144

    def hs8(self):
        c = self.c
        return self.hreg[:, 0:c.KT * 8 * self.JP].rearrange("p (k s j) -> p k s j", k=c.KT, s=8)

    def vb(self, ap, key):
        return V(ap, key)

    def fb(self, n, key, rows=None):
        ap = self.fc(n)
        if rows is not None:
            ap = ap[0:rows, :]
        return V(ap, key)

    def tt(self, o, a, b, op, eng="dve"):
        self.r.op(eng, lambda e: e.tensor_tensor(out=o.ap, in0=a.ap, in1=b.ap, op=op),
                  reads=[a.key, b.key], writes=[o.key])

    def ts(self, o, a, s1, op0, s2=None, op1=None, eng="dve"):
        rd = [a.key]
        v1 = s1
        if isinstance(s1, V):
            rd.append(s1.key)
            v1 = s1.ap
        if op1 is None:
            self.r.op(eng, lambda e: e.tensor_scalar(out=o.ap, in0=a.ap, scalar1=v1, scalar2=None, op0=op0),
                      reads=rd, writes=[o.key])
        else:
            self.r.op(eng, lambda e: e.tensor_scalar(out=o.ap, in0=a.ap, scalar1=v1, scalar2=s2, op0=op0, op1=op1),
                      reads=rd, writes=[o.key])

    def stt(self, o, a, sc, b, op0, op1, eng="dve"):
        rd = [a.key, b.key]
        v = sc
        if isinstance(sc, V):
            rd.append(sc.key)
            v = sc.ap
        self.r.op(eng, lambda e: e.scalar_tensor_tensor(out=o.ap, in0=a.ap, scalar=v, in1=b.ap, op0=op0, op1=op1),
                  reads=rd, writes=[o.key])

    def actv(self, o, a, func):
        self.r.op("act", lambda e: e.activation(out=o.ap, in_=a.ap, func=func), reads=[a.key], writes=[o.key])

    def cpy(self, o, a, eng="dve"):
        if eng == "act":
            self.r.op("act", lambda e: e.copy(out=o.ap, in_=a.ap), reads=[a.key], writes=[o.key])
        else:
            self.r.op(eng, lambda e: e.tensor_copy(out=o.ap, in_=a.ap), reads=[a.key], writes=[o.key])

    def recip(self, o, a):
        self.r.op("dve", lambda e: e.reciprocal(out=o.ap, in_=a.ap), reads=[a.key], writes=[o.key])

    def mset(self, o, val, eng="dve"):
        self.r.op(eng, lambda e: e.memset(o.ap, val), writes=[o.key])

    def sincos(self, arg, sn, cs, tmp, tmpi, phase_col=None):
        OFF = 64.0 * math.pi
        TWO_PI = 2.0 * math.pi
        if phase_col is None:
            self.ts(arg, arg, OFF, ALU.add)
        else:
            self.ts(arg, arg, phase_col, ALU.add)
        self.ts(tmp, arg, 1.0 / TWO_PI, ALU.mult)
        self.cpy(tmpi, tmp)
        self.cpy(tmp, tmpi)
        self.stt(arg, tmp, -TWO_PI, arg, ALU.mult, ALU.add)

        def wrap(hi):
            if hi:
                self.ts(tmp, arg, math.pi, ALU.is_gt, -TWO_PI, ALU.mult)
            else:
                self.ts(tmp, arg, -math.pi, ALU.is_lt, TWO_PI, ALU.mult)
            self.tt(arg, arg, tmp, ALU.add)
        wrap(True)
        wrap(False)
        self.actv(sn, arg, AF.Sin)
        if cs is not None:
            self.ts(arg, arg, 0.5 * math.pi, ALU.add)
            wrap(True)
            self.actv(cs, arg, AF.Sin)

    def ssm_setup(self):
        c, r = self.c, self.r
        if hasattr(self, "ssm_in"):
            return
        G, KT = c.G, c.KT
        Q = G // 2
        d = {}
        d["ssmc"] = self.inp("ssmc", [128, 32])
        d["ident"] = self.inp("ident", [128, 128])
        for nm in ("lamr1", "lami1", "bre1", "bim1"):
            d[nm] = self.inp(nm, [128, KT * 64])
        d["lstep1"] = self.inp("lstep1", [128, KT])
        for nm in ("lamr2", "lami2", "lstep2"):
            d[nm] = self.inp(nm, [128, G])
        for nm in ("CA2", "CB2", "BA2", "BB2"):
            d[nm] = self.inp(nm, [128, G * 16])
        d["lamrS"] = self.inp("lamrS", [Q, 128])
        d["lamiS"] = self.inp("lamiS", [Q, 128])
        d["lstepS"] = self.inp("lstepS", [Q, 2])
        d["dskip"] = self.inp("dskip", [128, KT])
        self.ssm_in = d
        self.E_d = self.scratch("E_d", [c.NCH, G * 128])
        self.X_d = self.scratch("X_d", [c.NCH, G * 128], BF16)
        self.ssmc = self.sb("ssmc", [128, 32], F32)
        self.ident = self.sb("ident", [128, 128], F32)
        self.identb = self.sb("identb", [128, 128], BF16)
        self.dskip = self.sb("dskip", [128, KT], F32)
        r.dma("sp", self.ssmc[:], d["ssmc"][:, :], writes=["ssmc"])
        r.dma("sp", self.ident[:], d["ident"][:, :], writes=["ident"])
        r.dma("sp", self.dskip[:], d["dskip"][:, :], writes=["dskip"])
        r.op("dve", lambda e: e.tensor_copy(out=self.identb[:], in_=self.ident[:]), reads=["ident"], writes=["identb"])

    def cabar_f(self, lamr, lami, lrd, lid, n, pre, rows=None):
        B = lambda k: self.fb(n, pre + k, rows)
        arg, tmp, sn, cs, mg, den, fr, fi = B("arg"), B("tmp"), B("sn"), B("cs"), B("mg"), B("den"), B("fr"), B("fi")
        tmpi = V(self.fc(n).bitcast(mybir.dt.int32) if rows is None else self.fc(n)[0:rows, :].bitcast(mybir.dt.int32),
                 pre + "tmpi")
        self.actv(mg, lrd, AF.Exp)
        self.cpy(arg, lid)
        self.sincos(arg, sn, cs, tmp, tmpi)
        self.tt(cs, cs, mg, ALU.mult)
        self.tt(sn, sn, mg, ALU.mult)
        self.ts(cs, cs, -1.0, ALU.add)
        self.tt(den, lamr, lamr, ALU.mult)
        self.tt(tmp, lami, lami, ALU.mult)
        self.tt(den, den, tmp, ALU.add)
        self.recip(den, den)
        self.tt(fr, cs, lamr, ALU.mult)
        self.tt(tmp, sn, lami, ALU.mult)
        self.tt(fr, fr, tmp, ALU.add)
        self.tt(fr, fr, den, ALU.mult)
        self.tt(fi, sn, lamr, ALU.mult)
        self.tt(tmp, cs, lami, ALU.mult)
        self.tt(fi, fi, tmp, ALU.subtract)
        self.tt(fi, fi, den, ALU.mult)
        return fr, fi

    def ssm_E(self):
        c, r = self.c, self.r
        ps = self.ps
        self.ssm_setup()
        d = self.ssm_in
        KT, JP, NCH = c.KT, self.JP, c.NCH
        self.phase_reset()
        hs8 = self.hs8()
        r.op("pool", lambda e: e.memset(self.hreg[:, 0:KT * 8 * JP], 0.0), writes=["hT"])
        self.rmsnorm(self.xres, "xres", 0, c.NT, 2 * KT, None, "hT", use_flag=True, s_major=True)

        BD = self.gc(8 * 1024).rearrange("p (s g x) -> p s g x", s=8, g=8)
        ssmc = V(self.ssmc[:, :], "ssmc")
        pw = ssmc.v(lambda a: a[:, 8:16].unsqueeze(2).broadcast_to([128, 8, 64]))
        mask4 = ssmc.v(lambda a: a[:, 0:8].unsqueeze(1).unsqueeze(3).broadcast_to([128, 8, 8, 64]))
        dl1 = self.fb(KT, "dl1")
        r.dma("sp", dl1.ap, d["lstep1"][:, :], writes=["dl1"])
        self.actv(dl1, dl1, AF.Exp)
        B64 = lambda k: self.fb(64, "E" + k)
        lamr, lami, bre, bim, lrd, lid = B64("lamr"), B64("lami"), B64("bre"), B64("bim"), B64("lrd"), B64("lid")
        bbr, bbi, t64 = B64("bbr"), B64("bbi"), B64("t64")
        B512 = lambda k: self.fb(512, "E" + k)
        ARG, MAG, SN, CS, W1, W2, TMP = B512("ARG"), B512("MAG"), B512("SN"), B512("CS"), B512("W1"), B512("W2"), B512("TMP")
        TMPI = V(self.fc(512).bitcast(mybir.dt.int32), "ETMPI")
        v3 = lambda a: a.rearrange("p (s x) -> p s x", s=8)
        b3 = lambda a: a.unsqueeze(1).broadcast_to([128, 8, 64])
        foff0 = self.foff
        rows_of = [(0, 64), (64, NCH - 64)]
        for dt in range(KT):
            self.foff = foff0
            for nm, tl in (("lamr1", lamr), ("lami1", lami), ("bre1", bre), ("bim1", bim)):
                r.dma("sp", tl.ap, d[nm][:, dt * 64:(dt + 1) * 64], writes=[tl.key])
            dcol = dl1.v(lambda a, dt=dt: a[:, dt:dt + 1])
            self.ts(lrd, lamr, dcol, ALU.mult)
            self.ts(lid, lami, dcol, ALU.mult)
            self.tt(MAG.v(v3), lrd.v(b3), pw, ALU.mult)
            self.actv(MAG, MAG, AF.Exp)
            fr, fi = self.cabar_f(lamr, lami, lrd, lid, 64, "Ef")
            self.tt(bbr, fr, bre, ALU.mult)
            self.tt(t64, fi, bim, ALU.mult)
            self.tt(bbr, bbr, t64, ALU.subtract)
            self.tt(bbi, fr, bim, ALU.mult)
            self.tt(t64, fi, bre, ALU.mult)
            self.tt(bbi, bbi, t64, ALU.add)
            self.tt(ARG.v(v3), lid.v(b3), pw, ALU.mult)
            self.sincos(ARG, SN, CS, TMP, TMPI)
            self.tt(CS, CS, MAG, ALU.mult)
            self.tt(SN, SN, MAG, ALU.mult)
            self.tt(W1.v(v3), CS.v(v3), bbr.v(b3), ALU.mult)
            self.tt(TMP.v(v3), SN.v(v3), bbi.v(b3), ALU.mult)
            self.tt(W1, W1, TMP, ALU.subtract)
            self.tt(W2.v(v3), CS.v(v3), bbi.v(b3), ALU.mult)
            self.tt(TMP.v(v3), SN.v(v3), bbr.v(b3), ALU.mult)
            self.tt(W2, W2, TMP, ALU.add)
            for comp, W in ((0, W1), (1, W2)):
                self.tt(V(BD[:, :, :, comp * 64:(comp + 1) * 64], "BD"),
                        W.v(lambda a: v3(a).unsqueeze(2).broadcast_to([128, 8, 8, 64])), mask4, ALU.mult, eng="pool")
            for jb, (j0, nr) in enumerate(rows_of):
                est = self.xbuf[jb]
                for s_ in range(8):
                    for half in range(2):
                        bank = jb * 2 + half
                        r.op("pe", lambda e, s_=s_, j0=j0, nr=nr, half=half, bank=bank, dt=dt: e.matmul(
                            ps[bank][0:nr, 0:512], lhsT=hs8[:, dt, s_, j0:j0 + nr],
                            rhs=BD[:, s_, half * 4:(half + 1) * 4, :].rearrange("p g x -> p (g x)"),
                            start=(s_ == 0), stop=(s_ == 7)), reads=["hT", "BD"], writes=[f"ps{bank}"])
                for half in range(2):
                    bank = jb * 2 + half
                    self.cpy(V(est[0:nr, half * 512:(half + 1) * 512], f"xbuf{jb}"),
                             V(ps[bank][0:nr, 0:512], f"ps{bank}"), eng=("act" if half == 0 else "dve"))
                r.dma("sp", self.E_d[j0:j0 + nr, dt * 1024:(dt + 1) * 1024], est[0:nr, 0:1024],
                      reads=[f"xbuf{jb}"], writes=["E_d"])
        r.barrier()

    def ssm_scan(self, passB, fused=False):
        c, r = self.c, self.r
        self.ssm_setup()
        d = self.ssm_in
        G, NCH = c.G, c.NCH
        Q = G // 2
        self.phase_reset()
        B = lambda n, k: self.fb(n, "S" + k, Q)
        lamr, lami, dl, lrd, lid = B(128, "lamr"), B(128, "lami"), B(2, "dl"), B(128, "lrd"), B(128, "lid")
        ARG, MAG, SN, CS, TMP = B(128, "ARG"), B(128, "MAG"), B(128, "SN"), B(128, "CS"), B(128, "TMP")
        TMPI = V(self.fc(128)[0:Q, :].bitcast(mybir.dt.int32), "STMPI")
        X, t1, t2 = B(256, "X"), B(256, "t1"), B(256, "t2")
        r.dma("sp", lamr.ap, d["lamrS"][:, :], writes=[lamr.key])
        r.dma("sp", lami.ap, d["lamiS"][:, :], writes=[lami.key])
        r.dma("sp", dl.ap, d["lstepS"][:, :], writes=[dl.key])
        self.actv(dl, dl, AF.Exp)
        for gi in range(2):
            sl = lambda a, gi=gi: a[:, gi * 64:(gi + 1) * 64]
            dcol = dl.v(lambda a, gi=gi: a[:, gi:gi + 1])
            self.ts(lrd.v(sl), lamr.v(sl), dcol, ALU.mult)
            self.ts(lid.v(sl), lami.v(sl), dcol, ALU.mult)

        def power(n, tag):
            oR, oI, oIn = B(128, tag + "R"), B(128, tag + "I"), B(128, tag + "In")
            self.ts(MAG, lrd, float(n), ALU.mult)
            self.actv(MAG, MAG, AF.Exp)
            self.ts(ARG, lid, float(n), ALU.mult)
            self.sincos(ARG, SN, CS, TMP, TMPI)
            self.tt(oR, CS, MAG, ALU.mult)
            self.tt(oI, SN, MAG, ALU.mult)
            self.ts(oIn, oI, -1.0, ALU.mult)
            return oR, oI, oIn

        x4 = lambda a: a.rearrange("q (g k p) -> q g k p", g=2, k=2)
        a4 = lambda a: a.rearrange("q (g p) -> q g p", g=2)

        def cstep(Xs, Ein, A, outX):
            aR, aI, aIn = A
            self.tt(t1.v(x4), Xs.v(x4), aR.v(lambda a: a4(a).unsqueeze(2).broadcast_to([Q, 2, 2, 64])), ALU.mult)
            self.tt(t2.v(lambda a: x4(a)[:, :, 0, :]), Xs.v(lambda a: x4(a)[:, :, 1, :]), aIn.v(a4), ALU.mult)
            self.tt(t2.v(lambda a: x4(a)[:, :, 1, :]), Xs.v(lambda a: x4(a)[:, :, 0, :]), aI.v(a4), ALU.mult)
            self.tt(t1, t1, t2, ALU.add)
            self.tt(outX, t1, Ein, ALU.add)

        A8 = power(8, "A8")
        self.mset(X, 0.0)
        if passB:
            if fused:
                nsrc = 4
                Lsrc = [self.Lall_d[rr * Q:(rr + 1) * Q, :] for rr in range(nsrc)]
                lkey = "Lall_d"
            else:
                nsrc = 8
                Lall_in = self.inp("Lall", [8, Q, 256])
                Lsrc = [Lall_in[rr, :, :] for rr in range(nsrc)]
                lkey = None
            wsel_in = self.inp("wsel", [Q, 8])
            A1k = power(1024, "A1k")
            wsel, Lt, Xn = B(8, "wsel"), B(256, "Lt"), B(256, "Xn")
            r.dma("sp", wsel.ap, wsel_in[:, :], writes=[wsel.key])
            for rr in range(nsrc):
                r.dma("sp", Lt.ap, Lsrc[rr], reads=([lkey] if lkey else []), writes=[Lt.key])
                cstep(X, Lt, A1k, Xn)
                self.tt(t2, Xn, X, ALU.subtract)
                self.stt(X, t2, wsel.v(lambda a, rr=rr: a[:, rr:rr + 1]), X, ALU.mult, ALU.add)
        JB = 5
        last = NCH - 1 if passB else NCH - 2
        xst = [V(self.gc(JB * 256)[0:Q, :], f"xst{i}") for i in range(2)]
        if passB:
            z = xst[0]
            self.mset(z.v(lambda a: a[:, 0:256]), 0.0, eng="pool")
            r.dma("sp", self.X_d[0:1, :].rearrange("j (q f) -> q j f", f=256),
                  z.ap[:, 0:256].rearrange("q (j f) -> q j f", j=1), reads=[z.key], writes=["X_d"])
        bi = 0
        for j0 in range(1, last + 1, JB):
            nj = min(JB, last + 1 - j0)
            eb = V(self.xbuf[bi % 2][0:Q, 0:JB * 256], f"xbuf{bi%2}")
            xs = xst[bi % 2]
            bi += 1
            r.dma("sp", eb.ap[:, 0:nj * 256].rearrange("q (j f) -> q j f", j=nj),
                  self.E_d[j0:j0 + nj, :].rearrange("j (q f) -> q j f", f=256), reads=["E_d"], writes=[eb.key])
            for jj in range(nj):
                sl = lambda a, jj=jj: a[:, jj * 256:(jj + 1) * 256]
                if passB:
                    self.cpy(xs.v(sl), X, eng="act")
                cstep(X, eb.v(sl), A8, X)
            if passB:
                r.dma("sp", self.X_d[j0:j0 + nj, :].rearrange("j (q f) -> q j f", f=256),
                      xs.ap[:, 0:nj * 256].rearrange("q (j f) -> q j f", j=nj), reads=[xs.key], writes=["X_d"])
        if not passB and not fused:
            Lout = self.outp("Lout", [Q, 256])
            r.dma("sp", Lout[:, :], X.ap, reads=[X.key], writes=["Lout"])
            self.final_keys = getattr(self, "final_keys", []) + ["Lout"]
        if not passB and fused:
            self.Lsrc_d = self.scratch("Lsrc_d", [Q, 256])
            self.Lall_d = self.scratch("Lall_d", [4 * Q, 256])
            r.dma("sp", self.Lsrc_d[:, :], X.ap, reads=[X.key], writes=["Lsrc_d"])
            src_t, dst_t = self.Lsrc_d, self.Lall_d
            r.op("pool", lambda e: e.collective_compute("AllGather", ALU.bypass,
                                                        replica_groups=[[0, 1, 2, 3], [4, 5, 6, 7]],
                                                        ins=[src_t[:, :]], outs=[dst_t[:, :]]),
                 reads=["Lsrc_d"], writes=["Lall_d"], cc=True)
        r.barrier()

    def ssm_y(self):
        c, r = self.c, self.r
        ps = self.ps
        self.ssm_setup()
        d = self.ssm_in
        G, KT, NCH, JP = c.G, c.KT, c.NCH, self.JP
        self.phase_reset()
        hs8 = self.hs8()
        MD = self.gc(8 * 1024).rearrange("p (s g t x) -> p s g t x", s=8, g=8, t=8)
        Hb = self.gc(8 * 128).rearrange("p (g x) -> p g x", g=8)
        XT = self.gc(8 * 72).rearrange("p (g x) -> p g x", g=8)
        Xtok = [self.gc(1024) for i in range(2)]
        lamr2, lami2, dl2, lrd2, lid2 = (self.fb(G, "Y" + k) for k in ("lamr2", "lami2", "dl2", "lrd2", "lid2"))
        r.dma("sp", lamr2.ap, d["lamr2"][:, :], writes=[lamr2.key])
        r.dma("sp", lami2.ap, d["lami2"][:, :], writes=[lami2.key])
        r.dma("sp", dl2.ap, d["lstep2"][:, :], writes=[dl2.key])
        self.actv(dl2, dl2, AF.Exp)
        self.tt(lrd2, lamr2, dl2, ALU.mult)
        self.tt(lid2, lami2, dl2, ALU.mult)
        ssmc = V(self.ssmc[:, :], "ssmc")
        tau = ssmc.v(lambda a: a[:, 16:25].unsqueeze(1).broadcast_to([128, 8, 9]))
        phA = ssmc.v(lambda a: a[:, 25:26])
        phB = ssmc.v(lambda a: a[:, 26:27])
        sgn = ssmc.v(lambda a: a[:, 27:28])
        B72 = lambda k: self.fb(72, "Y" + k)
        ARG, MG, TA, TB, TM = B72("ARG"), B72("MG"), B72("TA"), B72("TB"), B72("TM")
        TMI = V(self.fc(72).bitcast(mybir.dt.int32), "YTMI")
        B128 = lambda k: self.fb(128, "Y" + k)
        CA, CB, BA, BB, Bst, Bt, Ksb, KTT = (B128(k) for k in ("CA", "CB", "BA", "BB", "Bst", "Bt", "Ksb", "KTT"))
        Wst, Wt = self.fb(1152, "YWst"), self.fb(1152, "YWt")
        ypre = self.fb(4 * 72, "Yypre")
        lr8, li8, lrd8, lid8 = (self.fb(8, "Y" + k) for k in ("lr8", "li8", "lrd8", "lid8"))
        w4 = lambda a: a.rearrange("p (g t x) -> p g t x", g=8, t=9)
        g3 = lambda a: a.rearrange("p (g x) -> p g x", g=8)
        t3 = lambda a: a.rearrange("p (g t) -> p g t", g=8)
        foff0 = self.foff
        rows_of = [(0, 64), (64, NCH - 64)]
        for dt in range(KT):
            self.foff = foff0
            gsl = lambda a, dt=dt: a[:, dt * 8:(dt + 1) * 8]
            for nm, tl in (("CA2", CA), ("CB2", CB), ("BA2", BA), ("BB2", BB)):
                r.dma("sp", tl.ap, d[nm][:, dt * 128:(dt + 1) * 128], writes=[tl.key])
            self.cpy(lr8, lamr2.v(gsl))
            self.cpy(li8, lami2.v(gsl))
            self.cpy(lrd8, lrd2.v(gsl))
            self.cpy(lid8, lid2.v(gsl))
            b9 = lambda a: a.unsqueeze(2).broadcast_to([128, 8, 9])
            self.tt(MG.v(t3), lrd8.v(b9), tau, ALU.mult)
            self.actv(MG, MG, AF.Exp)
            for TX, ph in ((TA, phA), (TB, phB)):
                self.tt(ARG.v(t3), lid8.v(b9), tau, ALU.mult)
                self.sincos(ARG, TX, None, TM, TMI, phase_col=ph)
                self.tt(TX, TX, MG, ALU.mult)
            fr2, fi2 = self.cabar_f(lr8, li8, lrd8, lid8, 8, "Yf")
            self.ts(fi2, fi2, sgn, ALU.mult)
            b16 = lambda a: a.unsqueeze(2).broadcast_to([128, 8, 16])
            self.tt(Bst.v(g3), BA.v(g3), fr2.v(b16), ALU.mult)
            self.tt(Bt.v(g3), BB.v(g3), fi2.v(b16), ALU.mult)
            self.tt(Bst, Bst, Bt, ALU.add)
            cw = lambda a: g3(a).unsqueeze(2).broadcast_to([128, 8, 9, 16])
            tw = lambda a: t3(a).unsqueeze(3).broadcast_to([128, 8, 9, 16])
            self.tt(Wst.v(w4), CA.v(cw), TA.v(tw), ALU.mult)
            self.tt(Wt.v(w4), CB.v(cw), TB.v(tw), ALU.mult)
            self.tt(Wst, Wst, Wt, ALU.add)
            self.cpy(V(Hb.rearrange("p g (t x) -> p g t x", t=8), "Hb"), Wst.v(lambda a: w4(a)[:, :, 1:9, :]), eng="act")
            psK = ps[7]
            for g8 in range(8):
                r.op("pe", lambda e, g8=g8: e.matmul(psK[:, g8 * 16:(g8 + 1) * 16],
                                                     lhsT=w4(Wst.ap)[:, g8, 0:8, :].rearrange("p t x -> p (t x)"),
                                                     rhs=g3(Bst.ap)[:, g8, :], start=True, stop=True),
                     reads=[Wst.key, Bst.key], writes=["ps7"])
            self.cpy(Ksb, V(psK[:, 0:128], "ps7"), eng="act")
            r.op("pe", lambda e: e.transpose(out=psK[:, 128:256], in_=Ksb.ap, identity=self.ident[:, :]),
                 reads=[Ksb.key, "ident"], writes=["ps7"])
            self.cpy(KTT, V(psK[:, 128:256], "ps7"), eng="act")
            self.mset(V(MD.rearrange("p s g t x -> p (s g t x)"), "MD"), 0.0, eng="pool")
            for s_ in range(8):
                nt_ = 8 - s_
                self.tt(V(MD[:, s_, :, s_:8, :], "MD"),
                        KTT.v(lambda a, nt_=nt_: a[:, 0:nt_ * 16].rearrange("p (t x) -> p t x", t=nt_).unsqueeze(1)
                              .broadcast_to([128, 8, nt_, 16])),
                        ssmc.v(lambda a, nt_=nt_: a[:, 0:8].unsqueeze(2).unsqueeze(3).broadcast_to([128, 8, nt_, 16])),
                        ALU.mult, eng="pool")
            for jb, (j0, nr) in enumerate(rows_of):
                xt_ = Xtok[jb]
                r.dma("sp", xt_[0:nr, :], self.X_d[j0:j0 + nr, dt * 1024:(dt + 1) * 1024], reads=["X_d"],
                      writes=[f"Xtok{jb}"])
                psX = ps[4].bitcast(BF16)
                for g8 in range(8):
                    r.op("pe", lambda e, g8=g8, nr=nr, xt_=xt_: e.transpose(
                        out=psX[:, g8 * 72:g8 * 72 + nr], in_=xt_[0:nr, g8 * 128:(g8 + 1) * 128],
                        identity=self.identb[0:nr, 0:nr]), reads=[f"Xtok{jb}", "identb"], writes=["ps4"])
                self.cpy(V(XT[:, :, 0:nr], "XT"),
                         V(psX[:, 0:576].rearrange("p (g x) -> p g x", g=8)[:, :, 0:nr], "ps4"), eng="act")
                ytok = self.xbuf[jb]
                for half in range(2):
                    bank = jb * 2 + half
                    for s_ in range(8):
                        r.op("pe", lambda e, s_=s_, j0=j0, nr=nr, half=half, bank=bank, dt=dt: e.matmul(
                            ps[bank][0:nr, 0:512], lhsT=hs8[:, dt, s_, j0:j0 + nr],
                            rhs=MD[:, s_, half * 4:(half + 1) * 4, :, :].rearrange("p g t x -> p (g t x)"),
                            start=(s_ == 0), stop=False), reads=["hT", "MD"], writes=[f"ps{bank}"])
                    for gq in range(4):
                        g8 = half * 4 + gq
                        r.op("pe", lambda e, g8=g8, gq=gq, nr=nr, bank=bank: e.matmul(
                            ps[bank][0:nr, gq * 128:(gq + 1) * 128], lhsT=XT[:, g8, 0:nr], rhs=Hb[:, g8, :],
                            start=False, stop=(gq == 3)), reads=["XT", "Hb"], writes=[f"ps{bank}"])
                    self.cpy(V(ytok[0:nr, 0:1024].rearrange("j (t g x) -> j g t x", t=8, g=8)[:, half * 4:(half + 1) * 4, :, :],
                               f"xbuf{jb}"),
                             V(ps[bank][0:nr, 0:512].rearrange("j (g t x) -> j g t x", g=4, t=8), f"ps{bank}"),
                             eng=("act" if half == 0 else "dve"))
                for th in range(2):
                    bank = 5 + th
                    for tq in range(4):
                        t_ = th * 4 + tq
                        r.op("pe", lambda e, t_=t_, tq=tq, nr=nr, ytok=ytok, bank=bank: e.transpose(
                            out=ps[bank][:, tq * 72:tq * 72 + nr], in_=ytok[0:nr, t_ * 128:(t_ + 1) * 128],
                            identity=self.ident[0:nr, 0:nr]), reads=[f"xbuf{jb}", "ident"], writes=[f"ps{bank}"])
                    yp = ypre.v(lambda a, nr=nr: a.rearrange("p (t x) -> p t x", t=4)[:, :, 0:nr])
                    hsl = V(hs8[:, dt, th * 4:(th + 1) * 4, j0:j0 + nr], "hT")
                    self.stt(yp, hsl, V(self.dskip[:, dt:dt + 1], "dskip"),
                             V(ps[bank][:, 0:288].rearrange("p (t x) -> p t x", t=4)[:, :, 0:nr], f"ps{bank}"),
                             ALU.mult, ALU.add)
                    self.actv(hsl, yp, AF.Gelu_apprx_tanh)
        r.barrier()

    def glu(self):
        c, r = self.c, self.r
        ps = self.ps
        KT, JP, NCH = c.KT, self.JP, c.NCH
        w_glu = self.inp("w_glu", [c.D, 2 * c.D])
        self.phase_reset()
        gy = self.hreg[:, 0:KT * 8 * JP].rearrange("p (k n) -> p k n", k=KT)
        sig = self.fc(8 * JP)
        ysts = [self.fc(c.NT) for i in range(3)]
        ntl = [(0, 3), (3, 3), (6, 2)]
        gtiles = {}
        for m in range(KT):
            for which in range(2):
                if m % 2 == 0:
                    b = self.nextw()
                    self.load_w(b, w_glu, which * c.D + m * 128, 256, 0)
                    gtiles[which] = b
                b = gtiles[which]
                wb = self.wst[b]
                wc0 = (m % 2) * 128
                for k in range(KT):
                    for i, (t0, ntt) in enumerate(ntl):
                        bank = which * 3 + i
                        r.op("pe", lambda e, k=k, wb=wb, t0=t0, ntt=ntt, bank=bank, wc0=wc0: e.matmul(
                            ps[bank][:, 0:ntt * JP], lhsT=wb[:, k, wc0:wc0 + 128], rhs=gy[:, k, t0 * JP:(t0 + ntt) * JP],
                            start=(k == 0), stop=(k == KT - 1)), reads=[f"wst{b}", "hT"], writes=[f"ps{bank}"])
            yst = ysts[m % 3]
            ystk = f"gyst{m%3}"
            for i, (t0, ntt) in enumerate(ntl):
                r.op("act", lambda e, i=i, t0=t0, ntt=ntt: e.activation(
                    out=sig[:, t0 * JP:(t0 + ntt) * JP], in_=ps[3 + i][:, 0:ntt * JP], func=AF.Sigmoid),
                    reads=[f"ps{3+i}"], writes=["sig"])
                r.op("dve", lambda e, i=i, t0=t0, ntt=ntt, yst=yst: e.tensor_tensor(
                    out=yst[:, 0:c.NT].rearrange("p (j t) -> p t j", t=8)[:, t0:t0 + ntt, :],
                    in0=ps[i][:, 0:ntt * JP].rearrange("p (t j) -> p t j", j=JP)[:, :, 0:NCH],
                    in1=sig[:, t0 * JP:(t0 + ntt) * JP].rearrange("p (t j) -> p t j", j=JP)[:, :, 0:NCH], op=ALU.mult),
                    reads=[f"ps{i}", "sig"], writes=[ystk])
            self.defer(lambda m=m, yst=yst, ystk=ystk: r.dma(
                "pool", self.xres[m * 128:(m + 1) * 128, :], yst[:, :], reads=[ystk], writes=[f"xres{m}"],
                accum_op=ALU.add))
        self.flush()
        r.barrier()

    def load_x(self, name):
        c, r = self.c, self.r
        xin = self.inp(name, [c.D, c.NT])
        for m in range(c.KT):
            xb = self.xbuf[m % 2]
            r.dma("sp", xb[:, 0:c.NT], xin[m * 128:(m + 1) * 128, :], writes=[f"xbuf{m%2}"])
            r.dma("sp", self.xres[m * 128:(m + 1) * 128, :], xb[:, 0:c.NT], reads=[f"xbuf{m%2}"], writes=[f"xres{m}"])
        r.barrier()

    def copy_x_to_xres(self):
        c, r = self.c, self.r
        xT0 = self.inp("xT0", [c.D, c.NKV])
        for m in range(c.KT):
            xb = self.xbuf[m % 2]
            r.dma("sp", xb[:, 0:c.NT], xT0[m * 128:(m + 1) * 128, c.NKV - c.NT:c.NKV], writes=[f"xbuf{m%2}"])
            r.dma("sp", self.xres[m * 128:(m + 1) * 128, :], xb[:, 0:c.NT], reads=[f"xbuf{m%2}"], writes=[f"xres{m}"])
        r.barrier()

    def dump_xres(self, name):
        c, r = self.c, self.r
        o = self.outp(name, [c.D, c.NT])
        for m in range(c.KT):
            xb = self.xbuf[m % 2]
            r.dma("sp", xb[:, 0:c.NT], self.xres[m * 128:(m + 1) * 128, :], reads=[f"xres{m}"], writes=[f"xbuf{m%2}"])
            r.dma("sp", o[m * 128:(m + 1) * 128, :], xb[:, 0:c.NT], reads=[f"xbuf{m%2}"], writes=[name])
        self.final_keys = getattr(self, "final_keys", []) + [name]

    def final_norm(self):
        c, r = self.c, self.r
        ps = self.ps
        o = self.outp("outT", [c.D, c.NOWN])
        ncols = c.NT
        nts = ntiles(ncols)
        src = self.xres
        for k in range(c.KT):
            xb = self.xbuf[k % 2]
            sq = self.sq[k % 2]
            r.dma("sp", xb[:, 0:ncols], src[k * 128:(k + 1) * 128, :], reads=[f"xres{k}"], writes=[f"xbuf{k%2}"])
            r.op("act", lambda e, xb=xb, sq=sq: e.activation(out=sq[:, 0:ncols], in_=xb[:, 0:ncols], func=AF.Square),
                 reads=[f"xbuf{k%2}"], writes=[f"sq{k%2}"])
            for i, (s, n) in enumerate(nts):
                r.op("pe", lambda e, i=i, s=s, n=n, sq=sq, k=k: e.matmul(
                    ps[i][:, 0:n], lhsT=self.ones_bf[:, :], rhs=sq[:, s:s + n],
                    start=(k == 0), stop=(k == c.KT - 1)), reads=[f"sq{k%2}", "ones"], writes=[f"ps{i}"])
        for i, (s, n) in enumerate(nts):
            r.op("dve", lambda e, i=i, s=s, n=n: e.tensor_scalar(
                out=self.rstd[:, s:s + n], in0=ps[i][:, 0:n], scalar1=1.0 / c.D, scalar2=EPS,
                op0=ALU.mult, op1=ALU.add), reads=[f"ps{i}"], writes=["rstd"])
        r.op("act", lambda e: e.sqrt(out=self.rstd[:, 0:ncols], in_=self.rstd[:, 0:ncols]),
             reads=["rstd"], writes=["rstd"])
        r.op("dve", lambda e: e.reciprocal(out=self.rstd[:, 0:ncols], in_=self.rstd[:, 0:ncols]),
             reads=["rstd"], writes=["rstd"])
        gcol = 4 * c.KT
        for k in range(c.KT):
            xb = self.xbuf[k % 2]
            r.dma("sp", xb[:, 0:ncols], src[k * 128:(k + 1) * 128, :], reads=[f"xres{k}"], writes=[f"xbuf{k%2}"])
            r.op("dve", lambda e, xb=xb, k=k: e.scalar_tensor_tensor(
                out=xb[:, 0:ncols], in0=xb[:, 0:ncols], scalar=self.gains[:, gcol + k:gcol + k + 1],
                in1=self.rstd[:, 0:ncols], op0=ALU.mult, op1=ALU.mult),
                reads=[f"xbuf{k%2}", "rstd", "gains"], writes=[f"xbuf{k%2}"])
            r.dma("sp", o[k * 128:(k + 1) * 128, :], xb[:, c.HALO:c.HALO + c.NOWN], reads=[f"xbuf{k%2}"],
                  writes=["outT"])
        self.final_keys = getattr(self, "final_keys", []) + ["outT"]

    def finish(self):
        nc, r = self.nc, self.r
        r.wait_keys("sp", getattr(self, "final_keys", []))
        with contextlib.ExitStack() as st:
            sems = {e: st.enter_context(nc.semaphore("s_" + e)) for e in ENGS}
            dsems = {(e, s): st.enter_context(nc.semaphore(f"d_{e}{s}")) for e in ("sp", "pool", "act")
                     for s in range(NSLOT)}
            dsems[("cc", 0)] = st.enter_context(nc.semaphore("cc_sem"))
            r.emit(nc, sems, dsems)
        return nc


def build(cfg, phases):
    p = Prog(cfg, phases)
    p.setup_common()
    for ph in phases:
        if ph == "attn":
            p.attention()
        elif ph == "copyx":
            p.copy_x_to_xres()
        elif ph == "ffn0":
            p.ffn(0)
        elif ph == "ffn1":
            p.ffn(1)
        elif ph == "final":
            p.final_norm()
        elif ph == "ssm":
            p.ssm_E()
            p.ssm_scan(False, fused=True)
            p.ssm_scan(True, fused=True)
            p.ssm_y()
            p.glu()
        elif ph == "ssmA":
            p.ssm_E()
            p.ssm_scan(False)
        elif ph == "ssmB":
            p.ssm_E()
            p.ssm_scan(True)
            p.ssm_y()
            p.glu()
        elif ph.startswith("loadx:"):
            p.load_x(ph[6:])
        elif ph.startswith("dump:"):
            p.dump_xres(ph[5:])
        else:
            raise ValueError(ph)
    nc = p.finish()
    return p, nc


MASKV = -30000.0


def _t5_buckets(dist):
    n = np.maximum(dist, 0)
    is_small = n < 16
    large = 16 + (np.log(np.maximum(n, 1) / 16) / np.log(128 / 16) * (32 - 16)).astype(np.int32)
    large = np.minimum(large, 31)
    return np.where(is_small, n, large).astype(np.int32)


def _prep_shared(cfg, I):
    D, F, KT, FT, G, NH = cfg.D, cfg.F, cfg.KT, cfg.FT, cfg.G, cfg.NH
    m = {}
    vecs = [I['attn_norm'][0], I['ffn_norm'][0], I['ssm_norm'][0], I['ffn_norm'][1], I['final_norm']]
    m['gains'] = np.ascontiguousarray(np.concatenate([np.asarray(v).reshape(KT, 128).T for v in vecs], axis=1),
                                      dtype=np.float32)
    s = np.arange(128)[:, None]
    q = np.arange(128)[None, :]
    rb = np.asarray(I['rel_bias'])
    bt = np.empty((NH // 2, 128, 4, 128), np.float32)
    for kb in range(2):
        dist = q - s + (128 if kb == 0 else 0)
        valid = (dist >= 0) & (dist < 128)
        bk = _t5_buckets(dist)
        for j in range(NH // 2):
            for hh in range(2):
                bt[j, :, hh * 2 + kb, :] = np.where(valid, rb[bk, 2 * j + hh], np.float32(MASKV))
    m['bias_tab'] = bt.reshape(NH // 2, 128, 512)
    sk = np.asarray(I['sinks'][0])
    s2 = np.empty((128, NH // 2), np.float32)
    s2[:64, :] = sk[0::2][None, :]
    s2[64:, :] = sk[1::2][None, :]
    m['sinks2'] = s2
    m['w_qkv'] = I['w_qkv'][0]
    m['w_o'] = I['w_o'][0]
    for li in range(2):
        cw = np.asarray(I['conv_w'][li])
        cb = np.asarray(I['conv_b'][li])
        rows = [cw[0], cw[1], cw[2], cb]
        cp = np.stack([r_.reshape(2 * FT, 128).T for r_ in rows], axis=1)
        m[f'convp{li}'] = np.ascontiguousarray(cp.reshape(128, 4 * 2 * FT), dtype=np.float32)
        m[f'w_up{li}'] = I['w_up'][li]
        m[f'w_down{li}'] = I['w_down'][li]
    lr = np.asarray(I['lambda_re'][0]); li_ = np.asarray(I['lambda_im'][0]); ls = np.asarray(I['log_step'][0])
    br = np.asarray(I['b_re'][0]); bi = np.asarray(I['b_im'][0])
    cr = np.asarray(I['c_re'][0]); ci = np.asarray(I['c_im'][0])
    OFF = 64.0 * math.pi
    sc = np.zeros((128, 32), np.float32)
    p = np.arange(128)
    sc[:, 0:8] = (p[:, None] // 16 == np.arange(8)[None, :])
    sc[:, 8:16] = 7 - np.arange(8)[None, :]
    sc[:, 16:25] = np.arange(9)[None, :]
    sc[:, 25] = np.where(p < 64, 0.5 * math.pi, math.pi) + OFF
    sc[:, 26] = np.where(p < 64, math.pi, 1.5 * math.pi) + OFF
    sc[:, 27] = np.where(p < 64, -1.0, 1.0)
    m['ssmc'] = sc
    m['ident'] = np.eye(128, dtype=np.float32)
    g_of = (np.arange(KT)[None, :] * 8 + (p[:, None] // 16))
    cidx = p % 16
    m['lamr1'] = np.ascontiguousarray(lr[g_of].reshape(128, KT * 64))
    m['lami1'] = np.ascontiguousarray(li_[g_of].reshape(128, KT * 64))
    m['bre1'] = np.ascontiguousarray(br[g_of, :, cidx[:, None]].reshape(128, KT * 64))
    m['bim1'] = np.ascontiguousarray(bi[g_of, :, cidx[:, None]].reshape(128, KT * 64))
    m['lstep1'] = np.ascontiguousarray(ls[g_of])
    pp = p % 64
    m['lamr2'] = np.ascontiguousarray(lr[:, pp].T)
    m['lami2'] = np.ascontiguousarray(li_[:, pp].T)
    m['lstep2'] = np.ascontiguousarray(np.broadcast_to(ls[None, :], (128, G)), dtype=np.float32)
    crT = cr.transpose(2, 0, 1)
    ciT = ci.transpose(2, 0, 1)
    m['CA2'] = np.ascontiguousarray(np.concatenate([crT, crT], 0).reshape(128, G * 16))
    m['CB2'] = np.ascontiguousarray(np.concatenate([ciT, ciT], 0).reshape(128, G * 16))
    brT = br.transpose(1, 0, 2)
    biT = bi.transpose(1, 0, 2)
    m['BA2'] = np.ascontiguousarray(np.concatenate([brT, biT], 0).reshape(128, G * 16))
    m['BB2'] = np.ascontiguousarray(np.concatenate([biT, brT], 0).reshape(128, G * 16))
    m['lamrS'] = np.ascontiguousarray(lr.reshape(G // 2, 128))
    m['lamiS'] = np.ascontiguousarray(li_.reshape(G // 2, 128))
    m['lstepS'] = np.ascontiguousarray(ls.reshape(G // 2, 2))
    m['dskip'] = np.ascontiguousarray(np.asarray(I['d_skip'][0]).reshape(KT, 128).T)
    m['w_glu'] = I['w_glu'][0]
    return m


def _prep_core(c, cfg, I):
    b = c // 4
    ch = c % 4
    t0 = ch * cfg.NOWN
    m = {}
    xs = np.zeros((cfg.NKV, cfg.D), np.float32)
    lo = t0 - cfg.KVH
    src_lo = max(lo, 0)
    xs[src_lo - lo:, :] = I['x'][b, src_lo:t0 + cfg.NOWN, :]
    m['xT0'] = np.ascontiguousarray(xs.T)
    m['flag'] = np.full((128, 1), 0.0 if ch == 0 else 1.0, np.float32)
    m['negmask'] = np.full((128, 128), MASKV if ch == 0 else 0.0, np.float32)
    w = np.zeros((cfg.G // 2, 8), np.float32)
    if FUSED:
        w[:, 0:ch] = 1.0
    else:
        for r_ in range(8):
            if r_ // 4 == b and r_ % 4 < ch:
                w[:, r_] = 1.0
    m['wsel'] = w
    return m


FUSED = True
PHASES = ["attn", "ffn0", "ssm", "ffn1", "final"]
PHASES1 = ["attn", "ffn0", "ssmA", "dump:x2T"]
PHASES2 = ["loadx:x2T", "ssmB", "ffn1", "final"]


def run_model(cfg, I):
    shared = _prep_shared(cfg, I)
    cores = [_prep_core(c, cfg, I) for c in range(8)]
    if FUSED:
        p, nc = build(cfg, PHASES)
        maps = [{k: (cores[c][k] if k in cores[c] else shared[k]) for k in p.din} for c in range(8)]
        res = run_bass_kernel_spmd(nc, maps, core_ids=list(range(8))).results
        out = np.empty(I['x'].shape, np.float32)
        for c in range(8):
            out[c // 4, (c % 4) * cfg.NOWN:(c % 4 + 1) * cfg.NOWN, :] = np.asarray(res[c]['outT']).T
        return out
    p1, nc1 = build(cfg, PHASES1)
    maps1 = [{k: (cores[c][k] if k in cores[c] else shared[k]) for k in p1.din} for c in range(8)]
    r1 = run_bass_kernel_spmd(nc1, maps1, core_ids=list(range(8))).results
    del maps1
    Lall = np.stack([np.asarray(r1[c]['Lout']) for c in range(8)])
    p2, nc2 = build(cfg, PHASES2)
    maps2 = []
    for c in range(8):
        mm = {}
        for k in p2.din:
            if k == 'x2T':
                mm[k] = np.asarray(r1[c]['x2T'])
            elif k == 'Lall':
                mm[k] = Lall
            elif k in cores[c]:
                mm[k] = cores[c][k]
            else:
                mm[k] = shared[k]
        maps2.append(mm)
    r2 = run_bass_kernel_spmd(nc2, maps2, core_ids=list(range(8))).results
    B = I['x'].shape[0]
    out = np.empty(I['x'].shape, np.float32)
    for c in range(8):
        b = c // 4
        ch = c % 4
        out[b, ch * cfg.NOWN:(ch + 1) * cfg.NOWN, :] = np.asarray(r2[c]['outT']).T
    return out


def kernel(**inputs):
    I = {k: np.asarray(v) for k, v in inputs.items()}
    cfg = Cfg(D=I['x'].shape[2], F=I['w_down'].shape[1])
    return run_model(cfg, I)
```

```python
import contextlib
import math
import numpy as np
import concourse.bass as bass
import concourse.mybir as mybir
from concourse.bass_utils import run_bass_kernel_spmd

F32 = mybir.dt.float32
BF16 = mybir.dt.bfloat16
ALU = mybir.AluOpType
AF = mybir.ActivationFunctionType

ENGS = ("pe", "act", "dve", "pool", "sp")
NSLOT = 8
SAME_ENGINE_IN_ORDER = ("pe",)
EPS = 1e-6
NEG = -1e30


class Op:
    __slots__ = ("eng", "fn", "deps", "dma", "idx", "needed", "sem", "val", "slot")

    def __init__(self, eng, fn, dma):
        self.eng = eng
        self.fn = fn
        self.dma = dma
        self.deps = []
        self.needed = False
        self.sem = None
        self.val = 0
        self.slot = -1


class V:
    __slots__ = ("ap", "key")

    def __init__(self, ap, key):
        self.ap = ap
        self.key = key

    def v(self, f):
        return V(f(self.ap), self.key)


class Rec:
    def __init__(self):
        self.ops = {e: [] for e in ENGS}
        self.lastw = {}
        self.readers = {}
        self.seen = {e: {p: -1 for p in ENGS} for e in ENGS}
        self.seen_dma = {e: set() for e in ENGS}
        self.slot_last = {}
        self.slot_rr = {e: 0 for e in ENGS}

    def _add_dep(self, op, prod):
        if prod is None or prod is op:
            return
        e = op.eng
        if prod.dma:
            if id(prod) in self.seen_dma[e]:
                return
            self.seen_dma[e].add(id(prod))
        else:
            if prod.eng == e and e in SAME_ENGINE_IN_ORDER:
                return
            if prod.idx <= self.seen[e][prod.eng]:
                return
            self.seen[e][prod.eng] = prod.idx
        prod.needed = True
        op.deps.append(prod)

    def op(self, eng, fn, reads=(), writes=(), dma=False, cc=False):
        o = Op(eng, fn, dma or cc)
        o.idx = len(self.ops[eng])
        for k in reads:
            self._add_dep(o, self.lastw.get(k))
        for k in writes:
            self._add_dep(o, self.lastw.get(k))
            for r in self.readers.get(k, ()):
                self._add_dep(o, r)
        if cc:
            o.slot = "cc"
            prev = self.slot_last.get(("cc", 0))
            if prev is not None:
                self._add_dep(o, prev)
            self.slot_last[("cc", 0)] = o
            o.needed = True
        elif dma:
            s = self.slot_rr[eng]
            self.slot_rr[eng] = (s + 1) % NSLOT
            o.slot = s
            prev = self.slot_last.get((eng, s))
            if prev is not None:
                self._add_dep(o, prev)
            self.slot_last[(eng, s)] = o
            o.needed = True
        for k in reads:
            self.readers.setdefault(k, []).append(o)
        for k in writes:
            self.lastw[k] = o
            self.readers[k] = []
        self.ops[eng].append(o)
        return o

    def dma(self, eng, out, in_, reads=(), writes=(), **kw):
        return self.op(eng, lambda e: e.dma_start(out=out, in_=in_, **kw), reads, writes, dma=True)

    def barrier(self):
        lasts = []
        for e in ENGS:
            for o in reversed(self.ops[e]):
                if not o.dma and o.fn is not None:
                    lasts.append(o)
                    break
        dmas = [o for o in self.slot_last.values()]
        for e in ENGS:
            o = Op(e, None, False)
            o.idx = len(self.ops[e])
            for p in lasts + dmas:
                self._add_dep(o, p)
            self.ops[e].append(o)

    def wait_keys(self, eng, keys):
        o = Op(eng, None, False)
        o.idx = len(self.ops[eng])
        for k in keys:
            self._add_dep(o, self.lastw.get(k))
        self.ops[eng].append(o)

    def emit(self, nc, sems, dsems):
        for e in ENGS:
            cnt = 0
            dcnt = {}
            for o in self.ops[e]:
                if o.dma and o.slot == "cc":
                    k = ("cc", 0)
                    dcnt[k] = dcnt.get(k, 0) + 1
                    o.sem = dsems[k]
                    o.val = dcnt[k]
                elif o.dma:
                    k = (e, o.slot)
                    dcnt[k] = dcnt.get(k, 0) + 16
                    o.sem = dsems[k]
                    o.val = dcnt[k]
                elif o.needed:
                    cnt += 1
                    o.sem = sems[e]
                    o.val = cnt
        engobj = {"pe": "tensor", "act": "scalar", "dve": "vector", "pool": "gpsimd", "sp": "sync"}
        with nc.Block() as block:
            for e in ENGS:
                ops = self.ops[e]
                if not ops:
                    continue

                def body(eng, ops=ops):
                    for o in ops:
                        for p in o.deps:
                            eng.wait_ge(p.sem, p.val)
                        if o.fn is None:
                            continue
                        ins = o.fn(eng)
                        if o.dma and o.slot == "cc":
                            ins.then_inc(o.sem)
                        elif o.dma:
                            ins.then_inc(o.sem, 16)
                        elif o.needed:
                            ins.then_inc(o.sem, 1)

                getattr(block, engobj[e])(body)


class Cfg:
    def __init__(self, D=4096, F=11008, NOWN=1024):
        self.D = D
        self.F = F
        self.NOWN = NOWN
        self.HALO = 16
        self.NT = NOWN + self.HALO
        self.KVH = 256
        self.NKV = NOWN + self.KVH
        self.KT = D // 128
        self.FT = F // 128
        self.NH = D // 64
        self.NKVH = self.NH // 8
        self.QW = D
        self.KVW = self.NKVH * 64
        self.G = D // 16
        self.FC = 8
        self.NQB = self.NKV // 128 - 1
        self.NCH = self.NT // 8


def ntiles(n):
    if n == 1040:
        return [(0, 352), (352, 352), (704, 336)]
    if n == 1280:
        return [(i * 320, 320) for i in range(4)]
    k = (n + 511) // 512
    base = (n + k - 1) // k
    base = (base + 31) // 32 * 32
    out = []
    s = 0
    while s < n:
        m = min(base, n - s)
        out.append((s, m))
        s += m
    return out


class Prog:
    def __init__(self, cfg, phases):
        self.c = cfg
        self.phases = phases
        self.nc = bass.Bass("TRN2", target_bir_lowering=False)
        self.r = Rec()
        self.din = {}
        self.dout = {}

    def inp(self, name, shape, dt=F32):
        if name in self.din:
            return self.din[name]
        t = self.nc.dram_tensor(name, list(shape), dt, kind="ExternalInput").ap()
        self.din[name] = t
        return t

    def outp(self, name, shape, dt=F32):
        t = self.nc.dram_tensor(name, list(shape), dt, kind="ExternalOutput").ap()
        self.dout[name] = t
        return t

    def scratch(self, name, shape, dt=F32):
        return self.nc.dram_tensor(name, list(shape), dt).ap()

    def sb(self, name, shape, dt):
        return self.nc.alloc_sbuf_tensor("sb_" + name, list(shape), dt)

    def setup_common(self):
        c, nc, r = self.c, self.nc, self.r
        self.ps = [nc.alloc_psum_tensor(f"ps{i}", [128, 512], F32) for i in range(8)]
        self.hreg = self.sb("hreg", [128, c.KT * c.NKV], BF16)
        self.GCOLS = 12800
        self.FCOLS = 6400
        self.greg = self.sb("greg", [128, self.GCOLS], BF16)
        self.freg = self.sb("freg", [128, self.FCOLS], F32)
        self.goff = 0
        self.foff = 0
        self.ones_bf = self.sb("ones_bf", [128, 128], BF16)
        self.xbuf = [self.sb(f"xbuf{i}", [128, c.NKV], F32) for i in range(2)]
        self.sq = [self.sb(f"sq{i}", [128, c.NKV], BF16) for i in range(2)]
        self.rstd = self.sb("rstd", [128, c.NKV], F32)
        self.NW = 3
        self.wst = [self.sb(f"wst{i}", [128, c.KT, 256], BF16) for i in range(self.NW)]
        self.wi = 0
        self.gains = self.sb("gains", [128, 5 * c.KT], F32)
        self.flag = self.sb("flag", [128, 1], F32)
        g_in = self.inp("gains", [128, 5 * c.KT])
        f_in = self.inp("flag", [128, 1])
        r.op("pool", lambda e: e.memset(self.ones_bf[:], 1.0), writes=["ones"])
        r.dma("sp", self.gains[:], g_in[:, :], writes=["gains"])
        r.dma("sp", self.flag[:], f_in[:, :], writes=["flag"])
        self.xres = self.scratch("xres", [c.D, c.NT])

    def phase_reset(self):
        self.goff = 0
        self.foff = 0

    def gc(self, cols):
        a = self.greg[:, self.goff:self.goff + cols]
        self.goff += (cols + 15) // 16 * 16
        assert self.goff <= self.GCOLS, ("greg overflow", self.goff)
        return a

    def fc(self, cols):
        a = self.freg[:, self.foff:self.foff + cols]
        self.foff += (cols + 7) // 8 * 8
        assert self.foff <= self.FCOLS, ("freg overflow", self.foff)
        return a

    def defer(self, fn):
        if not hasattr(self, "pending"):
            self.pending = []
        self.pending.append(fn)

    def flush(self):
        for fn in getattr(self, "pending", []):
            fn()
        self.pending = []

    def nextw(self):
        b = self.wi % self.NW
        self.wi += 1
        return b

    def hview(self, ncols):
        c = self.c
        return self.hreg[:, 0:c.KT * ncols].rearrange("p (k n) -> p k n", k=c.KT)

    def rmsnorm(self, src, src_key, col0, ncols, gcol, hT, hkey, use_flag, s_major=False):
        c, r = self.c, self.r
        nts = ntiles(ncols)
        ps = self.ps
        for k in range(c.KT):
            xb = self.xbuf[k % 2]
            sq = self.sq[k % 2]
            r.dma("sp", xb[:, 0:ncols], src[k * 128:(k + 1) * 128, col0:col0 + ncols],
                  reads=[f"{src_key}{k}"], writes=[f"xbuf{k%2}"])
            r.op("act", lambda e, xb=xb, sq=sq: e.activation(out=sq[:, 0:ncols], in_=xb[:, 0:ncols], func=AF.Square),
                 reads=[f"xbuf{k%2}"], writes=[f"sq{k%2}"])
            for i, (s, n) in enumerate(nts):
                r.op("pe", lambda e, i=i, s=s, n=n, sq=sq, k=k: e.matmul(
                    ps[i][:, 0:n], lhsT=self.ones_bf[:, :], rhs=sq[:, s:s + n],
                    start=(k == 0), stop=(k == c.KT - 1)),
                    reads=[f"sq{k%2}", "ones"], writes=[f"ps{i}"])
        for i, (s, n) in enumerate(nts):
            r.op("dve", lambda e, i=i, s=s, n=n: e.tensor_scalar(
                out=self.rstd[:, s:s + n], in0=ps[i][:, 0:n], scalar1=1.0 / c.D, scalar2=EPS,
                op0=ALU.mult, op1=ALU.add), reads=[f"ps{i}"], writes=["rstd"])
        r.op("act", lambda e: e.sqrt(out=self.rstd[:, 0:ncols], in_=self.rstd[:, 0:ncols]),
             reads=["rstd"], writes=["rstd"])
        r.op("dve", lambda e: e.reciprocal(out=self.rstd[:, 0:ncols], in_=self.rstd[:, 0:ncols]),
             reads=["rstd"], writes=["rstd"])
        if use_flag:
            r.op("dve", lambda e: e.tensor_scalar(
                out=self.rstd[:, 0:c.HALO], in0=self.rstd[:, 0:c.HALO], scalar1=self.flag[:, 0:1],
                scalar2=None, op0=ALU.mult), reads=["rstd", "flag"], writes=["rstd"])
        for k in range(c.KT):
            xb = self.xbuf[k % 2]
            r.dma("sp", xb[:, 0:ncols], src[k * 128:(k + 1) * 128, col0:col0 + ncols],
                  reads=[f"{src_key}{k}"], writes=[f"xbuf{k%2}"])
            if s_major:
                hs8 = self.hs8()
                r.op("dve", lambda e, xb=xb, k=k, hs8=hs8: e.scalar_tensor_tensor(
                    out=hs8[:, k, :, 0:c.NCH].rearrange("p s j -> p j s"),
                    in0=xb[:, 0:ncols].rearrange("p (j s) -> p j s", s=8),
                    scalar=self.gains[:, gcol + k:gcol + k + 1],
                    in1=self.rstd[:, 0:ncols].rearrange("p (j s) -> p j s", s=8), op0=ALU.mult, op1=ALU.mult),
                    reads=[f"xbuf{k%2}", "rstd", "gains"], writes=[hkey])
                continue
            r.op("dve", lambda e, xb=xb, k=k: e.scalar_tensor_tensor(
                out=hT[:, k, :], in0=xb[:, 0:ncols], scalar=self.gains[:, gcol + k:gcol + k + 1],
                in1=self.rstd[:, 0:ncols], op0=ALU.mult, op1=ALU.mult),
                reads=[f"xbuf{k%2}", "rstd", "gains"], writes=[hkey])

    def load_w(self, buf_i, w, col0, ncols=128, dstcol=0, eng="pool"):
        wb = self.wst[buf_i]
        self.r.dma(eng, wb[:, :, dstcol:dstcol + ncols],
                   w[:, col0:col0 + ncols].rearrange("(k p) m -> p k m", p=128),
                   writes=[f"wst{buf_i}"])
        self.flush()
        return wb

    def linear_tile(self, wb, wkey, hT, hkey, nts, bank0, col_off=0, wc0=0):
        c, r = self.c, self.r
        for i, (s, n) in enumerate(nts):
            for k in range(c.KT):
                r.op("pe", lambda e, i=i, s=s, n=n, k=k: e.matmul(
                    self.ps[bank0 + i][:, 0:n], lhsT=wb[:, k, wc0:wc0 + 128], rhs=hT[:, k, col_off + s:col_off + s + n],
                    start=(k == 0), stop=(k == c.KT - 1)),
                    reads=[wkey, hkey], writes=[f"ps{bank0+i}"])

    def attention(self):
        c, nc, r = self.c, self.nc, self.r
        ps = self.ps
        xT0 = self.inp("xT0", [c.D, c.NKV])
        w_qkv = self.inp("w_qkv", [c.D, c.QW + 2 * c.KVW])
        w_o = self.inp("w_o", [c.QW, c.D])
        bias_tab = self.inp("bias_tab", [c.NH // 2, 128, 512])
        sinks2_in = self.inp("sinks2", [128, c.NH // 2])
        negmask_in = self.inp("negmask", [128, 128])
        oT_d = self.scratch("oT_d", [c.D, c.NQB * 128], BF16)
        nblk = c.NKV // 128

        self.phase_reset()
        kdup = [self.gc(c.NKV) for i in range(2)]
        vtok = [self.gc(nblk * 64).rearrange("p (a n) -> p a n", a=nblk) for i in range(2)]
        qT = [self.gc(c.NKV) for i in range(2)]
        pbf = [self.gc(512) for i in range(2)]
        otile = [self.gc(c.NQB * 128) for i in range(2)]
        biasT = [self.fc(512) for i in range(2)]
        esb = [self.fc(512) for i in range(2)]
        esink = self.fc(c.NH // 2)
        negmask = self.fc(128)
        rd = self.fc(128)

        r.dma("sp", esink, sinks2_in[:, :], writes=["esink"])
        r.op("act", lambda e: e.activation(out=esink, in_=esink, func=AF.Exp), reads=["esink"], writes=["esink"])
        r.dma("sp", negmask, negmask_in[:, :], writes=["negmask"])

        hT = self.hview(c.NKV)
        self.rmsnorm(xT0, "xT0", 0, c.NKV, 0 * c.KT, hT, "hT", use_flag=False)
        nts = ntiles(c.NKV)
        it_ctr = 0
        for kvh in range(c.NKVH):
            kd = kdup[kvh % 2]
            kk = f"kdup{kvh%2}"
            vt = vtok[kvh % 2]
            vk = f"vtok{kvh%2}"
            b = self.nextw()
            wb = self.wst[b]
            self.load_w(b, w_qkv, c.QW + kvh * 64, 64, 0)
            self.load_w(b, w_qkv, c.QW + kvh * 64, 64, 64)
            self.load_w(b, w_qkv, c.QW + c.KVW + kvh * 64, 64, 128)
            self.linear_tile(wb, f"wst{b}", hT, "hT", nts, 0, wc0=0)
            for i, (s, n) in enumerate(nts):
                r.op("act", lambda e, i=i, s=s, n=n, kd=kd: e.copy(out=kd[:, s:s + n], in_=ps[i][:, 0:n]),
                     reads=[f"ps{i}"], writes=[kk])
            for blk in range(nblk):
                bank = 4 if blk % 2 == 0 else 5
                for k in range(c.KT):
                    r.op("pe", lambda e, blk=blk, k=k, bank=bank, wb=wb: e.matmul(
                        ps[bank][:, 0:64], lhsT=hT[:, k, blk * 128:(blk + 1) * 128], rhs=wb[:, k, 128:192],
                        start=(k == 0), stop=(k == c.KT - 1)), reads=["hT", f"wst{b}"], writes=[f"ps{bank}"])
                r.op("act", lambda e, blk=blk, bank=bank, vt=vt: e.copy(out=vt[:, blk, :], in_=ps[bank][:, 0:64]),
                     reads=[f"ps{bank}"], writes=[vk])
            for jp in range(2):
                bq = self.nextw()
                wq = self.wst[bq]
                j0 = kvh * 4 + jp * 2
                self.load_w(bq, w_qkv, j0 * 128, 256, 0)
                for jj in range(2):
                    j = j0 + jj
                    q = qT[j % 2]
                    qk = f"qT{j%2}"
                    self.linear_tile(wq, f"wst{bq}", hT, "hT", nts, 0, wc0=jj * 128)
                    for i, (s, n) in enumerate(nts):
                        r.op("act", lambda e, i=i, s=s, n=n, q=q: e.copy(out=q[:, s:s + n], in_=ps[i][:, 0:n]),
                             reads=[f"ps{i}"], writes=[qk])
                    bt = biasT[j % 2]
                    r.dma("sp", bt, bias_tab[j, :, :], writes=[f"biasT{j%2}"])
                    ot = otile[j % 2]
                    otk = f"otile{j%2}"
                    for qb in range(1, c.NQB + 1):
                        it = it_ctr % 2
                        it_ctr += 1
                        scs = [ps[6], ps[7]]
                        od = ps[4 + it]
                        odk = f"ps{4+it}"
                        e_sb = esb[it]
                        p_sb = pbf[it]
                        for hh in range(2):
                            for kb in range(2):
                                kblk = qb - 1 + kb
                                r.op("pe", lambda e, hh=hh, kb=kb, kblk=kblk, q=q, qb=qb, kd=kd, scs=scs: e.matmul(
                                    scs[hh][:, kb * 128:(kb + 1) * 128],
                                    lhsT=kd[hh * 64:(hh + 1) * 64, kblk * 128:(kblk + 1) * 128],
                                    rhs=q[hh * 64:(hh + 1) * 64, qb * 128:(qb + 1) * 128],
                                    start=True, stop=True), reads=[kk, qk], writes=[f"ps{6+hh}"])
                        for hh in range(2):
                            r.op("dve", lambda e, e_sb=e_sb, bt=bt, hh=hh, scs=scs: e.scalar_tensor_tensor(
                                out=e_sb[:, hh * 256:(hh + 1) * 256], in0=scs[hh][:, 0:256], scalar=0.125,
                                in1=bt[:, hh * 256:(hh + 1) * 256], op0=ALU.mult, op1=ALU.add),
                                reads=[f"ps{6+hh}", f"biasT{j%2}"], writes=[f"esb{it}"])
                        if qb == 2:
                            for hh in range(2):
                                r.op("dve", lambda e, e_sb=e_sb, hh=hh: e.tensor_tensor(
                                    out=e_sb[:, hh * 256:hh * 256 + 128], in0=e_sb[:, hh * 256:hh * 256 + 128],
                                    in1=negmask, op=ALU.add), reads=[f"esb{it}", "negmask"], writes=[f"esb{it}"])
                        r.op("act", lambda e, e_sb=e_sb, p_sb=p_sb: e.activation(out=p_sb, in_=e_sb, func=AF.Exp),
                             reads=[f"esb{it}"], writes=[f"pbf{it}"])
                        for hh in range(2):
                            for kb in range(2):
                                r.op("pe", lambda e, hh=hh, kb=kb, p_sb=p_sb, od=od: e.matmul(
                                    od[hh * 64:(hh + 1) * 64, 0:128], lhsT=self.ones_bf[:, 0:64],
                                    rhs=p_sb[:, (hh * 2 + kb) * 128:(hh * 2 + kb + 1) * 128],
                                    start=(kb == 0), stop=(kb == 1)), reads=[f"pbf{it}", "ones"], writes=[odk])
                        for hh in range(2):
                            for kb in range(2):
                                kblk = qb - 1 + kb
                                r.op("pe", lambda e, hh=hh, kb=kb, kblk=kblk, p_sb=p_sb, od=od, vt=vt: e.matmul(
                                    od[hh * 64:(hh + 1) * 64, 128:256], lhsT=vt[:, kblk, :],
                                    rhs=p_sb[:, (hh * 2 + kb) * 128:(hh * 2 + kb + 1) * 128],
                                    start=(kb == 0), stop=(kb == 1)), reads=[f"pbf{it}", vk], writes=[odk])
                        r.op("dve", lambda e, j=j, od=od: e.tensor_scalar(
                            out=rd, in0=od[:, 0:128], scalar1=esink[:, j:j + 1], scalar2=None, op0=ALU.add),
                            reads=[odk, "esink"], writes=["rd"])
                        r.op("dve", lambda e: e.reciprocal(out=rd, in_=rd), reads=["rd"], writes=["rd"])
                        r.op("dve", lambda e, ot=ot, qb=qb, od=od: e.tensor_tensor(
                            out=ot[:, (qb - 1) * 128:qb * 128], in0=od[:, 128:256], in1=rd, op=ALU.mult),
                            reads=[odk, "rd"], writes=[otk])
                    r.dma("sp", oT_d[j * 128:(j + 1) * 128, :], ot, reads=[otk], writes=["oT_d"])

        r.barrier()
        oT = self.hview(c.NT)
        ocol0 = c.NQB * 128 - c.NT
        for k in range(c.KT):
            r.dma("sp", oT[:, k, :], oT_d[k * 128:(k + 1) * 128, ocol0:ocol0 + c.NT], reads=["oT_d"], writes=["oT"])
        nt3 = ntiles(c.NT)
        xcol0 = c.NKV - c.NT
        for mp in range(c.KT // 2):
            b = self.nextw()
            self.load_w(b, w_o, mp * 256, 256, 0)
            for half in range(2):
                m = mp * 2 + half
                bank0 = (m % 2) * 3
                self.linear_tile(self.wst[b], f"wst{b}", oT, "oT", nt3, bank0, wc0=half * 128)
                xb = self.xbuf[m % 2]
                r.dma("sp", xb[:, 0:c.NT], xT0[m * 128:(m + 1) * 128, xcol0:xcol0 + c.NT], reads=["xT0"],
                      writes=[f"xbuf{m%2}"])
                for i, (s, n) in enumerate(nt3):
                    r.op("dve", lambda e, i=i, s=s, n=n, xb=xb, bank0=bank0: e.tensor_tensor(
                        out=xb[:, s:s + n], in0=ps[bank0 + i][:, 0:n], in1=xb[:, s:s + n], op=ALU.add),
                        reads=[f"ps{bank0+i}", f"xbuf{m%2}"], writes=[f"xbuf{m%2}"])
                r.dma("sp", self.xres[m * 128:(m + 1) * 128, :], xb[:, 0:c.NT], reads=[f"xbuf{m%2}"], writes=[f"xres{m}"])
        r.barrier()

    def ffn(self, li):
        c, nc, r = self.c, self.nc, self.r
        ps = self.ps
        w_up = self.inp(f"w_up{li}", [c.D, 2 * c.F])
        w_down = self.inp(f"w_down{li}", [c.F, c.D])
        convp_in = self.inp(f"convp{li}", [128, 4 * 2 * c.FT])
        self.phase_reset()
        if not hasattr(self, "convp_sb"):
            self.convp_sb = self.sb("convp", [128, 4 * 2 * c.FT], F32)
        convp = self.convp_sb
        act = [self.gc(c.FC * c.NT).rearrange("p (a n) -> p a n", a=c.FC)] * 2
        wdn = [self.gc(c.FC * 256).rearrange("p (a n) -> p a n", a=c.FC) for i in range(2)]
        u = [self.fc(c.NT) for i in range(2)]
        cc = [self.fc(c.NT) for i in range(2)]
        sg = self.fc(c.NT)
        yst = [self.fc(c.NT)] * 2
        r.dma("sp", convp[:], convp_in[:, :], writes=["convp"])
        cp = convp[:, :].rearrange("p (j f) -> p j f", j=4)

        hT = self.hview(c.NT)
        self.rmsnorm(self.xres, "xres", 0, c.NT, (1 if li == 0 else 3) * c.KT, hT, "hT", use_flag=True)
        nt3 = ntiles(c.NT)
        NT = c.NT
        chunks = [list(range(s, min(s + c.FC, c.FT))) for s in range(0, c.FT, c.FC)]
        evi = 0
        sti = 0
        stg = [(u[0], "u0"), (u[1], "u1"), (cc[0], "cc0"), (cc[1], "cc1"), (sg, "sg"), (yst[0], "yst")]
        for ci, chunk in enumerate(chunks):
            actb = act[0]
            ak = "actb"
            wtiles = {}
            for li_, f in enumerate(chunk):
                for which in range(2):
                    if li_ % 2 == 0:
                        b = self.nextw()
                        ncol = 256 if li_ + 1 < len(chunk) else 128
                        self.load_w(b, w_up, which * c.F + f * 128, ncol, 0)
                        wtiles[which] = b
                    b = wtiles[which]
                    bank0 = which * 3
                    self.linear_tile(self.wst[b], f"wst{b}", hT, "hT", nt3, bank0, wc0=(li_ % 2) * 128)
                    uu = u[which]
                    cw = cc[which]
                    col = which * c.FT + f
                    for i, (s, n) in enumerate(nt3):
                        r.op("act", lambda e, i=i, s=s, n=n, uu=uu, bank0=bank0: e.copy(
                            out=uu[:, s:s + n], in_=ps[bank0 + i][:, 0:n]),
                            reads=[f"ps{bank0+i}"], writes=[f"u{which}"])
                    r.op("act", lambda e, uu=uu, cw=cw, col=col: e.activation(
                        out=cw[:, :], in_=uu[:, :], func=AF.Identity, scale=cp[:, 2, col:col + 1],
                        bias=cp[:, 3, col:col + 1]), reads=[f"u{which}", "convp"], writes=[f"cc{which}"])
                    r.op("dve", lambda e, uu=uu, cw=cw, col=col: e.scalar_tensor_tensor(
                        out=cw[:, 1:NT], in0=uu[:, 0:NT - 1], scalar=cp[:, 1, col:col + 1], in1=cw[:, 1:NT],
                        op0=ALU.mult, op1=ALU.add), reads=[f"u{which}", f"cc{which}", "convp"], writes=[f"cc{which}"])
                    r.op("dve", lambda e, uu=uu, cw=cw, col=col: e.scalar_tensor_tensor(
                        out=cw[:, 2:NT], in0=uu[:, 0:NT - 2], scalar=cp[:, 0, col:col + 1], in1=cw[:, 2:NT],
                        op0=ALU.mult, op1=ALU.add), reads=[f"u{which}", f"cc{which}", "convp"], writes=[f"cc{which}"])
                r.op("act", lambda e: e.activation(out=sg[:, :], in_=cc[0][:, :], func=AF.Silu),
                     reads=["cc0"], writes=["sg"])
                r.op("dve", lambda e, actb=actb, li_=li_: e.tensor_tensor(
                    out=actb[:, li_, :], in0=sg[:, :], in1=cc[1][:, :], op=ALU.mult),
                    reads=["sg", "cc1"], writes=[ak])
            nf = len(chunk)
            f0 = chunk[0]
            for mp in range(c.KT // 2):
                wd = wdn[mp % 2]
                wk = f"wdn{mp%2}"
                r.dma("pool", wd[:, 0:nf, :],
                      w_down[f0 * 128:(f0 + nf) * 128, mp * 256:(mp + 1) * 256].rearrange("(k p) m -> p k m", p=128),
                      writes=[wk])
                self.flush()
                for half in range(2):
                    m = mp * 2 + half
                    bank0 = (m % 2) * 3
                    for i, (s, n) in enumerate(nt3):
                        for q in range(nf):
                            r.op("pe", lambda e, i=i, s=s, n=n, q=q, wd=wd, half=half, bank0=bank0, actb=actb, nf=nf: e.matmul(
                                ps[bank0 + i][:, 0:n], lhsT=wd[:, q, half * 128:(half + 1) * 128],
                                rhs=actb[:, q, s:s + n], start=(q == 0), stop=(q == nf - 1)),
                                reads=[wk, ak], writes=[f"ps{bank0+i}"])
                    st, stk = stg[sti % len(stg)]
                    sti += 1
                    for i, (s, n) in enumerate(nt3):
                        eng = "act" if (evi % 2 == 0) else "dve"
                        evi += 1
                        if eng == "act":
                            r.op("act", lambda e, i=i, s=s, n=n, st=st, bank0=bank0: e.copy(
                                out=st[:, s:s + n], in_=ps[bank0 + i][:, 0:n]),
                                reads=[f"ps{bank0+i}"], writes=[stk])
                        else:
                            r.op("dve", lambda e, i=i, s=s, n=n, st=st, bank0=bank0: e.tensor_copy(
                                out=st[:, s:s + n], in_=ps[bank0 + i][:, 0:n]),
                                reads=[f"ps{bank0+i}"], writes=[stk])
                    self.defer(lambda m=m, st=st, stk=stk: r.dma(
                        "pool", self.xres[m * 128:(m + 1) * 128, :], st[:, :], reads=[stk],
                        writes=[f"xres{m}"], accum_op=ALU.add))
        self.flush()
        r.barrier()

    JP = 144

    def hs8(self):
        c = self.c
        return self.hreg[:, 0:c.KT * 8 * self.JP].rearrange("p (k s j) -> p k s j", k=c.KT, s=8)

    def vb(self, ap, key):
        return V(ap, key)

    def fb(self, n, key, rows=None):
        ap = self.fc(n)
        if rows is not None:
            ap = ap[0:rows, :]
        return V(ap, key)

    def tt(self, o, a, b, op, eng="dve"):
        self.r.op(eng, lambda e: e.tensor_tensor(out=o.ap, in0=a.ap, in1=b.ap, op=op),
                  reads=[a.key, b.key], writes=[o.key])

    def ts(self, o, a, s1, op0, s2=None, op1=None, eng="dve"):
        rd = [a.key]
        v1 = s1
        if isinstance(s1, V):
            rd.append(s1.key)
            v1 = s1.ap
        if op1 is None:
            self.r.op(eng, lambda e: e.tensor_scalar(out=o.ap, in0=a.ap, scalar1=v1, scalar2=None, op0=op0),
                      reads=rd, writes=[o.key])
        else:
            self.r.op(eng, lambda e: e.tensor_scalar(out=o.ap, in0=a.ap, scalar1=v1, scalar2=s2, op0=op0, op1=op1),
                      reads=rd, writes=[o.key])

    def stt(self, o, a, sc, b, op0, op1, eng="dve"):
        rd = [a.key, b.key]
        v = sc
        if isinstance(sc, V):
            rd.append(sc.key)
            v = sc.ap
        self.r.op(eng, lambda e: e.scalar_tensor_tensor(out=o.ap, in0=a.ap, scalar=v, in1=b.ap, op0=op0, op1=op1),
                  reads=rd, writes=[o.key])

    def actv(self, o, a, func):
        self.r.op("act", lambda e: e.activation(out=o.ap, in_=a.ap, func=func), reads=[a.key], writes=[o.key])

    def cpy(self, o, a, eng="dve"):
        if eng == "act":
            self.r.op("act", lambda e: e.copy(out=o.ap, in_=a.ap), reads=[a.key], writes=[o.key])
        else:
            self.r.op(eng, lambda e: e.tensor_copy(out=o.ap, in_=a.ap), reads=[a.key], writes=[o.key])

    def recip(self, o, a):
        self.r.op("dve", lambda e: e.reciprocal(out=o.ap, in_=a.ap), reads=[a.key], writes=[o.key])

    def mset(self, o, val, eng="dve"):
        self.r.op(eng, lambda e: e.memset(o.ap, val), writes=[o.key])

    def sincos(self, arg, sn, cs, tmp, tmpi, phase_col=None):
        OFF = 64.0 * math.pi
        TWO_PI = 2.0 * math.pi
        if phase_col is None:
            self.ts(arg, arg, OFF, ALU.add)
        else:
            self.ts(arg, arg, phase_col, ALU.add)
        self.ts(tmp, arg, 1.0 / TWO_PI, ALU.mult)
        self.cpy(tmpi, tmp)
        self.cpy(tmp, tmpi)
        self.stt(arg, tmp, -TWO_PI, arg, ALU.mult, ALU.add)

        def wrap(hi):
            if hi:
                self.ts(tmp, arg, math.pi, ALU.is_gt, -TWO_PI, ALU.mult)
            else:
                self.ts(tmp, arg, -math.pi, ALU.is_lt, TWO_PI, ALU.mult)
            self.tt(arg, arg, tmp, ALU.add)
        wrap(True)
        wrap(False)
        self.actv(sn, arg, AF.Sin)
        if cs is not None:
            self.ts(arg, arg, 0.5 * math.pi, ALU.add)
            wrap(True)
            self.actv(cs, arg, AF.Sin)

    def ssm_setup(self):
        c, r = self.c, self.r
        if hasattr(self, "ssm_in"):
            return
        G, KT = c.G, c.KT
        Q = G // 2
        d = {}
        d["ssmc"] = self.inp("ssmc", [128, 32])
        d["ident"] = self.inp("ident", [128, 128])
        for nm in ("lamr1", "lami1", "bre1", "bim1"):
            d[nm] = self.inp(nm, [128, KT * 64])
        d["lstep1"] = self.inp("lstep1", [128, KT])
        for nm in ("lamr2", "lami2", "lstep2"):
            d[nm] = self.inp(nm, [128, G])
        for nm in ("CA2", "CB2", "BA2", "BB2"):
            d[nm] = self.inp(nm, [128, G * 16])
        d["lamrS"] = self.inp("lamrS", [Q, 128])
        d["lamiS"] = self.inp("lamiS", [Q, 128])
        d["lstepS"] = self.inp("lstepS", [Q, 2])
        d["dskip"] = self.inp("dskip", [128, KT])
        self.ssm_in = d
        self.E_d = self.scratch("E_d", [c.NCH, G * 128])
        self.X_d = self.scratch("X_d", [c.NCH, G * 128], BF16)
        self.ssmc = self.sb("ssmc", [128, 32], F32)
        self.ident = self.sb("ident", [128, 128], F32)
        self.identb = self.sb("identb", [128, 128], BF16)
        self.dskip = self.sb("dskip", [128, KT], F32)
        r.dma("sp", self.ssmc[:], d["ssmc"][:, :], writes=["ssmc"])
        r.dma("sp", self.ident[:], d["ident"][:, :], writes=["ident"])
        r.dma("sp", self.dskip[:], d["dskip"][:, :], writes=["dskip"])
        r.op("dve", lambda e: e.tensor_copy(out=self.identb[:], in_=self.ident[:]), reads=["ident"], writes=["identb"])

    def cabar_f(self, lamr, lami, lrd, lid, n, pre, rows=None):
        B = lambda k: self.fb(n, pre + k, rows)
        arg, tmp, sn, cs, mg, den, fr, fi = B("arg"), B("tmp"), B("sn"), B("cs"), B("mg"), B("den"), B("fr"), B("fi")
        tmpi = V(self.fc(n).bitcast(mybir.dt.int32) if rows is None else self.fc(n)[0:rows, :].bitcast(mybir.dt.int32),
                 pre + "tmpi")
        self.actv(mg, lrd, AF.Exp)
        self.cpy(arg, lid)
        self.sincos(arg, sn, cs, tmp, tmpi)
        self.tt(cs, cs, mg, ALU.mult)
        self.tt(sn, sn, mg, ALU.mult)
        self.ts(cs, cs, -1.0, ALU.add)
        self.tt(den, lamr, lamr, ALU.mult)
        self.tt(tmp, lami, lami, ALU.mult)
        self.tt(den, den, tmp, ALU.add)
        self.recip(den, den)
        self.tt(fr, cs, lamr, ALU.mult)
        self.tt(tmp, sn, lami, ALU.mult)
        self.tt(fr, fr, tmp, ALU.add)
        self.tt(fr, fr, den, ALU.mult)
        self.tt(fi, sn, lamr, ALU.mult)
        self.tt(tmp, cs, lami, ALU.mult)
        self.tt(fi, fi, tmp, ALU.subtract)
        self.tt(fi, fi, den, ALU.mult)
        return fr, fi

    def ssm_E(self):
        c, r = self.c, self.r
        ps = self.ps
        self.ssm_setup()
        d = self.ssm_in
        KT, JP, NCH = c.KT, self.JP, c.NCH
        self.phase_reset()
        hs8 = self.hs8()
        r.op("pool", lambda e: e.memset(self.hreg[:, 0:KT * 8 * JP], 0.0), writes=["hT"])
        self.rmsnorm(self.xres, "xres", 0, c.NT, 2 * KT, None, "hT", use_flag=True, s_major=True)

        BD = self.gc(8 * 1024).rearrange("p (s g x) -> p s g x", s=8, g=8)
        ssmc = V(self.ssmc[:, :], "ssmc")
        pw = ssmc.v(lambda a: a[:, 8:16].unsqueeze(2).broadcast_to([128, 8, 64]))
        mask4 = ssmc.v(lambda a: a[:, 0:8].unsqueeze(1).unsqueeze(3).broadcast_to([128, 8, 8, 64]))
        dl1 = self.fb(KT, "dl1")
        r.dma("sp", dl1.ap, d["lstep1"][:, :], writes=["dl1"])
        self.actv(dl1, dl1, AF.Exp)
        B64 = lambda k: self.fb(64, "E" + k)
        lamr, lami, bre, bim, lrd, lid = B64("lamr"), B64("lami"), B64("bre"), B64("bim"), B64("lrd"), B64("lid")
        bbr, bbi, t64 = B64("bbr"), B64("bbi"), B64("t64")
        B512 = lambda k: self.fb(512, "E" + k)
        ARG, MAG, SN, CS, W1, W2, TMP = B512("ARG"), B512("MAG"), B512("SN"), B512("CS"), B512("W1"), B512("W2"), B512("TMP")
        TMPI = V(self.fc(512).bitcast(mybir.dt.int32), "ETMPI")
        v3 = lambda a: a.rearrange("p (s x) -> p s x", s=8)
        b3 = lambda a: a.unsqueeze(1).broadcast_to([128, 8, 64])
        foff0 = self.foff
        rows_of = [(0, 64), (64, NCH - 64)]
        for dt in range(KT):
            self.foff = foff0
            for nm, tl in (("lamr1", lamr), ("lami1", lami), ("bre1", bre), ("bim1", bim)):
                r.dma("sp", tl.ap, d[nm][:, dt * 64:(dt + 1) * 64], writes=[tl.key])
            dcol = dl1.v(lambda a, dt=dt: a[:, dt:dt + 1])
            self.ts(lrd, lamr, dcol, ALU.mult)
            self.ts(lid, lami, dcol, ALU.mult)
            self.tt(MAG.v(v3), lrd.v(b3), pw, ALU.mult)
            self.actv(MAG, MAG, AF.Exp)
            fr, fi = self.cabar_f(lamr, lami, lrd, lid, 64, "Ef")
            self.tt(bbr, fr, bre, ALU.mult)
            self.tt(t64, fi, bim, ALU.mult)
            self.tt(bbr, bbr, t64, ALU.subtract)
            self.tt(bbi, fr, bim, ALU.mult)
            self.tt(t64, fi, bre, ALU.mult)
            self.tt(bbi, bbi, t64, ALU.add)
            self.tt(ARG.v(v3), lid.v(b3), pw, ALU.mult)
            self.sincos(ARG, SN, CS, TMP, TMPI)
            self.tt(CS, CS, MAG, ALU.mult)
            self.tt(SN, SN, MAG, ALU.mult)
            self.tt(W1.v(v3), CS.v(v3), bbr.v(b3), ALU.mult)
            self.tt(TMP.v(v3), SN.v(v3), bbi.v(b3), ALU.mult)
            self.tt(W1, W1, TMP, ALU.subtract)
            self.tt(W2.v(v3), CS.v(v3), bbi.v(b3), ALU.mult)
            self.tt(TMP.v(v3), SN.v(v3), bbr.v(b3), ALU.mult)
            self.tt(W2, W2, TMP, ALU.add)
            for comp, W in ((0, W1), (1, W2)):
                self.tt(V(BD[:, :, :, comp * 64:(comp + 1) * 64], "BD"),
                        W.v(lambda a: v3(a).unsqueeze(2).broadcast_to([128, 8, 8, 64])), mask4, ALU.mult, eng="pool")
            for jb, (j0, nr) in enumerate(rows_of):
                est = self.xbuf[jb]
                for s_ in range(8):
                    for half in range(2):
                        bank = jb * 2 + half
                        r.op("pe", lambda e, s_=s_, j0=j0, nr=nr, half=half, bank=bank, dt=dt: e.matmul(
                            ps[bank][0:nr, 0:512], lhsT=hs8[:, dt, s_, j0:j0 + nr],
                            rhs=BD[:, s_, half * 4:(half + 1) * 4, :].rearrange("p g x -> p (g x)"),
                            start=(s_ == 0), stop=(s_ == 7)), reads=["hT", "BD"], writes=[f"ps{bank}"])
                for half in range(2):
                    bank = jb * 2 + half
                    self.cpy(V(est[0:nr, half * 512:(half + 1) * 512], f"xbuf{jb}"),
                             V(ps[bank][0:nr, 0:512], f"ps{bank}"), eng=("act" if half == 0 else "dve"))
                r.dma("sp", self.E_d[j0:j0 + nr, dt * 1024:(dt + 1) * 1024], est[0:nr, 0:1024],
                      reads=[f"xbuf{jb}"], writes=["E_d"])
        r.barrier()

    def ssm_scan(self, passB, fused=False):
        c, r = self.c, self.r
        self.ssm_setup()
        d = self.ssm_in
        G, NCH = c.G, c.NCH
        Q = G // 2
        self.phase_reset()
        B = lambda n, k: self.fb(n, "S" + k, Q)
        lamr, lami, dl, lrd, lid = B(128, "lamr"), B(128, "lami"), B(2, "dl"), B(128, "lrd"), B(128, "lid")
        ARG, MAG, SN, CS, TMP = B(128, "ARG"), B(128, "MAG"), B(128, "SN"), B(128, "CS"), B(128, "TMP")
        TMPI = V(self.fc(128)[0:Q, :].bitcast(mybir.dt.int32), "STMPI")
        X, t1, t2 = B(256, "X"), B(256, "t1"), B(256, "t2")
        r.dma("sp", lamr.ap, d["lamrS"][:, :], writes=[lamr.key])
        r.dma("sp", lami.ap, d["lamiS"][:, :], writes=[lami.key])
        r.dma("sp", dl.ap, d["lstepS"][:, :], writes=[dl.key])
        self.actv(dl, dl, AF.Exp)
        for gi in range(2):
            sl = lambda a, gi=gi: a[:, gi * 64:(gi + 1) * 64]
            dcol = dl.v(lambda a, gi=gi: a[:, gi:gi + 1])
            self.ts(lrd.v(sl), lamr.v(sl), dcol, ALU.mult)
            self.ts(lid.v(sl), lami.v(sl), dcol, ALU.mult)

        def power(n, tag):
            oR, oI, oIn = B(128, tag + "R"), B(128, tag + "I"), B(128, tag + "In")
            self.ts(MAG, lrd, float(n), ALU.mult)
            self.actv(MAG, MAG, AF.Exp)
            self.ts(ARG, lid, float(n), ALU.mult)
            self.sincos(ARG, SN, CS, TMP, TMPI)
            self.tt(oR, CS, MAG, ALU.mult)
            self.tt(oI, SN, MAG, ALU.mult)
            self.ts(oIn, oI, -1.0, ALU.mult)
            return oR, oI, oIn

        x4 = lambda a: a.rearrange("q (g k p) -> q g k p", g=2, k=2)
        a4 = lambda a: a.rearrange("q (g p) -> q g p", g=2)

        def cstep(Xs, Ein, A, outX):
            aR, aI, aIn = A
            self.tt(t1.v(x4), Xs.v(x4), aR.v(lambda a: a4(a).unsqueeze(2).broadcast_to([Q, 2, 2, 64])), ALU.mult)
            self.tt(t2.v(lambda a: x4(a)[:, :, 0, :]), Xs.v(lambda a: x4(a)[:, :, 1, :]), aIn.v(a4), ALU.mult)
            self.tt(t2.v(lambda a: x4(a)[:, :, 1, :]), Xs.v(lambda a: x4(a)[:, :, 0, :]), aI.v(a4), ALU.mult)
            self.tt(t1, t1, t2, ALU.add)
            self.tt(outX, t1, Ein, ALU.add)

        A8 = power(8, "A8")
        self.mset(X, 0.0)
        if passB:
            if fused:
                nsrc = 4
                Lsrc = [self.Lall_d[rr * Q:(rr + 1) * Q, :] for rr in range(nsrc)]
                lkey = "Lall_d"
            else:
                nsrc = 8
                Lall_in = self.inp("Lall", [8, Q, 256])
                Lsrc = [Lall_in[rr, :, :] for rr in range(nsrc)]
                lkey = None
            wsel_in = self.inp("wsel", [Q, 8])
            A1k = power(1024, "A1k")
            wsel, Lt, Xn = B(8, "wsel"), B(256, "Lt"), B(256, "Xn")
            r.dma("sp", wsel.ap, wsel_in[:, :], writes=[wsel.key])
            for rr in range(nsrc):
                r.dma("sp", Lt.ap, Lsrc[rr], reads=([lkey] if lkey else []), writes=[Lt.key])
                cstep(X, Lt, A1k, Xn)
                self.tt(t2, Xn, X, ALU.subtract)
                self.stt(X, t2, wsel.v(lambda a, rr=rr: a[:, rr:rr + 1]), X, ALU.mult, ALU.add)
        JB = 5
        last = NCH - 1 if passB else NCH - 2
        xst = [V(self.gc(JB * 256)[0:Q, :], f"xst{i}") for i in range(2)]
        if passB:
            z = xst[0]
            self.mset(z.v(lambda a: a[:, 0:256]), 0.0, eng="pool")
            r.dma("sp", self.X_d[0:1, :].rearrange("j (q f) -> q j f", f=256),
                  z.ap[:, 0:256].rearrange("q (j f) -> q j f", j=1), reads=[z.key], writes=["X_d"])
        bi = 0
        for j0 in range(1, last + 1, JB):
            nj = min(JB, last + 1 - j0)
            eb = V(self.xbuf[bi % 2][0:Q, 0:JB * 256], f"xbuf{bi%2}")
            xs = xst[bi % 2]
            bi += 1
            r.dma("sp", eb.ap[:, 0:nj * 256].rearrange("q (j f) -> q j f", j=nj),
                  self.E_d[j0:j0 + nj, :].rearrange("j (q f) -> q j f", f=256), reads=["E_d"], writes=[eb.key])
            for jj in range(nj):
                sl = lambda a, jj=jj: a[:, jj * 256:(jj + 1) * 256]
                if passB:
                    self.cpy(xs.v(sl), X, eng="act")
                cstep(X, eb.v(sl), A8, X)
            if passB:
                r.dma("sp", self.X_d[j0:j0 + nj, :].rearrange("j (q f) -> q j f", f=256),
                      xs.ap[:, 0:nj * 256].rearrange("q (j f) -> q j f", j=nj), reads=[xs.key], writes=["X_d"])
        if not passB and not fused:
            Lout = self.outp("Lout", [Q, 256])
            r.dma("sp", Lout[:, :], X.ap, reads=[X.key], writes=["Lout"])
            self.final_keys = getattr(self, "final_keys", []) + ["Lout"]
        if not passB and fused:
            self.Lsrc_d = self.scratch("Lsrc_d", [Q, 256])
            self.Lall_d = self.scratch("Lall_d", [4 * Q, 256])
            r.dma("sp", self.Lsrc_d[:, :], X.ap, reads=[X.key], writes=["Lsrc_d"])
            src_t, dst_t = self.Lsrc_d, self.Lall_d
            r.op("pool", lambda e: e.collective_compute("AllGather", ALU.bypass,
                                                        replica_groups=[[0, 1, 2, 3], [4, 5, 6, 7]],
                                                        ins=[src_t[:, :]], outs=[dst_t[:, :]]),
                 reads=["Lsrc_d"], writes=["Lall_d"], cc=True)
        r.barrier()

    def ssm_y(self):
        c, r = self.c, self.r
        ps = self.ps
        self.ssm_setup()
        d = self.ssm_in
        G, KT, NCH, JP = c.G, c.KT, c.NCH, self.JP
        self.phase_reset()
        hs8 = self.hs8()
        MD = self.gc(8 * 1024).rearrange("p (s g t x) -> p s g t x", s=8, g=8, t=8)
        Hb = self.gc(8 * 128).rearrange("p (g x) -> p g x", g=8)
        XT = self.gc(8 * 72).rearrange("p (g x) -> p g x", g=8)
        Xtok = [self.gc(1024) for i in range(2)]
        lamr2, lami2, dl2, lrd2, lid2 = (self.fb(G, "Y" + k) for k in ("lamr2", "lami2", "dl2", "lrd2", "lid2"))
        r.dma("sp", lamr2.ap, d["lamr2"][:, :], writes=[lamr2.key])
        r.dma("sp", lami2.ap, d["lami2"][:, :], writes=[lami2.key])
        r.dma("sp", dl2.ap, d["lstep2"][:, :], writes=[dl2.key])
        self.actv(dl2, dl2, AF.Exp)
        self.tt(lrd2, lamr2, dl2, ALU.mult)
        self.tt(lid2, lami2, dl2, ALU.mult)
        ssmc = V(self.ssmc[:, :], "ssmc")
        tau = ssmc.v(lambda a: a[:, 16:25].unsqueeze(1).broadcast_to([128, 8, 9]))
        phA = ssmc.v(lambda a: a[:, 25:26])
        phB = ssmc.v(lambda a: a[:, 26:27])
        sgn = ssmc.v(lambda a: a[:, 27:28])
        B72 = lambda k: self.fb(72, "Y" + k)
        ARG, MG, TA, TB, TM = B72("ARG"), B72("MG"), B72("TA"), B72("TB"), B72("TM")
        TMI = V(self.fc(72).bitcast(mybir.dt.int32), "YTMI")
        B128 = lambda k: self.fb(128, "Y" + k)
        CA, CB, BA, BB, Bst, Bt, Ksb, KTT = (B128(k) for k in ("CA", "CB", "BA", "BB", "Bst", "Bt", "Ksb", "KTT"))
        Wst, Wt = self.fb(1152, "YWst"), self.fb(1152, "YWt")
        ypre = self.fb(4 * 72, "Yypre")
        lr8, li8, lrd8, lid8 = (self.fb(8, "Y" + k) for k in ("lr8", "li8", "lrd8", "lid8"))
        w4 = lambda a: a.rearrange("p (g t x) -> p g t x", g=8, t=9)
        g3 = lambda a: a.rearrange("p (g x) -> p g x", g=8)
        t3 = lambda a: a.rearrange("p (g t) -> p g t", g=8)
        foff0 = self.foff
        rows_of = [(0, 64), (64, NCH - 64)]
        for dt in range(KT):
            self.foff = foff0
            gsl = lambda a, dt=dt: a[:, dt * 8:(dt + 1) * 8]
            for nm, tl in (("CA2", CA), ("CB2", CB), ("BA2", BA), ("BB2", BB)):
                r.dma("sp", tl.ap, d[nm][:, dt * 128:(dt + 1) * 128], writes=[tl.key])
            self.cpy(lr8, lamr2.v(gsl))
            self.cpy(li8, lami2.v(gsl))
            self.cpy(lrd8, lrd2.v(gsl))
            self.cpy(lid8, lid2.v(gsl))
            b9 = lambda a: a.unsqueeze(2).broadcast_to([128, 8, 9])
            self.tt(MG.v(t3), lrd8.v(b9), tau, ALU.mult)
            self.actv(MG, MG, AF.Exp)
            for TX, ph in ((TA, phA), (TB, phB)):
                self.tt(ARG.v(t3), lid8.v(b9), tau, ALU.mult)
                self.sincos(ARG, TX, None, TM, TMI, phase_col=ph)
                self.tt(TX, TX, MG, ALU.mult)
            fr2, fi2 = self.cabar_f(lr8, li8, lrd8, lid8, 8, "Yf")
            self.ts(fi2, fi2, sgn, ALU.mult)
            b16 = lambda a: a.unsqueeze(2).broadcast_to([128, 8, 16])
            self.tt(Bst.v(g3), BA.v(g3), fr2.v(b16), ALU.mult)
            self.tt(Bt.v(g3), BB.v(g3), fi2.v(b16), ALU.mult)
            self.tt(Bst, Bst, Bt, ALU.add)
            cw = lambda a: g3(a).unsqueeze(2).broadcast_to([128, 8, 9, 16])
            tw = lambda a: t3(a).unsqueeze(3).broadcast_to([128, 8, 9, 16])
            self.tt(Wst.v(w4), CA.v(cw), TA.v(tw), ALU.mult)
            self.tt(Wt.v(w4), CB.v(cw), TB.v(tw), ALU.mult)
            self.tt(Wst, Wst, Wt, ALU.add)
            self.cpy(V(Hb.rearrange("p g (t x) -> p g t x", t=8), "Hb"), Wst.v(lambda a: w4(a)[:, :, 1:9, :]), eng="act")
            psK = ps[7]
            for g8 in range(8):
                r.op("pe", lambda e, g8=g8: e.matmul(psK[:, g8 * 16:(g8 + 1) * 16],
                                                     lhsT=w4(Wst.ap)[:, g8, 0:8, :].rearrange("p t x -> p (t x)"),
                                                     rhs=g3(Bst.ap)[:, g8, :], start=True, stop=True),
                     reads=[Wst.key, Bst.key], writes=["ps7"])
            self.cpy(Ksb, V(psK[:, 0:128], "ps7"), eng="act")
            r.op("pe", lambda e: e.transpose(out=psK[:, 128:256], in_=Ksb.ap, identity=self.ident[:, :]),
                 reads=[Ksb.key, "ident"], writes=["ps7"])
            self.cpy(KTT, V(psK[:, 128:256], "ps7"), eng="act")
            self.mset(V(MD.rearrange("p s g t x -> p (s g t x)"), "MD"), 0.0, eng="pool")
            for s_ in range(8):
                nt_ = 8 - s_
                self.tt(V(MD[:, s_, :, s_:8, :], "MD"),
                        KTT.v(lambda a, nt_=nt_: a[:, 0:nt_ * 16].rearrange("p (t x) -> p t x", t=nt_).unsqueeze(1)
                              .broadcast_to([128, 8, nt_, 16])),
                        ssmc.v(lambda a, nt_=nt_: a[:, 0:8].unsqueeze(2).unsqueeze(3).broadcast_to([128, 8, nt_, 16])),
                        ALU.mult, eng="pool")
            for jb, (j0, nr) in enumerate(rows_of):
                xt_ = Xtok[jb]
                r.dma("sp", xt_[0:nr, :], self.X_d[j0:j0 + nr, dt * 1024:(dt + 1) * 1024], reads=["X_d"],
                      writes=[f"Xtok{jb}"])
                psX = ps[4].bitcast(BF16)
                for g8 in range(8):
                    r.op("pe", lambda e, g8=g8, nr=nr, xt_=xt_: e.transpose(
                        out=psX[:, g8 * 72:g8 * 72 + nr], in_=xt_[0:nr, g8 * 128:(g8 + 1) * 128],
                        identity=self.identb[0:nr, 0:nr]), reads=[f"Xtok{jb}", "identb"], writes=["ps4"])
                self.cpy(V(XT[:, :, 0:nr], "XT"),
                         V(psX[:, 0:576].rearrange("p (g x) -> p g x", g=8)[:, :, 0:nr], "ps4"), eng="act")
                ytok = self.xbuf[jb]
                for half in range(2):
                    bank = jb * 2 + half
                    for s_ in range(8):
                        r.op("pe", lambda e, s_=s_, j0=j0, nr=nr, half=half, bank=bank, dt=dt: e.matmul(
                            ps[bank][0:nr, 0:512], lhsT=hs8[:, dt, s_, j0:j0 + nr],
                            rhs=MD[:, s_, half * 4:(half + 1) * 4, :, :].rearrange("p g t x -> p (g t x)"),
                            start=(s_ == 0), stop=False), reads=["hT", "MD"], writes=[f"ps{bank}"])
                    for gq in range(4):
                        g8 = half * 4 + gq
                        r.op("pe", lambda e, g8=g8, gq=gq, nr=nr, bank=bank: e.matmul(
                            ps[bank][0:nr, gq * 128:(gq + 1) * 128], lhsT=XT[:, g8, 0:nr], rhs=Hb[:, g8, :],
                            start=False, stop=(gq == 3)), reads=["XT", "Hb"], writes=[f"ps{bank}"])
                    self.cpy(V(ytok[0:nr, 0:1024].rearrange("j (t g x) -> j g t x", t=8, g=8)[:, half * 4:(half + 1) * 4, :, :],
                               f"xbuf{jb}"),
                             V(ps[bank][0:nr, 0:512].rearrange("j (g t x) -> j g t x", g=4, t=8), f"ps{bank}"),
                             eng=("act" if half == 0 else "dve"))
                for th in range(2):
                    bank = 5 + th
                    for tq in range(4):
                        t_ = th * 4 + tq
                        r.op("pe", lambda e, t_=t_, tq=tq, nr=nr, ytok=ytok, bank=bank: e.transpose(
                            out=ps[bank][:, tq * 72:tq * 72 + nr], in_=ytok[0:nr, t_ * 128:(t_ + 1) * 128],
                            identity=self.ident[0:nr, 0:nr]), reads=[f"xbuf{jb}", "ident"], writes=[f"ps{bank}"])
                    yp = ypre.v(lambda a, nr=nr: a.rearrange("p (t x) -> p t x", t=4)[:, :, 0:nr])
                    hsl = V(hs8[:, dt, th * 4:(th + 1) * 4, j0:j0 + nr], "hT")
                    self.stt(yp, hsl, V(self.dskip[:, dt:dt + 1], "dskip"),
                             V(ps[bank][:, 0:288].rearrange("p (t x) -> p t x", t=4)[:, :, 0:nr], f"ps{bank}"),
                             ALU.mult, ALU.add)
                    self.actv(hsl, yp, AF.Gelu_apprx_tanh)
        r.barrier()

    def glu(self):
        c, r = self.c, self.r
        ps = self.ps
        KT, JP, NCH = c.KT, self.JP, c.NCH
        w_glu = self.inp("w_glu", [c.D, 2 * c.D])
        self.phase_reset()
        gy = self.hreg[:, 0:KT * 8 * JP].rearrange("p (k n) -> p k n", k=KT)
        sig = self.fc(8 * JP)
        ysts = [self.fc(c.NT) for i in range(3)]
        ntl = [(0, 3), (3, 3), (6, 2)]
        gtiles = {}
        for m in range(KT):
            for which in range(2):
                if m % 2 == 0:
                    b = self.nextw()
                    self.load_w(b, w_glu, which * c.D + m * 128, 256, 0)
                    gtiles[which] = b
                b = gtiles[which]
                wb = self.wst[b]
                wc0 = (m % 2) * 128
                for i, (t0, ntt) in enumerate(ntl):
                    bank = which * 3 + i
                    for k in range(KT):
                        r.op("pe", lambda e, k=k, wb=wb, t0=t0, ntt=ntt, bank=bank, wc0=wc0: e.matmul(
                            ps[bank][:, 0:ntt * JP], lhsT=wb[:, k, wc0:wc0 + 128], rhs=gy[:, k, t0 * JP:(t0 + ntt) * JP],
                            start=(k == 0), stop=(k == KT - 1)), reads=[f"wst{b}", "hT"], writes=[f"ps{bank}"])
            yst = ysts[m % 3]
            ystk = f"gyst{m%3}"
            for i, (t0, ntt) in enumerate(ntl):
                r.op("act", lambda e, i=i, t0=t0, ntt=ntt: e.activation(
                    out=sig[:, t0 * JP:(t0 + ntt) * JP], in_=ps[3 + i][:, 0:ntt * JP], func=AF.Sigmoid),
                    reads=[f"ps{3+i}"], writes=["sig"])
                r.op("dve", lambda e, i=i, t0=t0, ntt=ntt, yst=yst: e.tensor_tensor(
                    out=yst[:, 0:c.NT].rearrange("p (j t) -> p t j", t=8)[:, t0:t0 + ntt, :],
                    in0=ps[i][:, 0:ntt * JP].rearrange("p (t j) -> p t j", j=JP)[:, :, 0:NCH],
                    in1=sig[:, t0 * JP:(t0 + ntt) * JP].rearrange("p (t j) -> p t j", j=JP)[:, :, 0:NCH], op=ALU.mult),
                    reads=[f"ps{i}", "sig"], writes=[ystk])
            self.defer(lambda m=m, yst=yst, ystk=ystk: r.dma(
                "pool", self.xres[m * 128:(m + 1) * 128, :], yst[:, :], reads=[ystk], writes=[f"xres{m}"],
                accum_op=ALU.add))
        self.flush()
        r.barrier()

    def load_x(self, name):
        c, r = self.c, self.r
        xin = self.inp(name, [c.D, c.NT])
        for m in range(c.KT):
            xb = self.xbuf[m % 2]
            r.dma("sp", xb[:, 0:c.NT], xin[m * 128:(m + 1) * 128, :], writes=[f"xbuf{m%2}"])
            r.dma("sp", self.xres[m * 128:(m + 1) * 128, :], xb[:, 0:c.NT], reads=[f"xbuf{m%2}"], writes=[f"xres{m}"])
        r.barrier()

    def copy_x_to_xres(self):
        c, r = self.c, self.r
        xT0 = self.inp("xT0", [c.D, c.NKV])
        for m in range(c.KT):
            xb = self.xbuf[m % 2]
            r.dma("sp", xb[:, 0:c.NT], xT0[m * 128:(m + 1) * 128, c.NKV - c.NT:c.NKV], writes=[f"xbuf{m%2}"])
            r.dma("sp", self.xres[m * 128:(m + 1) * 128, :], xb[:, 0:c.NT], reads=[f"xbuf{m%2}"], writes=[f"xres{m}"])
        r.barrier()

    def dump_xres(self, name):
        c, r = self.c, self.r
        o = self.outp(name, [c.D, c.NT])
        for m in range(c.KT):
            xb = self.xbuf[m % 2]
            r.dma("sp", xb[:, 0:c.NT], self.xres[m * 128:(m + 1) * 128, :], reads=[f"xres{m}"], writes=[f"xbuf{m%2}"])
            r.dma("sp", o[m * 128:(m + 1) * 128, :], xb[:, 0:c.NT], reads=[f"xbuf{m%2}"], writes=[name])
        self.final_keys = getattr(self, "final_keys", []) + [name]

    def final_norm(self):
        c, r = self.c, self.r
        ps = self.ps
        o = self.outp("outT", [c.D, c.NOWN])
        ncols = c.NT
        nts = ntiles(ncols)
        src = self.xres
        for k in range(c.KT):
            xb = self.xbuf[k % 2]
            sq = self.sq[k % 2]
            r.dma("sp", xb[:, 0:ncols], src[k * 128:(k + 1) * 128, :], reads=[f"xres{k}"], writes=[f"xbuf{k%2}"])
            r.op("act", lambda e, xb=xb, sq=sq: e.activation(out=sq[:, 0:ncols], in_=xb[:, 0:ncols], func=AF.Square),
                 reads=[f"xbuf{k%2}"], writes=[f"sq{k%2}"])
            for i, (s, n) in enumerate(nts):
                r.op("pe", lambda e, i=i, s=s, n=n, sq=sq, k=k: e.matmul(
                    ps[i][:, 0:n], lhsT=self.ones_bf[:, :], rhs=sq[:, s:s + n],
                    start=(k == 0), stop=(k == c.KT - 1)), reads=[f"sq{k%2}", "ones"], writes=[f"ps{i}"])
        for i, (s, n) in enumerate(nts):
            r.op("dve", lambda e, i=i, s=s, n=n: e.tensor_scalar(
                out=self.rstd[:, s:s + n], in0=ps[i][:, 0:n], scalar1=1.0 / c.D, scalar2=EPS,
                op0=ALU.mult, op1=ALU.add), reads=[f"ps{i}"], writes=["rstd"])
        r.op("act", lambda e: e.sqrt(out=self.rstd[:, 0:ncols], in_=self.rstd[:, 0:ncols]),
             reads=["rstd"], writes=["rstd"])
        r.op("dve", lambda e: e.reciprocal(out=self.rstd[:, 0:ncols], in_=self.rstd[:, 0:ncols]),
             reads=["rstd"], writes=["rstd"])
        gcol = 4 * c.KT
        for k in range(c.KT):
            xb = self.xbuf[k % 2]
            r.dma("sp", xb[:, 0:ncols], src[k * 128:(k + 1) * 128, :], reads=[f"xres{k}"], writes=[f"xbuf{k%2}"])
            r.op("dve", lambda e, xb=xb, k=k: e.scalar_tensor_tensor(
                out=xb[:, 0:ncols], in0=xb[:, 0:ncols], scalar=self.gains[:, gcol + k:gcol + k + 1],
                in1=self.rstd[:, 0:ncols], op0=ALU.mult, op1=ALU.mult),
                reads=[f"xbuf{k%2}", "rstd", "gains"], writes=[f"xbuf{k%2}"])
            r.dma("sp", o[k * 128:(k + 1) * 128, :], xb[:, c.HALO:c.HALO + c.NOWN], reads=[f"xbuf{k%2}"],
                  writes=["outT"])
        self.final_keys = getattr(self, "final_keys", []) + ["outT"]

    def finish(self):
        nc, r = self.nc, self.r
        r.wait_keys("sp", getattr(self, "final_keys", []))
        with contextlib.ExitStack() as st:
            sems = {e: st.enter_context(nc.semaphore("s_" + e)) for e in ENGS}
            dsems = {(e, s): st.enter_context(nc.semaphore(f"d_{e}{s}")) for e in ("sp", "pool", "act")
                     for s in range(NSLOT)}
            dsems[("cc", 0)] = st.enter_context(nc.semaphore("cc_sem"))
            r.emit(nc, sems, dsems)
        return nc


def build(cfg, phases):
    p = Prog(cfg, phases)
    p.setup_common()
    for ph in phases:
        if ph == "attn":
            p.attention()
        elif ph == "copyx":
            p.copy_x_to_xres()
        elif ph == "ffn0":
            p.ffn(0)
        elif ph == "ffn1":
            p.ffn(1)
        elif ph == "final":
            p.final_norm()
        elif ph == "ssm":
            p.ssm_E()
            p.ssm_scan(False, fused=True)
            p.ssm_scan(True, fused=True)
            p.ssm_y()
            p.glu()
        elif ph == "ssmA":
            p.ssm_E()
            p.ssm_scan(False)
        elif ph == "ssmB":
            p.ssm_E()
            p.ssm_scan(True)
            p.ssm_y()
            p.glu()
        elif ph.startswith("loadx:"):
            p.load_x(ph[6:])
        elif ph.startswith("dump:"):
            p.dump_xres(ph[5:])
        else:
            raise ValueError(ph)
    nc = p.finish()
    return p, nc


MASKV = -30000.0


def _t5_buckets(dist):
    n = np.maximum(dist, 0)
    is_small = n < 16
    large = 16 + (np.log(np.maximum(n, 1) / 16) / np.log(128 / 16) * (32 - 16)).astype(np.int32)
    large = np.minimum(large, 31)
    return np.where(is_small, n, large).astype(np.int32)


def _prep_shared(cfg, I):
    D, F, KT, FT, G, NH = cfg.D, cfg.F, cfg.KT, cfg.FT, cfg.G, cfg.NH
    m = {}
    vecs = [I['attn_norm'][0], I['ffn_norm'][0], I['ssm_norm'][0], I['ffn_norm'][1], I['final_norm']]
    m['gains'] = np.ascontiguousarray(np.concatenate([np.asarray(v).reshape(KT, 128).T for v in vecs], axis=1),
                                      dtype=np.float32)
    s = np.arange(128)[:, None]
    q = np.arange(128)[None, :]
    rb = np.asarray(I['rel_bias'])
    bt = np.empty((NH // 2, 128, 4, 128), np.float32)
    for kb in range(2):
        dist = q - s + (128 if kb == 0 else 0)
        valid = (dist >= 0) & (dist < 128)
        bk = _t5_buckets(dist)
        for j in range(NH // 2):
            for hh in range(2):
                bt[j, :, hh * 2 + kb, :] = np.where(valid, rb[bk, 2 * j + hh], np.float32(MASKV))
    m['bias_tab'] = bt.reshape(NH // 2, 128, 512)
    sk = np.asarray(I['sinks'][0])
    s2 = np.empty((128, NH // 2), np.float32)
    s2[:64, :] = sk[0::2][None, :]
    s2[64:, :] = sk[1::2][None, :]
    m['sinks2'] = s2
    m['w_qkv'] = I['w_qkv'][0]
    m['w_o'] = I['w_o'][0]
    for li in range(2):
        cw = np.asarray(I['conv_w'][li])
        cb = np.asarray(I['conv_b'][li])
        rows = [cw[0], cw[1], cw[2], cb]
        cp = np.stack([r_.reshape(2 * FT, 128).T for r_ in rows], axis=1)
        m[f'convp{li}'] = np.ascontiguousarray(cp.reshape(128, 4 * 2 * FT), dtype=np.float32)
        m[f'w_up{li}'] = I['w_up'][li]
        m[f'w_down{li}'] = I['w_down'][li]
    lr = np.asarray(I['lambda_re'][0]); li_ = np.asarray(I['lambda_im'][0]); ls = np.asarray(I['log_step'][0])
    br = np.asarray(I['b_re'][0]); bi = np.asarray(I['b_im'][0])
    cr = np.asarray(I['c_re'][0]); ci = np.asarray(I['c_im'][0])
    OFF = 64.0 * math.pi
    sc = np.zeros((128, 32), np.float32)
    p = np.arange(128)
    sc[:, 0:8] = (p[:, None] // 16 == np.arange(8)[None, :])
    sc[:, 8:16] = 7 - np.arange(8)[None, :]
    sc[:, 16:25] = np.arange(9)[None, :]
    sc[:, 25] = np.where(p < 64, 0.5 * math.pi, math.pi) + OFF
    sc[:, 26] = np.where(p < 64, math.pi, 1.5 * math.pi) + OFF
    sc[:, 27] = np.where(p < 64, -1.0, 1.0)
    m['ssmc'] = sc
    m['ident'] = np.eye(128, dtype=np.float32)
    g_of = (np.arange(KT)[None, :] * 8 + (p[:, None] // 16))
    cidx = p % 16
    m['lamr1'] = np.ascontiguousarray(lr[g_of].reshape(128, KT * 64))
    m['lami1'] = np.ascontiguousarray(li_[g_of].reshape(128, KT * 64))
    m['bre1'] = np.ascontiguousarray(br[g_of, :, cidx[:, None]].reshape(128, KT * 64))
    m['bim1'] = np.ascontiguousarray(bi[g_of, :, cidx[:, None]].reshape(128, KT * 64))
    m['lstep1'] = np.ascontiguousarray(ls[g_of])
    pp = p % 64
    m['lamr2'] = np.ascontiguousarray(lr[:, pp].T)
    m['lami2'] = np.ascontiguousarray(li_[:, pp].T)
    m['lstep2'] = np.ascontiguousarray(np.broadcast_to(ls[None, :], (128, G)), dtype=np.float32)
    crT = cr.transpose(2, 0, 1)
    ciT = ci.transpose(2, 0, 1)
    m['CA2'] = np.ascontiguousarray(np.concatenate([crT, crT], 0).reshape(128, G * 16))
    m['CB2'] = np.ascontiguousarray(np.concatenate([ciT, ciT], 0).reshape(128, G * 16))
    brT = br.transpose(1, 0, 2)
    biT = bi.transpose(1, 0, 2)
    m['BA2'] = np.ascontiguousarray(np.concatenate([brT, biT], 0).reshape(128, G * 16))
    m['BB2'] = np.ascontiguousarray(np.concatenate([biT, brT], 0).reshape(128, G * 16))
    m['lamrS'] = np.ascontiguousarray(lr.reshape(G // 2, 128))
    m['lamiS'] = np.ascontiguousarray(li_.reshape(G // 2, 128))
    m['lstepS'] = np.ascontiguousarray(ls.reshape(G // 2, 2))
    m['dskip'] = np.ascontiguousarray(np.asarray(I['d_skip'][0]).reshape(KT, 128).T)
    m['w_glu'] = I['w_glu'][0]
    return m


def _prep_core(c, cfg, I):
    b = c // 4
    ch = c % 4
    t0 = ch * cfg.NOWN
    m = {}
    xs = np.zeros((cfg.NKV, cfg.D), np.float32)
    lo = t0 - cfg.KVH
    src_lo = max(lo, 0)
    xs[src_lo - lo:, :] = I['x'][b, src_lo:t0 + cfg.NOWN, :]
    m['xT0'] = np.ascontiguousarray(xs.T)
    m['flag'] = np.full((128, 1), 0.0 if ch == 0 else 1.0, np.float32)
    m['negmask'] = np.full((128, 128), MASKV if ch == 0 else 0.0, np.float32)
    w = np.zeros((cfg.G // 2, 8), np.float32)
    if FUSED:
        w[:, 0:ch] = 1.0
    else:
        for r_ in range(8):
            if r_ // 4 == b and r_ % 4 < ch:
                w[:, r_] = 1.0
    m['wsel'] = w
    return m


FUSED = True
PHASES = ["attn", "ffn0", "ssm", "ffn1", "final"]
PHASES1 = ["attn", "ffn0", "ssmA", "dump:x2T"]
PHASES2 = ["loadx:x2T", "ssmB", "ffn1", "final"]


def run_model(cfg, I):
    shared = _prep_shared(cfg, I)
    cores = [_prep_core(c, cfg, I) for c in range(8)]
    if FUSED:
        p, nc = build(cfg, PHASES)
        maps = [{k: (cores[c][k] if k in cores[c] else shared[k]) for k in p.din} for c in range(8)]
        res = run_bass_kernel_spmd(nc, maps, core_ids=list(range(8))).results
        out = np.empty(I['x'].shape, np.float32)
        for c in range(8):
            out[c // 4, (c % 4) * cfg.NOWN:(c % 4 + 1) * cfg.NOWN, :] = np.asarray(res[c]['outT']).T
        return out
    p1, nc1 = build(cfg, PHASES1)
    maps1 = [{k: (cores[c][k] if k in cores[c] else shared[k]) for k in p1.din} for c in range(8)]
    r1 = run_bass_kernel_spmd(nc1, maps1, core_ids=list(range(8))).results
    del maps1
    Lall = np.stack([np.asarray(r1[c]['Lout']) for c in range(8)])
    p2, nc2 = build(cfg, PHASES2)
    maps2 = []
    for c in range(8):
        mm = {}
        for k in p2.din:
            if k == 'x2T':
                mm[k] = np.asarray(r1[c]['x2T'])
            elif k == 'Lall':
                mm[k] = Lall
            elif k in cores[c]:
                mm[k] = cores[c][k]
            else:
                mm[k] = shared[k]
        maps2.append(mm)
    r2 = run_bass_kernel_spmd(nc2, maps2, core_ids=list(range(8))).results
    B = I['x'].shape[0]
    out = np.empty(I['x'].shape, np.float32)
    for c in range(8):
        b = c // 4
        ch = c % 4
        out[b, ch * cfg.NOWN:(ch + 1) * cfg.NOWN, :] = np.asarray(r2[c]['outT']).T
    return out


def kernel(**inputs):
    I = {k: np.asarray(v) for k, v in inputs.items()}
    cfg = Cfg(D=I['x'].shape[2], F=I['w_down'].shape[1])
    return run_model(cfg, I)
```

```python
import contextlib
import math
import numpy as np
import concourse.bass as bass
import concourse.mybir as mybir
from concourse.bass_utils import run_bass_kernel_spmd

F32 = mybir.dt.float32
BF16 = mybir.dt.bfloat16
ALU = mybir.AluOpType
AF = mybir.ActivationFunctionType

ENGS = ("pe", "act", "dve", "pool", "sp")
NSLOT = 8
SAME_ENGINE_IN_ORDER = ("pe",)
EPS = 1e-6
NEG = -1e30


class Op:
    __slots__ = ("eng", "fn", "deps", "dma", "idx", "needed", "sem", "val", "slot")

    def __init__(self, eng, fn, dma):
        self.eng = eng
        self.fn = fn
        self.dma = dma
        self.deps = []
        self.needed = False
        self.sem = None
        self.val = 0
        self.slot = -1


class V:
    __slots__ = ("ap", "key")

    def __init__(self, ap, key):
        self.ap = ap
        self.key = key

    def v(self, f):
        return V(f(self.ap), self.key)


class Rec:
    def __init__(self):
        self.ops = {e: [] for e in ENGS}
        self.lastw = {}
        self.readers = {}
        self.seen = {e: {p: -1 for p in ENGS} for e in ENGS}
        self.seen_dma = {e: set() for e in ENGS}
        self.slot_last = {}
        self.slot_rr = {e: 0 for e in ENGS}

    def _add_dep(self, op, prod):
        if prod is None or prod is op:
            return
        e = op.eng
        if prod.dma:
            if id(prod) in self.seen_dma[e]:
                return
            self.seen_dma[e].add(id(prod))
        else:
            if prod.eng == e and e in SAME_ENGINE_IN_ORDER:
                return
            if prod.idx <= self.seen[e][prod.eng]:
                return
            self.seen[e][prod.eng] = prod.idx
        prod.needed = True
        op.deps.append(prod)

    def op(self, eng, fn, reads=(), writes=(), dma=False, cc=False):
        o = Op(eng, fn, dma or cc)
        o.idx = len(self.ops[eng])
        for k in reads:
            self._add_dep(o, self.lastw.get(k))
        for k in writes:
            self._add_dep(o, self.lastw.get(k))
            for r in self.readers.get(k, ()):
                self._add_dep(o, r)
        if cc:
            o.slot = "cc"
            prev = self.slot_last.get(("cc", 0))
            if prev is not None:
                self._add_dep(o, prev)
            self.slot_last[("cc", 0)] = o
            o.needed = True
        elif dma:
            s = self.slot_rr[eng]
            self.slot_rr[eng] = (s + 1) % NSLOT
            o.slot = s
            prev = self.slot_last.get((eng, s))
            if prev is not None:
                self._add_dep(o, prev)
            self.slot_last[(eng, s)] = o
            o.needed = True
        for k in reads:
            self.readers.setdefault(k, []).append(o)
        for k in writes:
            self.lastw[k] = o
            self.readers[k] = []
        self.ops[eng].append(o)
        return o

    def dma(self, eng, out, in_, reads=(), writes=(), **kw):
        return self.op(eng, lambda e: e.dma_start(out=out, in_=in_, **kw), reads, writes, dma=True)

    def barrier(self):
        lasts = []
        for e in ENGS:
            for o in reversed(self.ops[e]):
                if not o.dma and o.fn is not None:
                    lasts.append(o)
                    break
        dmas = [o for o in self.slot_last.values()]
        for e in ENGS:
            o = Op(e, None, False)
            o.idx = len(self.ops[e])
            for p in lasts + dmas:
                self._add_dep(o, p)
            self.ops[e].append(o)

    def wait_keys(self, eng, keys):
        o = Op(eng, None, False)
        o.idx = len(self.ops[eng])
        for k in keys:
            self._add_dep(o, self.lastw.get(k))
        self.ops[eng].append(o)

    def emit(self, nc, sems, dsems):
        for e in ENGS:
            cnt = 0
            dcnt = {}
            for o in self.ops[e]:
                if o.dma and o.slot == "cc":
                    k = ("cc", 0)
                    dcnt[k] = dcnt.get(k, 0) + 1
                    o.sem = dsems[k]
                    o.val = dcnt[k]
                elif o.dma:
                    k = (e, o.slot)
                    dcnt[k] = dcnt.get(k, 0) + 16
                    o.sem = dsems[k]
                    o.val = dcnt[k]
                elif o.needed:
                    cnt += 1
                    o.sem = sems[e]
                    o.val = cnt
        engobj = {"pe": "tensor", "act": "scalar", "dve": "vector", "pool": "gpsimd", "sp": "sync"}
        with nc.Block() as block:
            for e in ENGS:
                ops = self.ops[e]
                if not ops:
                    continue

                def body(eng, ops=ops):
                    for o in ops:
                        for p in o.deps:
                            eng.wait_ge(p.sem, p.val)
                        if o.fn is None:
                            continue
                        ins = o.fn(eng)
                        if o.dma and o.slot == "cc":
                            ins.then_inc(o.sem)
                        elif o.dma:
                            ins.then_inc(o.sem, 16)
                        elif o.needed:
                            ins.then_inc(o.sem, 1)

                getattr(block, engobj[e])(body)


class Cfg:
    def __init__(self, D=4096, F=11008, NOWN=1024):
        self.D = D
        self.F = F
        self.NOWN = NOWN
        self.HALO = 16
        self.NT = NOWN + self.HALO
        self.KVH = 256
        self.NKV = NOWN + self.KVH
        self.KT = D // 128
        self.FT = F // 128
        self.NH = D // 64
        self.NKVH = self.NH // 8
        self.QW = D
        self.KVW = self.NKVH * 64
        self.G = D // 16
        self.FC = 14 if self.FT >= 28 else 4
        self.NQB = self.NKV // 128 - 1
        self.NCH = self.NT // 8


def ntiles(n):
    if n == 1040:
        return [(0, 352), (352, 352), (704, 336)]
    if n == 1280:
        return [(i * 320, 320) for i in range(4)]
    k = (n + 511) // 512
    base = (n + k - 1) // k
    base = (base + 31) // 32 * 32
    out = []
    s = 0
    while s < n:
        m = min(base, n - s)
        out.append((s, m))
        s += m
    return out


class Prog:
    def __init__(self, cfg, phases):
        self.c = cfg
        self.phases = phases
        self.nc = bass.Bass("TRN2", target_bir_lowering=False)
        self.r = Rec()
        self.din = {}
        self.dout = {}

    def inp(self, name, shape, dt=F32):
        if name in self.din:
            return self.din[name]
        t = self.nc.dram_tensor(name, list(shape), dt, kind="ExternalInput").ap()
        self.din[name] = t
        return t

    def outp(self, name, shape, dt=F32):
        t = self.nc.dram_tensor(name, list(shape), dt, kind="ExternalOutput").ap()
        self.dout[name] = t
        return t

    def scratch(self, name, shape, dt=F32):
        return self.nc.dram_tensor(name, list(shape), dt).ap()

    def sb(self, name, shape, dt):
        return self.nc.alloc_sbuf_tensor("sb_" + name, list(shape), dt)

    def setup_common(self):
        c, nc, r = self.c, self.nc, self.r
        self.ps = [nc.alloc_psum_tensor(f"ps{i}", [128, 512], F32) for i in range(8)]
        self.hreg = self.sb("hreg", [128, c.KT * c.NKV], BF16)
        self.GCOLS = 14592
        self.FCOLS = 6400
        self.greg = self.sb("greg", [128, self.GCOLS], BF16)
        self.freg = self.sb("freg", [128, self.FCOLS], F32)
        self.goff = 0
        self.foff = 0
        self.ones_bf = self.sb("ones_bf", [128, 128], BF16)
        self.xbuf = [self.sb(f"xbuf{i}", [128, c.NKV], F32) for i in range(2)]
        self.sq = [self.sb(f"sq{i}", [128, c.NKV], BF16) for i in range(2)]
        self.rstd = self.sb("rstd", [128, c.NKV], F32)
        self.NW = 3
        self.wst = [self.sb(f"wst{i}", [128, c.KT, 256], BF16) for i in range(self.NW)]
        self.wi = 0
        self.gains = self.sb("gains", [128, 5 * c.KT], F32)
        self.flag = self.sb("flag", [128, 1], F32)
        g_in = self.inp("gains", [128, 5 * c.KT])
        f_in = self.inp("flag", [128, 1])
        r.op("pool", lambda e: e.memset(self.ones_bf[:], 1.0), writes=["ones"])
        r.dma("sp", self.gains[:], g_in[:, :], writes=["gains"])
        r.dma("sp", self.flag[:], f_in[:, :], writes=["flag"])
        self.xres = self.scratch("xres", [c.D, c.NT])

    def phase_reset(self):
        self.goff = 0
        self.foff = 0

    def gc(self, cols):
        a = self.greg[:, self.goff:self.goff + cols]
        self.goff += (cols + 15) // 16 * 16
        assert self.goff <= self.GCOLS, ("greg overflow", self.goff)
        return a

    def fc(self, cols):
        a = self.freg[:, self.foff:self.foff + cols]
        self.foff += (cols + 7) // 8 * 8
        assert self.foff <= self.FCOLS, ("freg overflow", self.foff)
        return a

    def defer(self, fn):
        if not hasattr(self, "pending"):
            self.pending = []
        self.pending.append(fn)

    def flush(self):
        for fn in getattr(self, "pending", []):
            fn()
        self.pending = []

    def nextw(self):
        b = self.wi % self.NW
        self.wi += 1
        return b

    def hview(self, ncols):
        c = self.c
        return self.hreg[:, 0:c.KT * ncols].rearrange("p (k n) -> p k n", k=c.KT)

    def rmsnorm(self, src, src_key, col0, ncols, gcol, hT, hkey, use_flag, s_major=False):
        c, r = self.c, self.r
        nts = ntiles(ncols)
        ps = self.ps
        for k in range(c.KT):
            xb = self.xbuf[k % 2]
            sq = self.sq[k % 2]
            r.dma("sp", xb[:, 0:ncols], src[k * 128:(k + 1) * 128, col0:col0 + ncols],
                  reads=[f"{src_key}{k}"], writes=[f"xbuf{k%2}"])
            r.op("act", lambda e, xb=xb, sq=sq: e.activation(out=sq[:, 0:ncols], in_=xb[:, 0:ncols], func=AF.Square),
                 reads=[f"xbuf{k%2}"], writes=[f"sq{k%2}"])
            for i, (s, n) in enumerate(nts):
                r.op("pe", lambda e, i=i, s=s, n=n, sq=sq, k=k: e.matmul(
                    ps[i][:, 0:n], lhsT=self.ones_bf[:, :], rhs=sq[:, s:s + n],
                    start=(k == 0), stop=(k == c.KT - 1)),
                    reads=[f"sq{k%2}", "ones"], writes=[f"ps{i}"])
        for i, (s, n) in enumerate(nts):
            r.op("dve", lambda e, i=i, s=s, n=n: e.tensor_scalar(
                out=self.rstd[:, s:s + n], in0=ps[i][:, 0:n], scalar1=1.0 / c.D, scalar2=EPS,
                op0=ALU.mult, op1=ALU.add), reads=[f"ps{i}"], writes=["rstd"])
        r.op("act", lambda e: e.sqrt(out=self.rstd[:, 0:ncols], in_=self.rstd[:, 0:ncols]),
             reads=["rstd"], writes=["rstd"])
        r.op("dve", lambda e: e.reciprocal(out=self.rstd[:, 0:ncols], in_=self.rstd[:, 0:ncols]),
             reads=["rstd"], writes=["rstd"])
        if use_flag:
            r.op("dve", lambda e: e.tensor_scalar(
                out=self.rstd[:, 0:c.HALO], in0=self.rstd[:, 0:c.HALO], scalar1=self.flag[:, 0:1],
                scalar2=None, op0=ALU.mult), reads=["rstd", "flag"], writes=["rstd"])
        for k in range(c.KT):
            xb = self.xbuf[k % 2]
            r.dma("sp", xb[:, 0:ncols], src[k * 128:(k + 1) * 128, col0:col0 + ncols],
                  reads=[f"{src_key}{k}"], writes=[f"xbuf{k%2}"])
            if s_major:
                hs8 = self.hs8()
                r.op("dve", lambda e, xb=xb, k=k, hs8=hs8: e.scalar_tensor_tensor(
                    out=hs8[:, k, :, 0:c.NCH].rearrange("p s j -> p j s"),
                    in0=xb[:, 0:ncols].rearrange("p (j s) -> p j s", s=8),
                    scalar=self.gains[:, gcol + k:gcol + k + 1],
                    in1=self.rstd[:, 0:ncols].rearrange("p (j s) -> p j s", s=8), op0=ALU.mult, op1=ALU.mult),
                    reads=[f"xbuf{k%2}", "rstd", "gains"], writes=[hkey])
                continue
            r.op("dve", lambda e, xb=xb, k=k: e.scalar_tensor_tensor(
                out=hT[:, k, :], in0=xb[:, 0:ncols], scalar=self.gains[:, gcol + k:gcol + k + 1],
                in1=self.rstd[:, 0:ncols], op0=ALU.mult, op1=ALU.mult),
                reads=[f"xbuf{k%2}", "rstd", "gains"], writes=[hkey])

    def load_w(self, buf_i, w, col0, ncols=128, dstcol=0, eng="pool"):
        wb = self.wst[buf_i]
        self.r.dma(eng, wb[:, :, dstcol:dstcol + ncols],
                   w[:, col0:col0 + ncols].rearrange("(k p) m -> p k m", p=128),
                   writes=[f"wst{buf_i}"])
        self.flush()
        return wb

    def linear_tile(self, wb, wkey, hT, hkey, nts, bank0, col_off=0, wc0=0):
        c, r = self.c, self.r
        for i, (s, n) in enumerate(nts):
            for k in range(c.KT):
                r.op("pe", lambda e, i=i, s=s, n=n, k=k: e.matmul(
                    self.ps[bank0 + i][:, 0:n], lhsT=wb[:, k, wc0:wc0 + 128], rhs=hT[:, k, col_off + s:col_off + s + n],
                    start=(k == 0), stop=(k == c.KT - 1)),
                    reads=[wkey, hkey], writes=[f"ps{bank0+i}"])

    def attention(self):
        c, nc, r = self.c, self.nc, self.r
        ps = self.ps
        xT0 = self.inp("xT0", [c.D, c.NKV])
        w_qkv = self.inp("w_qkv", [c.D, c.QW + 2 * c.KVW])
        w_o = self.inp("w_o", [c.QW, c.D])
        bias_tab = self.inp("bias_tab", [c.NH // 2, 128, 512])
        sinks2_in = self.inp("sinks2", [128, c.NH // 2])
        negmask_in = self.inp("negmask", [128, 128])
        oT_d = self.scratch("oT_d", [c.D, c.NQB * 128], BF16)
        nblk = c.NKV // 128

        self.phase_reset()
        kdup = [self.gc(c.NKV) for i in range(2)]
        vtok = [self.gc(nblk * 64).rearrange("p (a n) -> p a n", a=nblk) for i in range(2)]
        qT = [self.gc(c.NKV) for i in range(2)]
        pbf = [self.gc(512) for i in range(2)]
        otile = [self.gc(c.NQB * 128) for i in range(2)]
        biasT = [self.fc(512) for i in range(2)]
        esb = [self.fc(512) for i in range(2)]
        esink = self.fc(c.NH // 2)
        negmask = self.fc(128)
        rd = self.fc(128)

        r.dma("sp", esink, sinks2_in[:, :], writes=["esink"])
        r.op("act", lambda e: e.activation(out=esink, in_=esink, func=AF.Exp), reads=["esink"], writes=["esink"])
        r.dma("sp", negmask, negmask_in[:, :], writes=["negmask"])

        hT = self.hview(c.NKV)
        self.rmsnorm(xT0, "xT0", 0, c.NKV, 0 * c.KT, hT, "hT", use_flag=False)
        nts = ntiles(c.NKV)
        it_ctr = 0
        for kvh in range(c.NKVH):
            kd = kdup[kvh % 2]
            kk = f"kdup{kvh%2}"
            vt = vtok[kvh % 2]
            vk = f"vtok{kvh%2}"
            b = self.nextw()
            wb = self.wst[b]
            self.load_w(b, w_qkv, c.QW + kvh * 64, 64, 0)
            self.load_w(b, w_qkv, c.QW + kvh * 64, 64, 64)
            self.load_w(b, w_qkv, c.QW + c.KVW + kvh * 64, 64, 128)
            self.linear_tile(wb, f"wst{b}", hT, "hT", nts, 0, wc0=0)
            for i, (s, n) in enumerate(nts):
                r.op("act", lambda e, i=i, s=s, n=n, kd=kd: e.copy(out=kd[:, s:s + n], in_=ps[i][:, 0:n]),
                     reads=[f"ps{i}"], writes=[kk])
            for blk in range(nblk):
                bank = 4 if blk % 2 == 0 else 5
                for k in range(c.KT):
                    r.op("pe", lambda e, blk=blk, k=k, bank=bank, wb=wb: e.matmul(
                        ps[bank][:, 0:64], lhsT=hT[:, k, blk * 128:(blk + 1) * 128], rhs=wb[:, k, 128:192],
                        start=(k == 0), stop=(k == c.KT - 1)), reads=["hT", f"wst{b}"], writes=[f"ps{bank}"])
                r.op("act", lambda e, blk=blk, bank=bank, vt=vt: e.copy(out=vt[:, blk, :], in_=ps[bank][:, 0:64]),
                     reads=[f"ps{bank}"], writes=[vk])
            for jp in range(2):
                bq = self.nextw()
                wq = self.wst[bq]
                j0 = kvh * 4 + jp * 2
                self.load_w(bq, w_qkv, j0 * 128, 256, 0)
                for jj in range(2):
                    j = j0 + jj
                    q = qT[j % 2]
                    qk = f"qT{j%2}"
                    self.linear_tile(wq, f"wst{bq}", hT, "hT", nts, 0, wc0=jj * 128)
                    for i, (s, n) in enumerate(nts):
                        r.op("act", lambda e, i=i, s=s, n=n, q=q: e.copy(out=q[:, s:s + n], in_=ps[i][:, 0:n]),
                             reads=[f"ps{i}"], writes=[qk])
                    bt = biasT[j % 2]
                    r.dma("sp", bt, bias_tab[j, :, :], writes=[f"biasT{j%2}"])
                    ot = otile[j % 2]
                    otk = f"otile{j%2}"
                    for qb in range(1, c.NQB + 1):
                        it = it_ctr % 2
                        it_ctr += 1
                        scs = [ps[6], ps[7]]
                        od = ps[4 + it]
                        odk = f"ps{4+it}"
                        e_sb = esb[it]
                        p_sb = pbf[it]
                        for hh in range(2):
                            for kb in range(2):
                                kblk = qb - 1 + kb
                                r.op("pe", lambda e, hh=hh, kb=kb, kblk=kblk, q=q, qb=qb, kd=kd, scs=scs: e.matmul(
                                    scs[hh][:, kb * 128:(kb + 1) * 128],
                                    lhsT=kd[hh * 64:(hh + 1) * 64, kblk * 128:(kblk + 1) * 128],
                                    rhs=q[hh * 64:(hh + 1) * 64, qb * 128:(qb + 1) * 128],
                                    start=True, stop=True), reads=[kk, qk], writes=[f"ps{6+hh}"])
                        for hh in range(2):
                            r.op("dve", lambda e, e_sb=e_sb, bt=bt, hh=hh, scs=scs: e.scalar_tensor_tensor(
                                out=e_sb[:, hh * 256:(hh + 1) * 256], in0=scs[hh][:, 0:256], scalar=0.125,
                                in1=bt[:, hh * 256:(hh + 1) * 256], op0=ALU.mult, op1=ALU.add),
                                reads=[f"ps{6+hh}", f"biasT{j%2}"], writes=[f"esb{it}"])
                        if qb == 2:
                            for hh in range(2):
                                r.op("dve", lambda e, e_sb=e_sb, hh=hh: e.tensor_tensor(
                                    out=e_sb[:, hh * 256:hh * 256 + 128], in0=e_sb[:, hh * 256:hh * 256 + 128],
                                    in1=negmask, op=ALU.add), reads=[f"esb{it}", "negmask"], writes=[f"esb{it}"])
                        r.op("act", lambda e, e_sb=e_sb, p_sb=p_sb: e.activation(out=p_sb, in_=e_sb, func=AF.Exp),
                             reads=[f"esb{it}"], writes=[f"pbf{it}"])
                        for hh in range(2):
                            for kb in range(2):
                                r.op("pe", lambda e, hh=hh, kb=kb, p_sb=p_sb, od=od: e.matmul(
                                    od[hh * 64:(hh + 1) * 64, 0:128], lhsT=self.ones_bf[:, 0:64],
                                    rhs=p_sb[:, (hh * 2 + kb) * 128:(hh * 2 + kb + 1) * 128],
                                    start=(kb == 0), stop=(kb == 1)), reads=[f"pbf{it}", "ones"], writes=[odk])
                        for hh in range(2):
                            for kb in range(2):
                                kblk = qb - 1 + kb
                                r.op("pe", lambda e, hh=hh, kb=kb, kblk=kblk, p_sb=p_sb, od=od, vt=vt: e.matmul(
                                    od[hh * 64:(hh + 1) * 64, 128:256], lhsT=vt[:, kblk, :],
                                    rhs=p_sb[:, (hh * 2 + kb) * 128:(hh * 2 + kb + 1) * 128],
                                    start=(kb == 0), stop=(kb == 1)), reads=[f"pbf{it}", vk], writes=[odk])
                        r.op("dve", lambda e, j=j, od=od: e.tensor_scalar(
                            out=rd, in0=od[:, 0:128], scalar1=esink[:, j:j + 1], scalar2=None, op0=ALU.add),
                            reads=[odk, "esink"], writes=["rd"])
                        r.op("dve", lambda e: e.reciprocal(out=rd, in_=rd), reads=["rd"], writes=["rd"])
                        r.op("dve", lambda e, ot=ot, qb=qb, od=od: e.tensor_tensor(
                            out=ot[:, (qb - 1) * 128:qb * 128], in0=od[:, 128:256], in1=rd, op=ALU.mult),
                            reads=[odk, "rd"], writes=[otk])
                    r.dma("sp", oT_d[j * 128:(j + 1) * 128, :], ot, reads=[otk], writes=["oT_d"])

        r.barrier()
        oT = self.hview(c.NT)
        ocol0 = c.NQB * 128 - c.NT
        for k in range(c.KT):
            r.dma("sp", oT[:, k, :], oT_d[k * 128:(k + 1) * 128, ocol0:ocol0 + c.NT], reads=["oT_d"], writes=["oT"])
        nt3 = ntiles(c.NT)
        xcol0 = c.NKV - c.NT
        for mp in range(c.KT // 2):
            b = self.nextw()
            self.load_w(b, w_o, mp * 256, 256, 0)
            for half in range(2):
                m = mp * 2 + half
                bank0 = (m % 2) * 3
                self.linear_tile(self.wst[b], f"wst{b}", oT, "oT", nt3, bank0, wc0=half * 128)
                xb = self.xbuf[m % 2]
                r.dma("sp", xb[:, 0:c.NT], xT0[m * 128:(m + 1) * 128, xcol0:xcol0 + c.NT], reads=["xT0"],
                      writes=[f"xbuf{m%2}"])
                for i, (s, n) in enumerate(nt3):
                    r.op("dve", lambda e, i=i, s=s, n=n, xb=xb, bank0=bank0: e.tensor_tensor(
                        out=xb[:, s:s + n], in0=ps[bank0 + i][:, 0:n], in1=xb[:, s:s + n], op=ALU.add),
                        reads=[f"ps{bank0+i}", f"xbuf{m%2}"], writes=[f"xbuf{m%2}"])
                r.dma("sp", self.xres[m * 128:(m + 1) * 128, :], xb[:, 0:c.NT], reads=[f"xbuf{m%2}"], writes=[f"xres{m}"])
        r.barrier()

    def ffn(self, li):
        c, nc, r = self.c, self.nc, self.r
        ps = self.ps
        w_up = self.inp(f"w_up{li}", [c.D, 2 * c.F])
        w_down = self.inp(f"w_down{li}", [c.F, c.D])
        convp_in = self.inp(f"convp{li}", [128, 4 * 2 * c.FT])
        self.phase_reset()
        if not hasattr(self, "convp_sb"):
            self.convp_sb = self.sb("convp", [128, 4 * 2 * c.FT], F32)
        convp = self.convp_sb
        act = [self.gc(c.FC * c.NT).rearrange("p (a n) -> p a n", a=c.FC)] * 2
        tail0 = c.KT * c.NT
        if c.KT * c.NKV - tail0 >= 2 * c.FC * 256:
            wdn = [self.hreg[:, tail0 + i * c.FC * 256:tail0 + (i + 1) * c.FC * 256].rearrange("p (a n) -> p a n", a=c.FC)
                   for i in range(2)]
        else:
            wdn = [self.gc(c.FC * 256).rearrange("p (a n) -> p a n", a=c.FC) for i in range(2)]
        u = [self.fc(c.NT) for i in range(2)]
        cc = [self.fc(c.NT) for i in range(2)]
        sg = self.fc(c.NT)
        yst = [self.fc(c.NT)] * 2
        r.dma("sp", convp[:], convp_in[:, :], writes=["convp"])
        cp = convp[:, :].rearrange("p (j f) -> p j f", j=4)

        hT = self.hview(c.NT)
        self.rmsnorm(self.xres, "xres", 0, c.NT, (1 if li == 0 else 3) * c.KT, hT, "hT", use_flag=True)
        nt3 = ntiles(c.NT)
        NT = c.NT
        chunks = [list(range(s, min(s + c.FC, c.FT))) for s in range(0, c.FT, c.FC)]
        evi = 0
        sti = 0
        stg = [(u[0], "u0"), (u[1], "u1"), (cc[0], "cc0"), (cc[1], "cc1"), (sg, "sg"), (yst[0], "yst")]
        for ci, chunk in enumerate(chunks):
            actb = act[0]
            ak = "actb"
            wtiles = {}
            for li_, f in enumerate(chunk):
                for which in range(2):
                    if li_ % 2 == 0:
                        b = self.nextw()
                        ncol = 256 if li_ + 1 < len(chunk) else 128
                        self.load_w(b, w_up, which * c.F + f * 128, ncol, 0)
                        wtiles[which] = b
                    b = wtiles[which]
                    bank0 = which * 3
                    self.linear_tile(self.wst[b], f"wst{b}", hT, "hT", nt3, bank0, wc0=(li_ % 2) * 128)
                    uu = u[which]
                    cw = cc[which]
                    col = which * c.FT + f
                    for i, (s, n) in enumerate(nt3):
                        r.op("act", lambda e, i=i, s=s, n=n, uu=uu, bank0=bank0: e.copy(
                            out=uu[:, s:s + n], in_=ps[bank0 + i][:, 0:n]),
                            reads=[f"ps{bank0+i}"], writes=[f"u{which}"])
                    r.op("act", lambda e, uu=uu, cw=cw, col=col: e.activation(
                        out=cw[:, :], in_=uu[:, :], func=AF.Identity, scale=cp[:, 2, col:col + 1],
                        bias=cp[:, 3, col:col + 1]), reads=[f"u{which}", "convp"], writes=[f"cc{which}"])
                    r.op("dve", lambda e, uu=uu, cw=cw, col=col: e.scalar_tensor_tensor(
                        out=cw[:, 1:NT], in0=uu[:, 0:NT - 1], scalar=cp[:, 1, col:col + 1], in1=cw[:, 1:NT],
                        op0=ALU.mult, op1=ALU.add), reads=[f"u{which}", f"cc{which}", "convp"], writes=[f"cc{which}"])
                    r.op("dve", lambda e, uu=uu, cw=cw, col=col: e.scalar_tensor_tensor(
                        out=cw[:, 2:NT], in0=uu[:, 0:NT - 2], scalar=cp[:, 0, col:col + 1], in1=cw[:, 2:NT],
                        op0=ALU.mult, op1=ALU.add), reads=[f"u{which}", f"cc{which}", "convp"], writes=[f"cc{which}"])
                r.op("act", lambda e: e.activation(out=sg[:, :], in_=cc[0][:, :], func=AF.Silu),
                     reads=["cc0"], writes=["sg"])
                r.op("dve", lambda e, actb=actb, li_=li_: e.tensor_tensor(
                    out=actb[:, li_, :], in0=sg[:, :], in1=cc[1][:, :], op=ALU.mult),
                    reads=["sg", "cc1"], writes=[ak])
            nf = len(chunk)
            f0 = chunk[0]
            for mp in range(c.KT // 2):
                wd = wdn[mp % 2]
                wk = f"wdn{mp%2}"
                r.dma("pool", wd[:, 0:nf, :],
                      w_down[f0 * 128:(f0 + nf) * 128, mp * 256:(mp + 1) * 256].rearrange("(k p) m -> p k m", p=128),
                      writes=[wk])
                self.flush()
                for half in range(2):
                    m = mp * 2 + half
                    bank0 = (m % 2) * 3
                    for i, (s, n) in enumerate(nt3):
                        for q in range(nf):
                            r.op("pe", lambda e, i=i, s=s, n=n, q=q, wd=wd, half=half, bank0=bank0, actb=actb, nf=nf: e.matmul(
                                ps[bank0 + i][:, 0:n], lhsT=wd[:, q, half * 128:(half + 1) * 128],
                                rhs=actb[:, q, s:s + n], start=(q == 0), stop=(q == nf - 1)),
                                reads=[wk, ak], writes=[f"ps{bank0+i}"])
                    st, stk = stg[sti % len(stg)]
                    sti += 1
                    for i, (s, n) in enumerate(nt3):
                        eng = "act" if (evi % 2 == 0) else "dve"
                        evi += 1
                        if eng == "act":
                            r.op("act", lambda e, i=i, s=s, n=n, st=st, bank0=bank0: e.copy(
                                out=st[:, s:s + n], in_=ps[bank0 + i][:, 0:n]),
                                reads=[f"ps{bank0+i}"], writes=[stk])
                        else:
                            r.op("dve", lambda e, i=i, s=s, n=n, st=st, bank0=bank0: e.tensor_copy(
                                out=st[:, s:s + n], in_=ps[bank0 + i][:, 0:n]),
                                reads=[f"ps{bank0+i}"], writes=[stk])
                    self.defer(lambda m=m, st=st, stk=stk: r.dma(
                        "pool", self.xres[m * 128:(m + 1) * 128, :], st[:, :], reads=[stk],
                        writes=[f"xres{m}"], accum_op=ALU.add))
        self.flush()
        r.barrier()

    JP = 144

    def hs8(self):
        c = self.c
        return self.hreg[:, 0:c.KT * 8 * self.JP].rearrange("p (k s j) -> p k s j", k=c.KT, s=8)

    def vb(self, ap, key):
        return V(ap, key)

    def fb(self, n, key, rows=None):
        ap = self.fc(n)
        if rows is not None:
            ap = ap[0:rows, :]
        return V(ap, key)

    def tt(self, o, a, b, op, eng="dve"):
        self.r.op(eng, lambda e: e.tensor_tensor(out=o.ap, in0=a.ap, in1=b.ap, op=op),
                  reads=[a.key, b.key], writes=[o.key])

    def ts(self, o, a, s1, op0, s2=None, op1=None, eng="dve"):
        rd = [a.key]
        v1 = s1
        if isinstance(s1, V):
            rd.append(s1.key)
            v1 = s1.ap
        if op1 is None:
            self.r.op(eng, lambda e: e.tensor_scalar(out=o.ap, in0=a.ap, scalar1=v1, scalar2=None, op0=op0),
                      reads=rd, writes=[o.key])
        else:
            self.r.op(eng, lambda e: e.tensor_scalar(out=o.ap, in0=a.ap, scalar1=v1, scalar2=s2, op0=op0, op1=op1),
                      reads=rd, writes=[o.key])

    def stt(self, o, a, sc, b, op0, op1, eng="dve"):
        rd = [a.key, b.key]
        v = sc
        if isinstance(sc, V):
            rd.append(sc.key)
            v = sc.ap
        self.r.op(eng, lambda e: e.scalar_tensor_tensor(out=o.ap, in0=a.ap, scalar=v, in1=b.ap, op0=op0, op1=op1),
                  reads=rd, writes=[o.key])

    def actv(self, o, a, func):
        self.r.op("act", lambda e: e.activation(out=o.ap, in_=a.ap, func=func), reads=[a.key], writes=[o.key])

    def cpy(self, o, a, eng="dve"):
        if eng == "act":
            self.r.op("act", lambda e: e.copy(out=o.ap, in_=a.ap), reads=[a.key], writes=[o.key])
        else:
            self.r.op(eng, lambda e: e.tensor_copy(out=o.ap, in_=a.ap), reads=[a.key], writes=[o.key])

    def recip(self, o, a):
        self.r.op("dve", lambda e: e.reciprocal(out=o.ap, in_=a.ap), reads=[a.key], writes=[o.key])

    def mset(self, o, val, eng="dve"):
        self.r.op(eng, lambda e: e.memset(o.ap, val), writes=[o.key])

    def sincos(self, arg, sn, cs, tmp, tmpi, phase_col=None):
        OFF = 64.0 * math.pi
        TWO_PI = 2.0 * math.pi
        if phase_col is None:
            self.ts(arg, arg, OFF, ALU.add)
        else:
            self.ts(arg, arg, phase_col, ALU.add)
        self.ts(tmp, arg, 1.0 / TWO_PI, ALU.mult)
        self.cpy(tmpi, tmp)
        self.cpy(tmp, tmpi)
        self.stt(arg, tmp, -TWO_PI, arg, ALU.mult, ALU.add)

        def wrap(hi):
            if hi:
                self.ts(tmp, arg, math.pi, ALU.is_gt, -TWO_PI, ALU.mult)
            else:
                self.ts(tmp, arg, -math.pi, ALU.is_lt, TWO_PI, ALU.mult)
            self.tt(arg, arg, tmp, ALU.add)
        wrap(True)
        wrap(False)
        self.actv(sn, arg, AF.Sin)
        if cs is not None:
            self.ts(arg, arg, 0.5 * math.pi, ALU.add)
            wrap(True)
            self.actv(cs, arg, AF.Sin)

    def ssm_setup(self):
        c, r = self.c, self.r
        if hasattr(self, "ssm_in"):
            return
        G, KT = c.G, c.KT
        Q = G // 2
        d = {}
        d["ssmc"] = self.inp("ssmc", [128, 32])
        d["ident"] = self.inp("ident", [128, 128])
        for nm in ("lamr1", "lami1", "bre1", "bim1"):
            d[nm] = self.inp(nm, [128, KT * 64])
        d["lstep1"] = self.inp("lstep1", [128, KT])
        for nm in ("lamr2", "lami2", "lstep2"):
            d[nm] = self.inp(nm, [128, G])
        for nm in ("CA2", "CB2", "BA2", "BB2"):
            d[nm] = self.inp(nm, [128, G * 16])
        d["lamrS"] = self.inp("lamrS", [Q, 128])
        d["lamiS"] = self.inp("lamiS", [Q, 128])
        d["lstepS"] = self.inp("lstepS", [Q, 2])
        d["dskip"] = self.inp("dskip", [128, KT])
        self.ssm_in = d
        self.E_d = self.scratch("E_d", [c.NCH, G * 128])
        self.X_d = self.scratch("X_d", [c.NCH, G * 128], BF16)
        self.ssmc = self.sb("ssmc", [128, 32], F32)
        self.ident = self.sb("ident", [128, 128], F32)
        self.identb = self.sb("identb", [128, 128], BF16)
        self.dskip = self.sb("dskip", [128, KT], F32)
        r.dma("sp", self.ssmc[:], d["ssmc"][:, :], writes=["ssmc"])
        r.dma("sp", self.ident[:], d["ident"][:, :], writes=["ident"])
        r.dma("sp", self.dskip[:], d["dskip"][:, :], writes=["dskip"])
        r.op("dve", lambda e: e.tensor_copy(out=self.identb[:], in_=self.ident[:]), reads=["ident"], writes=["identb"])

    def cabar_f(self, lamr, lami, lrd, lid, n, pre, rows=None):
        B = lambda k: self.fb(n, pre + k, rows)
        arg, tmp, sn, cs, mg, den, fr, fi = B("arg"), B("tmp"), B("sn"), B("cs"), B("mg"), B("den"), B("fr"), B("fi")
        tmpi = V(self.fc(n).bitcast(mybir.dt.int32) if rows is None else self.fc(n)[0:rows, :].bitcast(mybir.dt.int32),
                 pre + "tmpi")
        self.actv(mg, lrd, AF.Exp)
        self.cpy(arg, lid)
        self.sincos(arg, sn, cs, tmp, tmpi)
        self.tt(cs, cs, mg, ALU.mult)
        self.tt(sn, sn, mg, ALU.mult)
        self.ts(cs, cs, -1.0, ALU.add)
        self.tt(den, lamr, lamr, ALU.mult)
        self.tt(tmp, lami, lami, ALU.mult)
        self.tt(den, den, tmp, ALU.add)
        self.recip(den, den)
        self.tt(fr, cs, lamr, ALU.mult)
        self.tt(tmp, sn, lami, ALU.mult)
        self.tt(fr, fr, tmp, ALU.add)
        self.tt(fr, fr, den, ALU.mult)
        self.tt(fi, sn, lamr, ALU.mult)
        self.tt(tmp, cs, lami, ALU.mult)
        self.tt(fi, fi, tmp, ALU.subtract)
        self.tt(fi, fi, den, ALU.mult)
        return fr, fi

    def ssm_E(self):
        c, r = self.c, self.r
        ps = self.ps
        self.ssm_setup()
        d = self.ssm_in
        KT, JP, NCH = c.KT, self.JP, c.NCH
        self.phase_reset()
        hs8 = self.hs8()
        r.op("pool", lambda e: e.memset(self.hreg[:, 0:KT * 8 * JP], 0.0), writes=["hT"])
        self.rmsnorm(self.xres, "xres", 0, c.NT, 2 * KT, None, "hT", use_flag=True, s_major=True)

        BD = self.gc(8 * 1024).rearrange("p (s g x) -> p s g x", s=8, g=8)
        ssmc = V(self.ssmc[:, :], "ssmc")
        pw = ssmc.v(lambda a: a[:, 8:16].unsqueeze(2).broadcast_to([128, 8, 64]))
        mask4 = ssmc.v(lambda a: a[:, 0:8].unsqueeze(1).unsqueeze(3).broadcast_to([128, 8, 8, 64]))
        dl1 = self.fb(KT, "dl1")
        r.dma("sp", dl1.ap, d["lstep1"][:, :], writes=["dl1"])
        self.actv(dl1, dl1, AF.Exp)
        B64 = lambda k: self.fb(64, "E" + k)
        lamr, lami, bre, bim, lrd, lid = B64("lamr"), B64("lami"), B64("bre"), B64("bim"), B64("lrd"), B64("lid")
        bbr, bbi, t64 = B64("bbr"), B64("bbi"), B64("t64")
        B512 = lambda k: self.fb(512, "E" + k)
        ARG, MAG, SN, CS, W1, W2, TMP = B512("ARG"), B512("MAG"), B512("SN"), B512("CS"), B512("W1"), B512("W2"), B512("TMP")
        TMPI = V(self.fc(512).bitcast(mybir.dt.int32), "ETMPI")
        v3 = lambda a: a.rearrange("p (s x) -> p s x", s=8)
        b3 = lambda a: a.unsqueeze(1).broadcast_to([128, 8, 64])
        foff0 = self.foff
        rows_of = [(0, 64), (64, NCH - 64)]
        for dt in range(KT):
            self.foff = foff0
            for nm, tl in (("lamr1", lamr), ("lami1", lami), ("bre1", bre), ("bim1", bim)):
                r.dma("sp", tl.ap, d[nm][:, dt * 64:(dt + 1) * 64], writes=[tl.key])
            dcol = dl1.v(lambda a, dt=dt: a[:, dt:dt + 1])
            self.ts(lrd, lamr, dcol, ALU.mult)
            self.ts(lid, lami, dcol, ALU.mult)
            self.tt(MAG.v(v3), lrd.v(b3), pw, ALU.mult)
            self.actv(MAG, MAG, AF.Exp)
            fr, fi = self.cabar_f(lamr, lami, lrd, lid, 64, "Ef")
            self.tt(bbr, fr, bre, ALU.mult)
            self.tt(t64, fi, bim, ALU.mult)
            self.tt(bbr, bbr, t64, ALU.subtract)
            self.tt(bbi, fr, bim, ALU.mult)
            self.tt(t64, fi, bre, ALU.mult)
            self.tt(bbi, bbi, t64, ALU.add)
            self.tt(ARG.v(v3), lid.v(b3), pw, ALU.mult)
            self.sincos(ARG, SN, CS, TMP, TMPI)
            self.tt(CS, CS, MAG, ALU.mult)
            self.tt(SN, SN, MAG, ALU.mult)
            self.tt(W1.v(v3), CS.v(v3), bbr.v(b3), ALU.mult)
            self.tt(TMP.v(v3), SN.v(v3), bbi.v(b3), ALU.mult)
            self.tt(W1, W1, TMP, ALU.subtract)
            self.tt(W2.v(v3), CS.v(v3), bbi.v(b3), ALU.mult)
            self.tt(TMP.v(v3), SN.v(v3), bbr.v(b3), ALU.mult)
            self.tt(W2, W2, TMP, ALU.add)
            for comp, W in ((0, W1), (1, W2)):
                self.tt(V(BD[:, :, :, comp * 64:(comp + 1) * 64], "BD"),
                        W.v(lambda a: v3(a).unsqueeze(2).broadcast_to([128, 8, 8, 64])), mask4, ALU.mult, eng="pool")
            for jb, (j0, nr) in enumerate(rows_of):
                est = self.xbuf[jb]
                for s_ in range(8):
                    for half in range(2):
                        bank = jb * 2 + half
                        r.op("pe", lambda e, s_=s_, j0=j0, nr=nr, half=half, bank=bank, dt=dt: e.matmul(
                            ps[bank][0:nr, 0:512], lhsT=hs8[:, dt, s_, j0:j0 + nr],
                            rhs=BD[:, s_, half * 4:(half + 1) * 4, :].rearrange("p g x -> p (g x)"),
                            start=(s_ == 0), stop=(s_ == 7)), reads=["hT", "BD"], writes=[f"ps{bank}"])
                for half in range(2):
                    bank = jb * 2 + half
                    self.cpy(V(est[0:nr, half * 512:(half + 1) * 512], f"xbuf{jb}"),
                             V(ps[bank][0:nr, 0:512], f"ps{bank}"), eng=("act" if half == 0 else "dve"))
                r.dma("sp", self.E_d[j0:j0 + nr, dt * 1024:(dt + 1) * 1024], est[0:nr, 0:1024],
                      reads=[f"xbuf{jb}"], writes=["E_d"])
        r.barrier()

    def ssm_scan(self, passB, fused=False):
        c, r = self.c, self.r
        self.ssm_setup()
        d = self.ssm_in
        G, NCH = c.G, c.NCH
        Q = G // 2
        self.phase_reset()
        B = lambda n, k: self.fb(n, "S" + k, Q)
        lamr, lami, dl, lrd, lid = B(128, "lamr"), B(128, "lami"), B(2, "dl"), B(128, "lrd"), B(128, "lid")
        ARG, MAG, SN, CS, TMP = B(128, "ARG"), B(128, "MAG"), B(128, "SN"), B(128, "CS"), B(128, "TMP")
        TMPI = V(self.fc(128)[0:Q, :].bitcast(mybir.dt.int32), "STMPI")
        X, t1, t2 = B(256, "X"), B(256, "t1"), B(256, "t2")
        r.dma("sp", lamr.ap, d["lamrS"][:, :], writes=[lamr.key])
        r.dma("sp", lami.ap, d["lamiS"][:, :], writes=[lami.key])
        r.dma("sp", dl.ap, d["lstepS"][:, :], writes=[dl.key])
        self.actv(dl, dl, AF.Exp)
        for gi in range(2):
            sl = lambda a, gi=gi: a[:, gi * 64:(gi + 1) * 64]
            dcol = dl.v(lambda a, gi=gi: a[:, gi:gi + 1])
            self.ts(lrd.v(sl), lamr.v(sl), dcol, ALU.mult)
            self.ts(lid.v(sl), lami.v(sl), dcol, ALU.mult)

        def power(n, tag):
            oR, oI, oIn = B(128, tag + "R"), B(128, tag + "I"), B(128, tag + "In")
            self.ts(MAG, lrd, float(n), ALU.mult)
            self.actv(MAG, MAG, AF.Exp)
            self.ts(ARG, lid, float(n), ALU.mult)
            self.sincos(ARG, SN, CS, TMP, TMPI)
            self.tt(oR, CS, MAG, ALU.mult)
            self.tt(oI, SN, MAG, ALU.mult)
            self.ts(oIn, oI, -1.0, ALU.mult)
            return oR, oI, oIn

        x4 = lambda a: a.rearrange("q (g k p) -> q g k p", g=2, k=2)
        a4 = lambda a: a.rearrange("q (g p) -> q g p", g=2)

        def cstep(Xs, Ein, A, outX):
            aR, aI, aIn = A
            self.tt(t1.v(x4), Xs.v(x4), aR.v(lambda a: a4(a).unsqueeze(2).broadcast_to([Q, 2, 2, 64])), ALU.mult)
            self.tt(t2.v(lambda a: x4(a)[:, :, 0, :]), Xs.v(lambda a: x4(a)[:, :, 1, :]), aIn.v(a4), ALU.mult)
            self.tt(t2.v(lambda a: x4(a)[:, :, 1, :]), Xs.v(lambda a: x4(a)[:, :, 0, :]), aI.v(a4), ALU.mult)
            self.tt(t1, t1, t2, ALU.add)
            self.tt(outX, t1, Ein, ALU.add)

        A8 = power(8, "A8")
        self.mset(X, 0.0)
        if passB:
            if fused:
                nsrc = 4
                Lsrc = [self.Lall_d[rr * Q:(rr + 1) * Q, :] for rr in range(nsrc)]
                lkey = "Lall_d"
            else:
                nsrc = 8
                Lall_in = self.inp("Lall", [8, Q, 256])
                Lsrc = [Lall_in[rr, :, :] for rr in range(nsrc)]
                lkey = None
            wsel_in = self.inp("wsel", [Q, 8])
            A1k = power(1024, "A1k")
            wsel, Lt, Xn = B(8, "wsel"), B(256, "Lt"), B(256, "Xn")
            r.dma("sp", wsel.ap, wsel_in[:, :], writes=[wsel.key])
            for rr in range(nsrc):
                r.dma("sp", Lt.ap, Lsrc[rr], reads=([lkey] if lkey else []), writes=[Lt.key])
                cstep(X, Lt, A1k, Xn)
                self.tt(t2, Xn, X, ALU.subtract)
                self.stt(X, t2, wsel.v(lambda a, rr=rr: a[:, rr:rr + 1]), X, ALU.mult, ALU.add)
        JB = 5
        last = NCH - 1 if passB else NCH - 2
        xst = [V(self.gc(JB * 256)[0:Q, :], f"xst{i}") for i in range(2)]
        if passB:
            z = xst[0]
            self.mset(z.v(lambda a: a[:, 0:256]), 0.0, eng="pool")
            r.dma("sp", self.X_d[0:1, :].rearrange("j (q f) -> q j f", f=256),
                  z.ap[:, 0:256].rearrange("q (j f) -> q j f", j=1), reads=[z.key], writes=["X_d"])
        bi = 0
        for j0 in range(1, last + 1, JB):
            nj = min(JB, last + 1 - j0)
            eb = V(self.xbuf[bi % 2][0:Q, 0:JB * 256], f"xbuf{bi%2}")
            xs = xst[bi % 2]
            bi += 1
            r.dma("sp", eb.ap[:, 0:nj * 256].rearrange("q (j f) -> q j f", j=nj),
                  self.E_d[j0:j0 + nj, :].rearrange("j (q f) -> q j f", f=256), reads=["E_d"], writes=[eb.key])
            for jj in range(nj):
                sl = lambda a, jj=jj: a[:, jj * 256:(jj + 1) * 256]
                if passB:
                    self.cpy(xs.v(sl), X, eng="act")
                cstep(X, eb.v(sl), A8, X)
            if passB:
                r.dma("sp", self.X_d[j0:j0 + nj, :].rearrange("j (q f) -> q j f", f=256),
                      xs.ap[:, 0:nj * 256].rearrange("q (j f) -> q j f", j=nj), reads=[xs.key], writes=["X_d"])
        if not passB and not fused:
            Lout = self.outp("Lout", [Q, 256])
            r.dma("sp", Lout[:, :], X.ap, reads=[X.key], writes=["Lout"])
            self.final_keys = getattr(self, "final_keys", []) + ["Lout"]
        if not passB and fused:
            self.Lsrc_d = self.scratch("Lsrc_d", [Q, 256])
            self.Lall_d = self.scratch("Lall_d", [4 * Q, 256])
            r.dma("sp", self.Lsrc_d[:, :], X.ap, reads=[X.key], writes=["Lsrc_d"])
            src_t, dst_t = self.Lsrc_d, self.Lall_d
            r.op("pool", lambda e: e.collective_compute("AllGather", ALU.bypass,
                                                        replica_groups=[[0, 1, 2, 3], [4, 5, 6, 7]],
                                                        ins=[src_t[:, :]], outs=[dst_t[:, :]]),
                 reads=["Lsrc_d"], writes=["Lall_d"], cc=True)
        r.barrier()

    def ssm_y(self):
        c, r = self.c, self.r
        ps = self.ps
        self.ssm_setup()
        d = self.ssm_in
        G, KT, NCH, JP = c.G, c.KT, c.NCH, self.JP
        self.phase_reset()
        hs8 = self.hs8()
        MD = self.gc(8 * 1024).rearrange("p (s g t x) -> p s g t x", s=8, g=8, t=8)
        Hb = self.gc(8 * 128).rearrange("p (g x) -> p g x", g=8)
        XT = self.gc(8 * 72).rearrange("p (g x) -> p g x", g=8)
        Xtok = [self.gc(1024) for i in range(2)]
        lamr2, lami2, dl2, lrd2, lid2 = (self.fb(G, "Y" + k) for k in ("lamr2", "lami2", "dl2", "lrd2", "lid2"))
        r.dma("sp", lamr2.ap, d["lamr2"][:, :], writes=[lamr2.key])
        r.dma("sp", lami2.ap, d["lami2"][:, :], writes=[lami2.key])
        r.dma("sp", dl2.ap, d["lstep2"][:, :], writes=[dl2.key])
        self.actv(dl2, dl2, AF.Exp)
        self.tt(lrd2, lamr2, dl2, ALU.mult)
        self.tt(lid2, lami2, dl2, ALU.mult)
        ssmc = V(self.ssmc[:, :], "ssmc")
        tau = ssmc.v(lambda a: a[:, 16:25].unsqueeze(1).broadcast_to([128, 8, 9]))
        phA = ssmc.v(lambda a: a[:, 25:26])
        phB = ssmc.v(lambda a: a[:, 26:27])
        sgn = ssmc.v(lambda a: a[:, 27:28])
        B72 = lambda k: self.fb(72, "Y" + k)
        ARG, MG, TA, TB, TM = B72("ARG"), B72("MG"), B72("TA"), B72("TB"), B72("TM")
        TMI = V(self.fc(72).bitcast(mybir.dt.int32), "YTMI")
        B128 = lambda k: self.fb(128, "Y" + k)
        CA, CB, BA, BB, Bst, Bt, Ksb, KTT = (B128(k) for k in ("CA", "CB", "BA", "BB", "Bst", "Bt", "Ksb", "KTT"))
        Wst, Wt = self.fb(1152, "YWst"), self.fb(1152, "YWt")
        ypre = self.fb(4 * 72, "Yypre")
        lr8, li8, lrd8, lid8 = (self.fb(8, "Y" + k) for k in ("lr8", "li8", "lrd8", "lid8"))
        w4 = lambda a: a.rearrange("p (g t x) -> p g t x", g=8, t=9)
        g3 = lambda a: a.rearrange("p (g x) -> p g x", g=8)
        t3 = lambda a: a.rearrange("p (g t) -> p g t", g=8)
        foff0 = self.foff
        rows_of = [(0, 64), (64, NCH - 64)]
        for dt in range(KT):
            self.foff = foff0
            gsl = lambda a, dt=dt: a[:, dt * 8:(dt + 1) * 8]
            for nm, tl in (("CA2", CA), ("CB2", CB), ("BA2", BA), ("BB2", BB)):
                r.dma("sp", tl.ap, d[nm][:, dt * 128:(dt + 1) * 128], writes=[tl.key])
            self.cpy(lr8, lamr2.v(gsl))
            self.cpy(li8, lami2.v(gsl))
            self.cpy(lrd8, lrd2.v(gsl))
            self.cpy(lid8, lid2.v(gsl))
            b9 = lambda a: a.unsqueeze(2).broadcast_to([128, 8, 9])
            self.tt(MG.v(t3), lrd8.v(b9), tau, ALU.mult)
            self.actv(MG, MG, AF.Exp)
            for TX, ph in ((TA, phA), (TB, phB)):
                self.tt(ARG.v(t3), lid8.v(b9), tau, ALU.mult)
                self.sincos(ARG, TX, None, TM, TMI, phase_col=ph)
                self.tt(TX, TX, MG, ALU.mult)
            fr2, fi2 = self.cabar_f(lr8, li8, lrd8, lid8, 8, "Yf")
            self.ts(fi2, fi2, sgn, ALU.mult)
            b16 = lambda a: a.unsqueeze(2).broadcast_to([128, 8, 16])
            self.tt(Bst.v(g3), BA.v(g3), fr2.v(b16), ALU.mult)
            self.tt(Bt.v(g3), BB.v(g3), fi2.v(b16), ALU.mult)
            self.tt(Bst, Bst, Bt, ALU.add)
            cw = lambda a: g3(a).unsqueeze(2).broadcast_to([128, 8, 9, 16])
            tw = lambda a: t3(a).unsqueeze(3).broadcast_to([128, 8, 9, 16])
            self.tt(Wst.v(w4), CA.v(cw), TA.v(tw), ALU.mult)
            self.tt(Wt.v(w4), CB.v(cw), TB.v(tw), ALU.mult)
            self.tt(Wst, Wst, Wt, ALU.add)
            self.cpy(V(Hb.rearrange("p g (t x) -> p g t x", t=8), "Hb"), Wst.v(lambda a: w4(a)[:, :, 1:9, :]), eng="act")
            psK = ps[7]
            for g8 in range(8):
                r.op("pe", lambda e, g8=g8: e.matmul(psK[:, g8 * 16:(g8 + 1) * 16],
                                                     lhsT=w4(Wst.ap)[:, g8, 0:8, :].rearrange("p t x -> p (t x)"),
                                                     rhs=g3(Bst.ap)[:, g8, :], start=True, stop=True),
                     reads=[Wst.key, Bst.key], writes=["ps7"])
            self.cpy(Ksb, V(psK[:, 0:128], "ps7"), eng="act")
            r.op("pe", lambda e: e.transpose(out=psK[:, 128:256], in_=Ksb.ap, identity=self.ident[:, :]),
                 reads=[Ksb.key, "ident"], writes=["ps7"])
            self.cpy(KTT, V(psK[:, 128:256], "ps7"), eng="act")
            self.mset(V(MD.rearrange("p s g t x -> p (s g t x)"), "MD"), 0.0, eng="pool")
            for s_ in range(8):
                nt_ = 8 - s_
                self.tt(V(MD[:, s_, :, s_:8, :], "MD"),
                        KTT.v(lambda a, nt_=nt_: a[:, 0:nt_ * 16].rearrange("p (t x) -> p t x", t=nt_).unsqueeze(1)
                              .broadcast_to([128, 8, nt_, 16])),
                        ssmc.v(lambda a, nt_=nt_: a[:, 0:8].unsqueeze(2).unsqueeze(3).broadcast_to([128, 8, nt_, 16])),
                        ALU.mult, eng="pool")
            for jb, (j0, nr) in enumerate(rows_of):
                xt_ = Xtok[jb]
                r.dma("sp", xt_[0:nr, :], self.X_d[j0:j0 + nr, dt * 1024:(dt + 1) * 1024], reads=["X_d"],
                      writes=[f"Xtok{jb}"])
                psX = ps[4].bitcast(BF16)
                for g8 in range(8):
                    r.op("pe", lambda e, g8=g8, nr=nr, xt_=xt_: e.transpose(
                        out=psX[:, g8 * 72:g8 * 72 + nr], in_=xt_[0:nr, g8 * 128:(g8 + 1) * 128],
                        identity=self.identb[0:nr, 0:nr]), reads=[f"Xtok{jb}", "identb"], writes=["ps4"])
                self.cpy(V(XT[:, :, 0:nr], "XT"),
                         V(psX[:, 0:576].rearrange("p (g x) -> p g x", g=8)[:, :, 0:nr], "ps4"), eng="act")
                ytok = self.xbuf[jb]
                for half in range(2):
                    bank = jb * 2 + half
                    for s_ in range(8):
                        r.op("pe", lambda e, s_=s_, j0=j0, nr=nr, half=half, bank=bank, dt=dt: e.matmul(
                            ps[bank][0:nr, 0:512], lhsT=hs8[:, dt, s_, j0:j0 + nr],
                            rhs=MD[:, s_, half * 4:(half + 1) * 4, :, :].rearrange("p g t x -> p (g t x)"),
                            start=(s_ == 0), stop=False), reads=["hT", "MD"], writes=[f"ps{bank}"])
                    for gq in range(4):
                        g8 = half * 4 + gq
                        r.op("pe", lambda e, g8=g8, gq=gq, nr=nr, bank=bank: e.matmul(
                            ps[bank][0:nr, gq * 128:(gq + 1) * 128], lhsT=XT[:, g8, 0:nr], rhs=Hb[:, g8, :],
                            start=False, stop=(gq == 3)), reads=["XT", "Hb"], writes=[f"ps{bank}"])
                    self.cpy(V(ytok[0:nr, 0:1024].rearrange("j (t g x) -> j g t x", t=8, g=8)[:, half * 4:(half + 1) * 4, :, :],
                               f"xbuf{jb}"),
                             V(ps[bank][0:nr, 0:512].rearrange("j (g t x) -> j g t x", g=4, t=8), f"ps{bank}"),
                             eng=("act" if half == 0 else "dve"))
                for th in range(2):
                    bank = 5 + th
                    for tq in range(4):
                        t_ = th * 4 + tq
                        r.op("pe", lambda e, t_=t_, tq=tq, nr=nr, ytok=ytok, bank=bank: e.transpose(
                            out=ps[bank][:, tq * 72:tq * 72 + nr], in_=ytok[0:nr, t_ * 128:(t_ + 1) * 128],
                            identity=self.ident[0:nr, 0:nr]), reads=[f"xbuf{jb}", "ident"], writes=[f"ps{bank}"])
                    yp = ypre.v(lambda a, nr=nr: a.rearrange("p (t x) -> p t x", t=4)[:, :, 0:nr])
                    hsl = V(hs8[:, dt, th * 4:(th + 1) * 4, j0:j0 + nr], "hT")
                    self.stt(yp, hsl, V(self.dskip[:, dt:dt + 1], "dskip"),
                             V(ps[bank][:, 0:288].rearrange("p (t x) -> p t x", t=4)[:, :, 0:nr], f"ps{bank}"),
                             ALU.mult, ALU.add)
                    self.actv(hsl, yp, AF.Gelu_apprx_tanh)
        r.barrier()

    def glu(self):
        c, r = self.c, self.r
        ps = self.ps
        KT, JP, NCH = c.KT, self.JP, c.NCH
        w_glu = self.inp("w_glu", [c.D, 2 * c.D])
        self.phase_reset()
        gy = self.hreg[:, 0:KT * 8 * JP].rearrange("p (k n) -> p k n", k=KT)
        sig = self.fc(8 * JP)
        ysts = [self.fc(c.NT) for i in range(3)]
        ntl = [(0, 3), (3, 3), (6, 2)]
        gtiles = {}
        for m in range(KT):
            for which in range(2):
                if m % 2 == 0:
                    b = self.nextw()
                    self.load_w(b, w_glu, which * c.D + m * 128, 256, 0)
                    gtiles[which] = b
                b = gtiles[which]
                wb = self.wst[b]
                wc0 = (m % 2) * 128
                for i, (t0, ntt) in enumerate(ntl):
                    bank = which * 3 + i
                    for k in range(KT):
                        r.op("pe", lambda e, k=k, wb=wb, t0=t0, ntt=ntt, bank=bank, wc0=wc0: e.matmul(
                            ps[bank][:, 0:ntt * JP], lhsT=wb[:, k, wc0:wc0 + 128], rhs=gy[:, k, t0 * JP:(t0 + ntt) * JP],
                            start=(k == 0), stop=(k == KT - 1)), reads=[f"wst{b}", "hT"], writes=[f"ps{bank}"])
            yst = ysts[m % 3]
            ystk = f"gyst{m%3}"
            for i, (t0, ntt) in enumerate(ntl):
                r.op("act", lambda e, i=i, t0=t0, ntt=ntt: e.activation(
                    out=sig[:, t0 * JP:(t0 + ntt) * JP], in_=ps[3 + i][:, 0:ntt * JP], func=AF.Sigmoid),
                    reads=[f"ps{3+i}"], writes=["sig"])
                r.op("dve", lambda e, i=i, t0=t0, ntt=ntt, yst=yst: e.tensor_tensor(
                    out=yst[:, 0:c.NT].rearrange("p (j t) -> p t j", t=8)[:, t0:t0 + ntt, :],
                    in0=ps[i][:, 0:ntt * JP].rearrange("p (t j) -> p t j", j=JP)[:, :, 0:NCH],
                    in1=sig[:, t0 * JP:(t0 + ntt) * JP].rearrange("p (t j) -> p t j", j=JP)[:, :, 0:NCH], op=ALU.mult),
                    reads=[f"ps{i}", "sig"], writes=[ystk])
            self.defer(lambda m=m, yst=yst, ystk=ystk: r.dma(
                "pool", self.xres[m * 128:(m + 1) * 128, :], yst[:, :], reads=[ystk], writes=[f"xres{m}"],
                accum_op=ALU.add))
        self.flush()
        r.barrier()

    def load_x(self, name):
        c, r = self.c, self.r
        xin = self.inp(name, [c.D, c.NT])
        for m in range(c.KT):
            xb = self.xbuf[m % 2]
            r.dma("sp", xb[:, 0:c.NT], xin[m * 128:(m + 1) * 128, :], writes=[f"xbuf{m%2}"])
            r.dma("sp", self.xres[m * 128:(m + 1) * 128, :], xb[:, 0:c.NT], reads=[f"xbuf{m%2}"], writes=[f"xres{m}"])
        r.barrier()

    def copy_x_to_xres(self):
        c, r = self.c, self.r
        xT0 = self.inp("xT0", [c.D, c.NKV])
        for m in range(c.KT):
            xb = self.xbuf[m % 2]
            r.dma("sp", xb[:, 0:c.NT], xT0[m * 128:(m + 1) * 128, c.NKV - c.NT:c.NKV], writes=[f"xbuf{m%2}"])
            r.dma("sp", self.xres[m * 128:(m + 1) * 128, :], xb[:, 0:c.NT], reads=[f"xbuf{m%2}"], writes=[f"xres{m}"])
        r.barrier()

    def dump_xres(self, name):
        c, r = self.c, self.r
        o = self.outp(name, [c.D, c.NT])
        for m in range(c.KT):
            xb = self.xbuf[m % 2]
            r.dma("sp", xb[:, 0:c.NT], self.xres[m * 128:(m + 1) * 128, :], reads=[f"xres{m}"], writes=[f"xbuf{m%2}"])
            r.dma("sp", o[m * 128:(m + 1) * 128, :], xb[:, 0:c.NT], reads=[f"xbuf{m%2}"], writes=[name])
        self.final_keys = getattr(self, "final_keys", []) + [name]

    def final_norm(self):
        c, r = self.c, self.r
        ps = self.ps
        o = self.outp("outT", [c.D, c.NOWN])
        ncols = c.NT
        nts = ntiles(ncols)
        src = self.xres
        for k in range(c.KT):
            xb = self.xbuf[k % 2]
            sq = self.sq[k % 2]
            r.dma("sp", xb[:, 0:ncols], src[k * 128:(k + 1) * 128, :], reads=[f"xres{k}"], writes=[f"xbuf{k%2}"])
            r.op("act", lambda e, xb=xb, sq=sq: e.activation(out=sq[:, 0:ncols], in_=xb[:, 0:ncols], func=AF.Square),
                 reads=[f"xbuf{k%2}"], writes=[f"sq{k%2}"])
            for i, (s, n) in enumerate(nts):
                r.op("pe", lambda e, i=i, s=s, n=n, sq=sq, k=k: e.matmul(
                    ps[i][:, 0:n], lhsT=self.ones_bf[:, :], rhs=sq[:, s:s + n],
                    start=(k == 0), stop=(k == c.KT - 1)), reads=[f"sq{k%2}", "ones"], writes=[f"ps{i}"])
        for i, (s, n) in enumerate(nts):
            r.op("dve", lambda e, i=i, s=s, n=n: e.tensor_scalar(
                out=self.rstd[:, s:s + n], in0=ps[i][:, 0:n], scalar1=1.0 / c.D, scalar2=EPS,
                op0=ALU.mult, op1=ALU.add), reads=[f"ps{i}"], writes=["rstd"])
        r.op("act", lambda e: e.sqrt(out=self.rstd[:, 0:ncols], in_=self.rstd[:, 0:ncols]),
             reads=["rstd"], writes=["rstd"])
        r.op("dve", lambda e: e.reciprocal(out=self.rstd[:, 0:ncols], in_=self.rstd[:, 0:ncols]),
             reads=["rstd"], writes=["rstd"])
        gcol = 4 * c.KT
        for k in range(c.KT):
            xb = self.xbuf[k % 2]
            r.dma("sp", xb[:, 0:ncols], src[k * 128:(k + 1) * 128, :], reads=[f"xres{k}"], writes=[f"xbuf{k%2}"])
            r.op("dve", lambda e, xb=xb, k=k: e.scalar_tensor_tensor(
                out=xb[:, 0:ncols], in0=xb[:, 0:ncols], scalar=self.gains[:, gcol + k:gcol + k + 1],
                in1=self.rstd[:, 0:ncols], op0=ALU.mult, op1=ALU.mult),
                reads=[f"xbuf{k%2}", "rstd", "gains"], writes=[f"xbuf{k%2}"])
            r.dma("sp", o[k * 128:(k + 1) * 128, :], xb[:, c.HALO:c.HALO + c.NOWN], reads=[f"xbuf{k%2}"],
                  writes=["outT"])
        self.final_keys = getattr(self, "final_keys", []) + ["outT"]

    def finish(self):
        nc, r = self.nc, self.r
        r.wait_keys("sp", getattr(self, "final_keys", []))
        with contextlib.ExitStack() as st:
            sems = {e: st.enter_context(nc.semaphore("s_" + e)) for e in ENGS}
            dsems = {(e, s): st.enter_context(nc.semaphore(f"d_{e}{s}")) for e in ("sp", "pool", "act")
                     for s in range(NSLOT)}
            dsems[("cc", 0)] = st.enter_context(nc.semaphore("cc_sem"))
            r.emit(nc, sems, dsems)
        return nc


def build(cfg, phases):
    p = Prog(cfg, phases)
    p.setup_common()
    for ph in phases:
        if ph == "attn":
            p.attention()
        elif ph == "copyx":
            p.copy_x_to_xres()
        elif ph == "ffn0":
            p.ffn(0)
        elif ph == "ffn1":
            p.ffn(1)
        elif ph == "final":
            p.final_norm()
        elif ph == "ssm":
            p.ssm_E()
            p.ssm_scan(False, fused=True)
            p.ssm_scan(True, fused=True)
            p.ssm_y()
            p.glu()
        elif ph == "ssmA":
            p.ssm_E()
            p.ssm_scan(False)
        elif ph == "ssmB":
            p.ssm_E()
            p.ssm_scan(True)
            p.ssm_y()
            p.glu()
        elif ph.startswith("loadx:"):
            p.load_x(ph[6:])
        elif ph.startswith("dump:"):
            p.dump_xres(ph[5:])
        else:
            raise ValueError(ph)
    nc = p.finish()
    return p, nc


MASKV = -30000.0


def _t5_buckets(dist):
    n = np.maximum(dist, 0)
    is_small = n < 16
    large = 16 + (np.log(np.maximum(n, 1) / 16) / np.log(128 / 16) * (32 - 16)).astype(np.int32)
    large = np.minimum(large, 31)
    return np.where(is_small, n, large).astype(np.int32)


def _prep_shared(cfg, I):
    D, F, KT, FT, G, NH = cfg.D, cfg.F, cfg.KT, cfg.FT, cfg.G, cfg.NH
    m = {}
    vecs = [I['attn_norm'][0], I['ffn_norm'][0], I['ssm_norm'][0], I['ffn_norm'][1], I['final_norm']]
    m['gains'] = np.ascontiguousarray(np.concatenate([np.asarray(v).reshape(KT, 128).T for v in vecs], axis=1),
                                      dtype=np.float32)
    s = np.arange(128)[:, None]
    q = np.arange(128)[None, :]
    rb = np.asarray(I['rel_bias'])
    bt = np.empty((NH // 2, 128, 4, 128), np.float32)
    for kb in range(2):
        dist = q - s + (128 if kb == 0 else 0)
        valid = (dist >= 0) & (dist < 128)
        bk = _t5_buckets(dist)
        for j in range(NH // 2):
            for hh in range(2):
                bt[j, :, hh * 2 + kb, :] = np.where(valid, rb[bk, 2 * j + hh], np.float32(MASKV))
    m['bias_tab'] = bt.reshape(NH // 2, 128, 512)
    sk = np.asarray(I['sinks'][0])
    s2 = np.empty((128, NH // 2), np.float32)
    s2[:64, :] = sk[0::2][None, :]
    s2[64:, :] = sk[1::2][None, :]
    m['sinks2'] = s2
    m['w_qkv'] = I['w_qkv'][0]
    m['w_o'] = I['w_o'][0]
    for li in range(2):
        cw = np.asarray(I['conv_w'][li])
        cb = np.asarray(I['conv_b'][li])
        rows = [cw[0], cw[1], cw[2], cb]
        cp = np.stack([r_.reshape(2 * FT, 128).T for r_ in rows], axis=1)
        m[f'convp{li}'] = np.ascontiguousarray(cp.reshape(128, 4 * 2 * FT), dtype=np.float32)
        m[f'w_up{li}'] = I['w_up'][li]
        m[f'w_down{li}'] = I['w_down'][li]
    lr = np.asarray(I['lambda_re'][0]); li_ = np.asarray(I['lambda_im'][0]); ls = np.asarray(I['log_step'][0])
    br = np.asarray(I['b_re'][0]); bi = np.asarray(I['b_im'][0])
    cr = np.asarray(I['c_re'][0]); ci = np.asarray(I['c_im'][0])
    OFF = 64.0 * math.pi
    sc = np.zeros((128, 32), np.float32)
    p = np.arange(128)
    sc[:, 0:8] = (p[:, None] // 16 == np.arange(8)[None, :])
    sc[:, 8:16] = 7 - np.arange(8)[None, :]
    sc[:, 16:25] = np.arange(9)[None, :]
    sc[:, 25] = np.where(p < 64, 0.5 * math.pi, math.pi) + OFF
    sc[:, 26] = np.where(p < 64, math.pi, 1.5 * math.pi) + OFF
    sc[:, 27] = np.where(p < 64, -1.0, 1.0)
    m['ssmc'] = sc
    m['ident'] = np.eye(128, dtype=np.float32)
    g_of = (np.arange(KT)[None, :] * 8 + (p[:, None] // 16))
    cidx = p % 16
    m['lamr1'] = np.ascontiguousarray(lr[g_of].reshape(128, KT * 64))
    m['lami1'] = np.ascontiguousarray(li_[g_of].reshape(128, KT * 64))
    m['bre1'] = np.ascontiguousarray(br[g_of, :, cidx[:, None]].reshape(128, KT * 64))
    m['bim1'] = np.ascontiguousarray(bi[g_of, :, cidx[:, None]].reshape(128, KT * 64))
    m['lstep1'] = np.ascontiguousarray(ls[g_of])
    pp = p % 64
    m['lamr2'] = np.ascontiguousarray(lr[:, pp].T)
    m['lami2'] = np.ascontiguousarray(li_[:, pp].T)
    m['lstep2'] = np.ascontiguousarray(np.broadcast_to(ls[None, :], (128, G)), dtype=np.float32)
    crT = cr.transpose(2, 0, 1)
    ciT = ci.transpose(2, 0, 1)
    m['CA2'] = np.ascontiguousarray(np.concatenate([crT, crT], 0).reshape(128, G * 16))
    m['CB2'] = np.ascontiguousarray(np.concatenate([ciT, ciT], 0).reshape(128, G * 16))
    brT = br.transpose(1, 0, 2)
    biT = bi.transpose(1, 0, 2)
    m['BA2'] = np.ascontiguousarray(np.concatenate([brT, biT], 0).reshape(128, G * 16))
    m['BB2'] = np.ascontiguousarray(np.concatenate([biT, brT], 0).reshape(128, G * 16))
    m['lamrS'] = np.ascontiguousarray(lr.reshape(G // 2, 128))
    m['lamiS'] = np.ascontiguousarray(li_.reshape(G // 2, 128))
    m['lstepS'] = np.ascontiguousarray(ls.reshape(G // 2, 2))
    m['dskip'] = np.ascontiguousarray(np.asarray(I['d_skip'][0]).reshape(KT, 128).T)
    m['w_glu'] = I['w_glu'][0]
    return m


def _prep_core(c, cfg, I):
    b = c // 4
    ch = c % 4
    t0 = ch * cfg.NOWN
    m = {}
    xs = np.zeros((cfg.NKV, cfg.D), np.float32)
    lo = t0 - cfg.KVH
    src_lo = max(lo, 0)
    xs[src_lo - lo:, :] = I['x'][b, src_lo:t0 + cfg.NOWN, :]
    m['xT0'] = np.ascontiguousarray(xs.T)
    m['flag'] = np.full((128, 1), 0.0 if ch == 0 else 1.0, np.float32)
    m['negmask'] = np.full((128, 128), MASKV if ch == 0 else 0.0, np.float32)
    w = np.zeros((cfg.G // 2, 8), np.float32)
    if FUSED:
        w[:, 0:ch] = 1.0
    else:
        for r_ in range(8):
            if r_ // 4 == b and r_ % 4 < ch:
                w[:, r_] = 1.0
    m['wsel'] = w
    return m


FUSED = True
PHASES = ["attn", "ffn0", "ssm", "ffn1", "final"]
PHASES1 = ["attn", "ffn0", "ssmA", "dump:x2T"]
PHASES2 = ["loadx:x2T", "ssmB", "ffn1", "final"]


def run_model(cfg, I):
    shared = _prep_shared(cfg, I)
    cores = [_prep_core(c, cfg, I) for c in range(8)]
    if FUSED:
        p, nc = build(cfg, PHASES)
        maps = [{k: (cores[c][k] if k in cores[c] else shared[k]) for k in p.din} for c in range(8)]
        res = run_bass_kernel_spmd(nc, maps, core_ids=list(range(8))).results
        out = np.empty(I['x'].shape, np.float32)
        for c in range(8):
            out[c // 4, (c % 4) * cfg.NOWN:(c % 4 + 1) * cfg.NOWN, :] = np.asarray(res[c]['outT']).T
        return out
    p1, nc1 = build(cfg, PHASES1)
    maps1 = [{k: (cores[c][k] if k in cores[c] else shared[k]) for k in p1.din} for c in range(8)]
    r1 = run_bass_kernel_spmd(nc1, maps1, core_ids=list(range(8))).results
    del maps1
    Lall = np.stack([np.asarray(r1[c]['Lout']) for c in range(8)])
    p2, nc2 = build(cfg, PHASES2)
    maps2 = []
    for c in range(8):
        mm = {}
        for k in p2.din:
            if k == 'x2T':
                mm[k] = np.asarray(r1[c]['x2T'])
            elif k == 'Lall':
                mm[k] = Lall
            elif k in cores[c]:
                mm[k] = cores[c][k]
            else:
                mm[k] = shared[k]
        maps2.append(mm)
    r2 = run_bass_kernel_spmd(nc2, maps2, core_ids=list(range(8))).results
    B = I['x'].shape[0]
    out = np.empty(I['x'].shape, np.float32)
    for c in range(8):
        b = c // 4
        ch = c % 4
        out[b, ch * cfg.NOWN:(ch + 1) * cfg.NOWN, :] = np.asarray(r2[c]['outT']).T
    return out


def kernel(**inputs):
    I = {k: np.asarray(v) for k, v in inputs.items()}
    cfg = Cfg(D=I['x'].shape[2], F=I['w_down'].shape[1])
    return run_model(cfg, I)
```

```python
import contextlib
import math
import numpy as np
import concourse.bass as bass
import concourse.mybir as mybir
from concourse.bass_utils import run_bass_kernel_spmd

F32 = mybir.dt.float32
BF16 = mybir.dt.bfloat16
ALU = mybir.AluOpType
AF = mybir.ActivationFunctionType

ENGS = ("pe", "act", "dve", "pool", "sp")
NSLOT = 8
SAME_ENGINE_IN_ORDER = ("pe",)
EPS = 1e-6
NEG = -1e30


class Op:
    __slots__ = ("eng", "fn", "deps", "dma", "idx", "needed", "sem", "val", "slot")

    def __init__(self, eng, fn, dma):
        self.eng = eng
        self.fn = fn
        self.dma = dma
        self.deps = []
        self.needed = False
        self.sem = None
        self.val = 0
        self.slot = -1


class V:
    __slots__ = ("ap", "key")

    def __init__(self, ap, key):
        self.ap = ap
        self.key = key

    def v(self, f):
        return V(f(self.ap), self.key)


class Rec:
    def __init__(self):
        self.ops = {e: [] for e in ENGS}
        self.lastw = {}
        self.readers = {}
        self.seen = {e: {p: -1 for p in ENGS} for e in ENGS}
        self.seen_dma = {e: set() for e in ENGS}
        self.slot_last = {}
        self.slot_rr = {e: 0 for e in ENGS}

    def _add_dep(self, op, prod):
        if prod is None or prod is op:
            return
        e = op.eng
        if prod.dma:
            if id(prod) in self.seen_dma[e]:
                return
            self.seen_dma[e].add(id(prod))
        else:
            if prod.eng == e and e in SAME_ENGINE_IN_ORDER:
                return
            if prod.idx <= self.seen[e][prod.eng]:
                return
            self.seen[e][prod.eng] = prod.idx
        prod.needed = True
        op.deps.append(prod)

    def op(self, eng, fn, reads=(), writes=(), dma=False, cc=False):
        o = Op(eng, fn, dma or cc)
        o.idx = len(self.ops[eng])
        for k in reads:
            self._add_dep(o, self.lastw.get(k))
        for k in writes:
            self._add_dep(o, self.lastw.get(k))
            for r in self.readers.get(k, ()):
                self._add_dep(o, r)
        if cc:
            o.slot = "cc"
            prev = self.slot_last.get(("cc", 0))
            if prev is not None:
                self._add_dep(o, prev)
            self.slot_last[("cc", 0)] = o
            o.needed = True
        elif dma:
            s = self.slot_rr[eng]
            self.slot_rr[eng] = (s + 1) % NSLOT
            o.slot = s
            prev = self.slot_last.get((eng, s))
            if prev is not None:
                self._add_dep(o, prev)
            self.slot_last[(eng, s)] = o
            o.needed = True
        for k in reads:
            self.readers.setdefault(k, []).append(o)
        for k in writes:
            self.lastw[k] = o
            self.readers[k] = []
        self.ops[eng].append(o)
        return o

    def dma(self, eng, out, in_, reads=(), writes=(), **kw):
        return self.op(eng, lambda e: e.dma_start(out=out, in_=in_, **kw), reads, writes, dma=True)

    def barrier(self):
        lasts = []
        for e in ENGS:
            for o in reversed(self.ops[e]):
                if not o.dma and o.fn is not None:
                    lasts.append(o)
                    break
        dmas = [o for o in self.slot_last.values()]
        for e in ENGS:
            o = Op(e, None, False)
            o.idx = len(self.ops[e])
            for p in lasts + dmas:
                self._add_dep(o, p)
            self.ops[e].append(o)

    def wait_keys(self, eng, keys):
        o = Op(eng, None, False)
        o.idx = len(self.ops[eng])
        for k in keys:
            self._add_dep(o, self.lastw.get(k))
        self.ops[eng].append(o)

    def emit(self, nc, sems, dsems):
        for e in ENGS:
            cnt = 0
            dcnt = {}
            for o in self.ops[e]:
                if o.dma and o.slot == "cc":
                    k = ("cc", 0)
                    dcnt[k] = dcnt.get(k, 0) + 1
                    o.sem = dsems[k]
                    o.val = dcnt[k]
                elif o.dma:
                    k = (e, o.slot)
                    dcnt[k] = dcnt.get(k, 0) + 16
                    o.sem = dsems[k]
                    o.val = dcnt[k]
                elif o.needed:
                    cnt += 1
                    o.sem = sems[e]
                    o.val = cnt
        engobj = {"pe": "tensor", "act": "scalar", "dve": "vector", "pool": "gpsimd", "sp": "sync"}
        with nc.Block() as block:
            for e in ENGS:
                ops = self.ops[e]
                if not ops:
                    continue

                def body(eng, ops=ops):
                    for o in ops:
                        for p in o.deps:
                            eng.wait_ge(p.sem, p.val)
                        if o.fn is None:
                            continue
                        ins = o.fn(eng)
                        if o.dma and o.slot == "cc":
                            ins.then_inc(o.sem)
                        elif o.dma:
                            ins.then_inc(o.sem, 16)
                        elif o.needed:
                            ins.then_inc(o.sem, 1)

                getattr(block, engobj[e])(body)


class Cfg:
    def __init__(self, D=4096, F=11008, NOWN=1024):
        self.D = D
        self.F = F
        self.NOWN = NOWN
        self.HALO = 16
        self.NT = NOWN + self.HALO
        self.KVH = 256
        self.NKV = NOWN + self.KVH
        self.KT = D // 128
        self.FT = F // 128
        self.NH = D // 64
        self.NKVH = self.NH // 8
        self.QW = D
        self.KVW = self.NKVH * 64
        self.G = D // 16
        self.FC = 14 if self.FT >= 28 else 4
        self.NQB = self.NKV // 128 - 1
        self.NCH = self.NT // 8


def ntiles(n):
    if n == 1040:
        return [(0, 352), (352, 352), (704, 336)]
    if n == 1280:
        return [(i * 320, 320) for i in range(4)]
    k = (n + 511) // 512
    base = (n + k - 1) // k
    base = (base + 31) // 32 * 32
    out = []
    s = 0
    while s < n:
        m = min(base, n - s)
        out.append((s, m))
        s += m
    return out


class Prog:
    def __init__(self, cfg, phases):
        self.c = cfg
        self.phases = phases
        self.nc = bass.Bass("TRN2", target_bir_lowering=False)
        self.r = Rec()
        self.din = {}
        self.dout = {}

    def inp(self, name, shape, dt=F32):
        if name in self.din:
            return self.din[name]
        t = self.nc.dram_tensor(name, list(shape), dt, kind="ExternalInput").ap()
        self.din[name] = t
        return t

    def outp(self, name, shape, dt=F32):
        t = self.nc.dram_tensor(name, list(shape), dt, kind="ExternalOutput").ap()
        self.dout[name] = t
        return t

    def scratch(self, name, shape, dt=F32):
        return self.nc.dram_tensor(name, list(shape), dt).ap()

    def sb(self, name, shape, dt):
        return self.nc.alloc_sbuf_tensor("sb_" + name, list(shape), dt)

    def setup_common(self):
        c, nc, r = self.c, self.nc, self.r
        self.ps = [nc.alloc_psum_tensor(f"ps{i}", [128, 512], F32) for i in range(8)]
        self.hreg = self.sb("hreg", [128, c.KT * c.NKV], BF16)
        self.GCOLS = 14592
        self.FCOLS = 6400
        self.greg = self.sb("greg", [128, self.GCOLS], BF16)
        self.freg = self.sb("freg", [128, self.FCOLS], F32)
        self.goff = 0
        self.foff = 0
        self.ones_bf = self.sb("ones_bf", [128, 128], BF16)
        self.xbuf = [self.sb(f"xbuf{i}", [128, c.NKV], F32) for i in range(2)]
        self.sq = [self.sb(f"sq{i}", [128, c.NKV], BF16) for i in range(2)]
        self.rstd = self.sb("rstd", [128, c.NKV], F32)
        self.NW = 3
        self.wst = [self.sb(f"wst{i}", [128, c.KT, 256], BF16) for i in range(self.NW)]
        self.wi = 0
        self.gains = self.sb("gains", [128, 5 * c.KT], F32)
        self.flag = self.sb("flag", [128, 1], F32)
        g_in = self.inp("gains", [128, 5 * c.KT])
        f_in = self.inp("flag", [128, 1])
        r.op("pool", lambda e: e.memset(self.ones_bf[:], 1.0), writes=["ones"])
        r.dma("sp", self.gains[:], g_in[:, :], writes=["gains"])
        r.dma("sp", self.flag[:], f_in[:, :], writes=["flag"])
        self.xres = self.scratch("xres", [c.D, c.NT])

    def phase_reset(self):
        self.goff = 0
        self.foff = 0

    def gc(self, cols):
        a = self.greg[:, self.goff:self.goff + cols]
        self.goff += (cols + 15) // 16 * 16
        assert self.goff <= self.GCOLS, ("greg overflow", self.goff)
        return a

    def fc(self, cols):
        a = self.freg[:, self.foff:self.foff + cols]
        self.foff += (cols + 7) // 8 * 8
        assert self.foff <= self.FCOLS, ("freg overflow", self.foff)
        return a

    def defer(self, fn):
        if not hasattr(self, "pending"):
            self.pending = []
        self.pending.append(fn)

    def flush(self):
        for fn in getattr(self, "pending", []):
            fn()
        self.pending = []

    def nextw(self):
        b = self.wi % self.NW
        self.wi += 1
        return b

    def hview(self, ncols):
        c = self.c
        return self.hreg[:, 0:c.KT * ncols].rearrange("p (k n) -> p k n", k=c.KT)

    def rmsnorm(self, src, src_key, col0, ncols, gcol, hT, hkey, use_flag, s_major=False):
        c, r = self.c, self.r
        nts = ntiles(ncols)
        ps = self.ps
        for k in range(c.KT):
            xb = self.xbuf[k % 2]
            sq = self.sq[k % 2]
            r.dma("sp", xb[:, 0:ncols], src[k * 128:(k + 1) * 128, col0:col0 + ncols],
                  reads=[f"{src_key}{k}"], writes=[f"xbuf{k%2}"])
            r.op("act", lambda e, xb=xb, sq=sq: e.activation(out=sq[:, 0:ncols], in_=xb[:, 0:ncols], func=AF.Square),
                 reads=[f"xbuf{k%2}"], writes=[f"sq{k%2}"])
            for i, (s, n) in enumerate(nts):
                r.op("pe", lambda e, i=i, s=s, n=n, sq=sq, k=k: e.matmul(
                    ps[i][:, 0:n], lhsT=self.ones_bf[:, :], rhs=sq[:, s:s + n],
                    start=(k == 0), stop=(k == c.KT - 1)),
                    reads=[f"sq{k%2}", "ones"], writes=[f"ps{i}"])
        for i, (s, n) in enumerate(nts):
            r.op("dve", lambda e, i=i, s=s, n=n: e.tensor_scalar(
                out=self.rstd[:, s:s + n], in0=ps[i][:, 0:n], scalar1=1.0 / c.D, scalar2=EPS,
                op0=ALU.mult, op1=ALU.add), reads=[f"ps{i}"], writes=["rstd"])
        r.op("act", lambda e: e.sqrt(out=self.rstd[:, 0:ncols], in_=self.rstd[:, 0:ncols]),
             reads=["rstd"], writes=["rstd"])
        r.op("dve", lambda e: e.reciprocal(out=self.rstd[:, 0:ncols], in_=self.rstd[:, 0:ncols]),
             reads=["rstd"], writes=["rstd"])
        if use_flag:
            r.op("dve", lambda e: e.tensor_scalar(
                out=self.rstd[:, 0:c.HALO], in0=self.rstd[:, 0:c.HALO], scalar1=self.flag[:, 0:1],
                scalar2=None, op0=ALU.mult), reads=["rstd", "flag"], writes=["rstd"])
        for k in range(c.KT):
            xb = self.xbuf[k % 2]
            r.dma("sp", xb[:, 0:ncols], src[k * 128:(k + 1) * 128, col0:col0 + ncols],
                  reads=[f"{src_key}{k}"], writes=[f"xbuf{k%2}"])
            if s_major:
                hs8 = self.hs8()
                r.op("dve", lambda e, xb=xb, k=k, hs8=hs8: e.scalar_tensor_tensor(
                    out=hs8[:, k, :, 0:c.NCH].rearrange("p s j -> p j s"),
                    in0=xb[:, 0:ncols].rearrange("p (j s) -> p j s", s=8),
                    scalar=self.gains[:, gcol + k:gcol + k + 1],
                    in1=self.rstd[:, 0:ncols].rearrange("p (j s) -> p j s", s=8), op0=ALU.mult, op1=ALU.mult),
                    reads=[f"xbuf{k%2}", "rstd", "gains"], writes=[hkey])
                continue
            r.op("dve", lambda e, xb=xb, k=k: e.scalar_tensor_tensor(
                out=hT[:, k, :], in0=xb[:, 0:ncols], scalar=self.gains[:, gcol + k:gcol + k + 1],
                in1=self.rstd[:, 0:ncols], op0=ALU.mult, op1=ALU.mult),
                reads=[f"xbuf{k%2}", "rstd", "gains"], writes=[hkey])

    def load_w(self, buf_i, w, col0, ncols=128, dstcol=0, eng="pool"):
        wb = self.wst[buf_i]
        self.r.dma(eng, wb[:, :, dstcol:dstcol + ncols],
                   w[:, col0:col0 + ncols].rearrange("(k p) m -> p k m", p=128),
                   writes=[f"wst{buf_i}"])
        self.flush()
        return wb

    def linear_tile(self, wb, wkey, hT, hkey, nts, bank0, col_off=0, wc0=0):
        c, r = self.c, self.r
        for i, (s, n) in enumerate(nts):
            for k in range(c.KT):
                r.op("pe", lambda e, i=i, s=s, n=n, k=k: e.matmul(
                    self.ps[bank0 + i][:, 0:n], lhsT=wb[:, k, wc0:wc0 + 128], rhs=hT[:, k, col_off + s:col_off + s + n],
                    start=(k == 0), stop=(k == c.KT - 1)),
                    reads=[wkey, hkey], writes=[f"ps{bank0+i}"])

    def attention(self):
        c, nc, r = self.c, self.nc, self.r
        ps = self.ps
        xT0 = self.inp("xT0", [c.D, c.NKV])
        w_qkv = self.inp("w_qkv", [c.D, c.QW + 2 * c.KVW])
        w_o = self.inp("w_o", [c.QW, c.D])
        bias_tab = self.inp("bias_tab", [c.NH // 2, 128, 512])
        sinks2_in = self.inp("sinks2", [128, c.NH // 2])
        negmask_in = self.inp("negmask", [128, 128])
        oT_d = self.scratch("oT_d", [c.D, c.NQB * 128], BF16)
        nblk = c.NKV // 128

        self.phase_reset()
        kdup = [self.gc(c.NKV) for i in range(2)]
        vtok = [self.gc(nblk * 64).rearrange("p (a n) -> p a n", a=nblk) for i in range(2)]
        qT = [self.gc(c.NKV) for i in range(2)]
        pbf = [self.gc(512) for i in range(2)]
        otile = [self.gc(c.NQB * 128) for i in range(2)]
        biasT = [self.fc(512) for i in range(2)]
        esb = [self.fc(512) for i in range(2)]
        esink = self.fc(c.NH // 2)
        negmask = self.fc(128)
        rd = self.fc(128)

        r.dma("sp", esink, sinks2_in[:, :], writes=["esink"])
        r.op("act", lambda e: e.activation(out=esink, in_=esink, func=AF.Exp), reads=["esink"], writes=["esink"])
        r.dma("sp", negmask, negmask_in[:, :], writes=["negmask"])

        hT = self.hview(c.NKV)
        self.rmsnorm(xT0, "xT0", 0, c.NKV, 0 * c.KT, hT, "hT", use_flag=False)
        nts = ntiles(c.NKV)
        it_ctr = 0
        for kvh in range(c.NKVH):
            kd = kdup[kvh % 2]
            kk = f"kdup{kvh%2}"
            vt = vtok[kvh % 2]
            vk = f"vtok{kvh%2}"
            b = self.nextw()
            wb = self.wst[b]
            self.load_w(b, w_qkv, c.QW + kvh * 64, 64, 0)
            self.load_w(b, w_qkv, c.QW + kvh * 64, 64, 64)
            self.load_w(b, w_qkv, c.QW + c.KVW + kvh * 64, 64, 128)
            self.linear_tile(wb, f"wst{b}", hT, "hT", nts, 0, wc0=0)
            for i, (s, n) in enumerate(nts):
                r.op("act", lambda e, i=i, s=s, n=n, kd=kd: e.copy(out=kd[:, s:s + n], in_=ps[i][:, 0:n]),
                     reads=[f"ps{i}"], writes=[kk])
            for blk in range(nblk):
                bank = 4 if blk % 2 == 0 else 5
                for k in range(c.KT):
                    r.op("pe", lambda e, blk=blk, k=k, bank=bank, wb=wb: e.matmul(
                        ps[bank][:, 0:64], lhsT=hT[:, k, blk * 128:(blk + 1) * 128], rhs=wb[:, k, 128:192],
                        start=(k == 0), stop=(k == c.KT - 1)), reads=["hT", f"wst{b}"], writes=[f"ps{bank}"])
                r.op("act", lambda e, blk=blk, bank=bank, vt=vt: e.copy(out=vt[:, blk, :], in_=ps[bank][:, 0:64]),
                     reads=[f"ps{bank}"], writes=[vk])
            for jp in range(2):
                bq = self.nextw()
                wq = self.wst[bq]
                j0 = kvh * 4 + jp * 2
                self.load_w(bq, w_qkv, j0 * 128, 256, 0)
                for jj in range(2):
                    j = j0 + jj
                    q = qT[j % 2]
                    qk = f"qT{j%2}"
                    self.linear_tile(wq, f"wst{bq}", hT, "hT", nts, 0, wc0=jj * 128)
                    for i, (s, n) in enumerate(nts):
                        r.op("act", lambda e, i=i, s=s, n=n, q=q: e.copy(out=q[:, s:s + n], in_=ps[i][:, 0:n]),
                             reads=[f"ps{i}"], writes=[qk])
                    bt = biasT[j % 2]
                    r.dma("sp", bt, bias_tab[j, :, :], writes=[f"biasT{j%2}"])
                    ot = otile[j % 2]
                    otk = f"otile{j%2}"
                    prev_second = None
                    for qb in range(1, c.NQB + 1):
                        it = it_ctr % 2
                        it_ctr += 1
                        scs = [ps[6], ps[7]]
                        od = ps[4 + it]
                        odk = f"ps{4+it}"
                        e_sb = esb[it]
                        p_sb = pbf[it]
                        for hh in range(2):
                            for kb in range(2):
                                kblk = qb - 1 + kb
                                r.op("pe", lambda e, hh=hh, kb=kb, kblk=kblk, q=q, qb=qb, kd=kd, scs=scs: e.matmul(
                                    scs[hh][:, kb * 128:(kb + 1) * 128],
                                    lhsT=kd[hh * 64:(hh + 1) * 64, kblk * 128:(kblk + 1) * 128],
                                    rhs=q[hh * 64:(hh + 1) * 64, qb * 128:(qb + 1) * 128],
                                    start=True, stop=True), reads=[kk, qk], writes=[f"ps{6+hh}"])
                        for hh in range(2):
                            r.op("dve", lambda e, e_sb=e_sb, bt=bt, hh=hh, scs=scs: e.scalar_tensor_tensor(
                                out=e_sb[:, hh * 256:(hh + 1) * 256], in0=scs[hh][:, 0:256], scalar=0.125,
                                in1=bt[:, hh * 256:(hh + 1) * 256], op0=ALU.mult, op1=ALU.add),
                                reads=[f"ps{6+hh}", f"biasT{j%2}"], writes=[f"esb{it}"])
                        if qb == 2:
                            for hh in range(2):
                                r.op("dve", lambda e, e_sb=e_sb, hh=hh: e.tensor_tensor(
                                    out=e_sb[:, hh * 256:hh * 256 + 128], in0=e_sb[:, hh * 256:hh * 256 + 128],
                                    in1=negmask, op=ALU.add), reads=[f"esb{it}", "negmask"], writes=[f"esb{it}"])
                        r.op("act", lambda e, e_sb=e_sb, p_sb=p_sb: e.activation(out=p_sb, in_=e_sb, func=AF.Exp),
                             reads=[f"esb{it}"], writes=[f"pbf{it}"])
                        def second(qb=qb, it=it, od=od, odk=odk, p_sb=p_sb):
                            for hh in range(2):
                                for kb in range(2):
                                    r.op("pe", lambda e, hh=hh, kb=kb, p_sb=p_sb, od=od: e.matmul(
                                        od[hh * 64:(hh + 1) * 64, 0:128], lhsT=self.ones_bf[:, 0:64],
                                        rhs=p_sb[:, (hh * 2 + kb) * 128:(hh * 2 + kb + 1) * 128],
                                        start=(kb == 0), stop=(kb == 1)), reads=[f"pbf{it}", "ones"], writes=[odk])
                            for hh in range(2):
                                for kb in range(2):
                                    kblk = qb - 1 + kb
                                    r.op("pe", lambda e, hh=hh, kb=kb, kblk=kblk, p_sb=p_sb, od=od, vt=vt: e.matmul(
                                        od[hh * 64:(hh + 1) * 64, 128:256], lhsT=vt[:, kblk, :],
                                        rhs=p_sb[:, (hh * 2 + kb) * 128:(hh * 2 + kb + 1) * 128],
                                        start=(kb == 0), stop=(kb == 1)), reads=[f"pbf{it}", vk], writes=[odk])
                            r.op("dve", lambda e, j=j, od=od: e.tensor_scalar(
                                out=rd, in0=od[:, 0:128], scalar1=esink[:, j:j + 1], scalar2=None, op0=ALU.add),
                                reads=[odk, "esink"], writes=["rd"])
                            r.op("dve", lambda e: e.reciprocal(out=rd, in_=rd), reads=["rd"], writes=["rd"])
                            r.op("dve", lambda e, ot=ot, qb=qb, od=od: e.tensor_tensor(
                                out=ot[:, (qb - 1) * 128:qb * 128], in0=od[:, 128:256], in1=rd, op=ALU.mult),
                                reads=[odk, "rd"], writes=[otk])
                        if prev_second is not None:
                            prev_second()
                        prev_second = second
                    prev_second()
                    r.dma("sp", oT_d[j * 128:(j + 1) * 128, :], ot, reads=[otk], writes=["oT_d"])

        r.barrier()
        oT = self.hview(c.NT)
        ocol0 = c.NQB * 128 - c.NT
        for k in range(c.KT):
            r.dma("sp", oT[:, k, :], oT_d[k * 128:(k + 1) * 128, ocol0:ocol0 + c.NT], reads=["oT_d"], writes=["oT"])
        nt3 = ntiles(c.NT)
        xcol0 = c.NKV - c.NT
        for mp in range(c.KT // 2):
            b = self.nextw()
            self.load_w(b, w_o, mp * 256, 256, 0)
            for half in range(2):
                m = mp * 2 + half
                bank0 = (m % 2) * 3
                self.linear_tile(self.wst[b], f"wst{b}", oT, "oT", nt3, bank0, wc0=half * 128)
                xb = self.xbuf[m % 2]
                r.dma("sp", xb[:, 0:c.NT], xT0[m * 128:(m + 1) * 128, xcol0:xcol0 + c.NT], reads=["xT0"],
                      writes=[f"xbuf{m%2}"])
                for i, (s, n) in enumerate(nt3):
                    r.op("dve", lambda e, i=i, s=s, n=n, xb=xb, bank0=bank0: e.tensor_tensor(
                        out=xb[:, s:s + n], in0=ps[bank0 + i][:, 0:n], in1=xb[:, s:s + n], op=ALU.add),
                        reads=[f"ps{bank0+i}", f"xbuf{m%2}"], writes=[f"xbuf{m%2}"])
                r.dma("sp", self.xres[m * 128:(m + 1) * 128, :], xb[:, 0:c.NT], reads=[f"xbuf{m%2}"], writes=[f"xres{m}"])
        r.barrier()

    def ffn(self, li):
        c, nc, r = self.c, self.nc, self.r
        ps = self.ps
        w_up = self.inp(f"w_up{li}", [c.D, 2 * c.F])
        w_down = self.inp(f"w_down{li}", [c.F, c.D])
        convp_in = self.inp(f"convp{li}", [128, 4 * 2 * c.FT])
        self.phase_reset()
        if not hasattr(self, "convp_sb"):
            self.convp_sb = self.sb("convp", [128, 4 * 2 * c.FT], F32)
        convp = self.convp_sb
        act = [self.gc(c.FC * c.NT).rearrange("p (a n) -> p a n", a=c.FC)] * 2
        tail0 = c.KT * c.NT
        if c.KT * c.NKV - tail0 >= 2 * c.FC * 256:
            wdn = [self.hreg[:, tail0 + i * c.FC * 256:tail0 + (i + 1) * c.FC * 256].rearrange("p (a n) -> p a n", a=c.FC)
                   for i in range(2)]
        else:
            wdn = [self.gc(c.FC * 256).rearrange("p (a n) -> p a n", a=c.FC) for i in range(2)]
        u = [self.fc(c.NT) for i in range(2)]
        cc = [self.fc(c.NT) for i in range(2)]
        sg = self.fc(c.NT)
        yst = [self.fc(c.NT)] * 2
        r.dma("sp", convp[:], convp_in[:, :], writes=["convp"])
        cp = convp[:, :].rearrange("p (j f) -> p j f", j=4)

        hT = self.hview(c.NT)
        self.rmsnorm(self.xres, "xres", 0, c.NT, (1 if li == 0 else 3) * c.KT, hT, "hT", use_flag=True)
        nt3 = ntiles(c.NT)
        NT = c.NT
        chunks = [list(range(s, min(s + c.FC, c.FT))) for s in range(0, c.FT, c.FC)]
        evi = 0
        sti = 0
        stg = [(u[0], "u0"), (u[1], "u1"), (cc[0], "cc0"), (cc[1], "cc1"), (sg, "sg"), (yst[0], "yst")]
        for ci, chunk in enumerate(chunks):
            actb = act[0]
            ak = "actb"
            wtiles = {}
            for li_, f in enumerate(chunk):
                for which in range(2):
                    if li_ % 2 == 0:
                        b = self.nextw()
                        ncol = 256 if li_ + 1 < len(chunk) else 128
                        self.load_w(b, w_up, which * c.F + f * 128, ncol, 0)
                        wtiles[which] = b
                    b = wtiles[which]
                    bank0 = which * 3
                    self.linear_tile(self.wst[b], f"wst{b}", hT, "hT", nt3, bank0, wc0=(li_ % 2) * 128)
                    uu = u[which]
                    cw = cc[which]
                    col = which * c.FT + f
                    for i, (s, n) in enumerate(nt3):
                        r.op("act", lambda e, i=i, s=s, n=n, uu=uu, bank0=bank0: e.copy(
                            out=uu[:, s:s + n], in_=ps[bank0 + i][:, 0:n]),
                            reads=[f"ps{bank0+i}"], writes=[f"u{which}"])
                    r.op("act", lambda e, uu=uu, cw=cw, col=col: e.activation(
                        out=cw[:, :], in_=uu[:, :], func=AF.Identity, scale=cp[:, 2, col:col + 1],
                        bias=cp[:, 3, col:col + 1]), reads=[f"u{which}", "convp"], writes=[f"cc{which}"])
                    r.op("dve", lambda e, uu=uu, cw=cw, col=col: e.scalar_tensor_tensor(
                        out=cw[:, 1:NT], in0=uu[:, 0:NT - 1], scalar=cp[:, 1, col:col + 1], in1=cw[:, 1:NT],
                        op0=ALU.mult, op1=ALU.add), reads=[f"u{which}", f"cc{which}", "convp"], writes=[f"cc{which}"])
                    r.op("dve", lambda e, uu=uu, cw=cw, col=col: e.scalar_tensor_tensor(
                        out=cw[:, 2:NT], in0=uu[:, 0:NT - 2], scalar=cp[:, 0, col:col + 1], in1=cw[:, 2:NT],
                        op0=ALU.mult, op1=ALU.add), reads=[f"u{which}", f"cc{which}", "convp"], writes=[f"cc{which}"])
                r.op("act", lambda e: e.activation(out=sg[:, :], in_=cc[0][:, :], func=AF.Silu),
                     reads=["cc0"], writes=["sg"])
                r.op("dve", lambda e, actb=actb, li_=li_: e.tensor_tensor(
                    out=actb[:, li_, :], in0=sg[:, :], in1=cc[1][:, :], op=ALU.mult),
                    reads=["sg", "cc1"], writes=[ak])
            nf = len(chunk)
            f0 = chunk[0]
            for mp in range(c.KT // 2):
                wd = wdn[mp % 2]
                wk = f"wdn{mp%2}"
                r.dma("pool", wd[:, 0:nf, :],
                      w_down[f0 * 128:(f0 + nf) * 128, mp * 256:(mp + 1) * 256].rearrange("(k p) m -> p k m", p=128),
                      writes=[wk])
                self.flush()
                for half in range(2):
                    m = mp * 2 + half
                    bank0 = (m % 2) * 3
                    for i, (s, n) in enumerate(nt3):
                        for q in range(nf):
                            r.op("pe", lambda e, i=i, s=s, n=n, q=q, wd=wd, half=half, bank0=bank0, actb=actb, nf=nf: e.matmul(
                                ps[bank0 + i][:, 0:n], lhsT=wd[:, q, half * 128:(half + 1) * 128],
                                rhs=actb[:, q, s:s + n], start=(q == 0), stop=(q == nf - 1)),
                                reads=[wk, ak], writes=[f"ps{bank0+i}"])
                    st, stk = stg[sti % len(stg)]
                    sti += 1
                    for i, (s, n) in enumerate(nt3):
                        eng = "act" if (evi % 2 == 0) else "dve"
                        evi += 1
                        if eng == "act":
                            r.op("act", lambda e, i=i, s=s, n=n, st=st, bank0=bank0: e.copy(
                                out=st[:, s:s + n], in_=ps[bank0 + i][:, 0:n]),
                                reads=[f"ps{bank0+i}"], writes=[stk])
                        else:
                            r.op("dve", lambda e, i=i, s=s, n=n, st=st, bank0=bank0: e.tensor_copy(
                                out=st[:, s:s + n], in_=ps[bank0 + i][:, 0:n]),
                                reads=[f"ps{bank0+i}"], writes=[stk])
                    self.defer(lambda m=m, st=st, stk=stk: r.dma(
                        "pool", self.xres[m * 128:(m + 1) * 128, :], st[:, :], reads=[stk],
                        writes=[f"xres{m}"], accum_op=ALU.add))
        self.flush()
        r.barrier()

    JP = 144

    def hs8(self):
        c = self.c
        return self.hreg[:, 0:c.KT * 8 * self.JP].rearrange("p (k s j) -> p k s j", k=c.KT, s=8)

    def vb(self, ap, key):
        return V(ap, key)

    def fb(self, n, key, rows=None):
        ap = self.fc(n)
        if rows is not None:
            ap = ap[0:rows, :]
        return V(ap, key)

    def tt(self, o, a, b, op, eng="dve"):
        self.r.op(eng, lambda e: e.tensor_tensor(out=o.ap, in0=a.ap, in1=b.ap, op=op),
                  reads=[a.key, b.key], writes=[o.key])

    def ts(self, o, a, s1, op0, s2=None, op1=None, eng="dve"):
        rd = [a.key]
        v1 = s1
        if isinstance(s1, V):
            rd.append(s1.key)
            v1 = s1.ap
        if op1 is None:
            self.r.op(eng, lambda e: e.tensor_scalar(out=o.ap, in0=a.ap, scalar1=v1, scalar2=None, op0=op0),
                      reads=rd, writes=[o.key])
        else:
            self.r.op(eng, lambda e: e.tensor_scalar(out=o.ap, in0=a.ap, scalar1=v1, scalar2=s2, op0=op0, op1=op1),
                      reads=rd, writes=[o.key])

    def stt(self, o, a, sc, b, op0, op1, eng="dve"):
        rd = [a.key, b.key]
        v = sc
        if isinstance(sc, V):
            rd.append(sc.key)
            v = sc.ap
        self.r.op(eng, lambda e: e.scalar_tensor_tensor(out=o.ap, in0=a.ap, scalar=v, in1=b.ap, op0=op0, op1=op1),
                  reads=rd, writes=[o.key])

    def actv(self, o, a, func):
        self.r.op("act", lambda e: e.activation(out=o.ap, in_=a.ap, func=func), reads=[a.key], writes=[o.key])

    def cpy(self, o, a, eng="dve"):
        if eng == "act":
            self.r.op("act", lambda e: e.copy(out=o.ap, in_=a.ap), reads=[a.key], writes=[o.key])
        else:
            self.r.op(eng, lambda e: e.tensor_copy(out=o.ap, in_=a.ap), reads=[a.key], writes=[o.key])

    def recip(self, o, a):
        self.r.op("dve", lambda e: e.reciprocal(out=o.ap, in_=a.ap), reads=[a.key], writes=[o.key])

    def mset(self, o, val, eng="dve"):
        self.r.op(eng, lambda e: e.memset(o.ap, val), writes=[o.key])

    def sincos(self, arg, sn, cs, tmp, tmpi, phase_col=None):
        OFF = 64.0 * math.pi
        TWO_PI = 2.0 * math.pi
        if phase_col is None:
            self.ts(arg, arg, OFF, ALU.add)
        else:
            self.ts(arg, arg, phase_col, ALU.add)
        self.ts(tmp, arg, 1.0 / TWO_PI, ALU.mult)
        self.cpy(tmpi, tmp)
        self.cpy(tmp, tmpi)
        self.stt(arg, tmp, -TWO_PI, arg, ALU.mult, ALU.add)

        def wrap(hi):
            if hi:
                self.ts(tmp, arg, math.pi, ALU.is_gt, -TWO_PI, ALU.mult)
            else:
                self.ts(tmp, arg, -math.pi, ALU.is_lt, TWO_PI, ALU.mult)
            self.tt(arg, arg, tmp, ALU.add)
        wrap(True)
        wrap(False)
        self.actv(sn, arg, AF.Sin)
        if cs is not None:
            self.ts(arg, arg, 0.5 * math.pi, ALU.add)
            wrap(True)
            self.actv(cs, arg, AF.Sin)

    def ssm_setup(self):
        c, r = self.c, self.r
        if hasattr(self, "ssm_in"):
            return
        G, KT = c.G, c.KT
        Q = G // 2
        d = {}
        d["ssmc"] = self.inp("ssmc", [128, 32])
        d["ident"] = self.inp("ident", [128, 128])
        for nm in ("lamr1", "lami1", "bre1", "bim1"):
            d[nm] = self.inp(nm, [128, KT * 64])
        d["lstep1"] = self.inp("lstep1", [128, KT])
        for nm in ("lamr2", "lami2", "lstep2"):
            d[nm] = self.inp(nm, [128, G])
        for nm in ("CA2", "CB2", "BA2", "BB2"):
            d[nm] = self.inp(nm, [128, G * 16])
        d["lamrS"] = self.inp("lamrS", [Q, 128])
        d["lamiS"] = self.inp("lamiS", [Q, 128])
        d["lstepS"] = self.inp("lstepS", [Q, 2])
        d["dskip"] = self.inp("dskip", [128, KT])
        self.ssm_in = d
        self.E_d = self.scratch("E_d", [c.NCH, G * 128])
        self.X_d = self.scratch("X_d", [c.NCH, G * 128], BF16)
        self.ssmc = self.sb("ssmc", [128, 32], F32)
        self.ident = self.sb("ident", [128, 128], F32)
        self.identb = self.sb("identb", [128, 128], BF16)
        self.dskip = self.sb("dskip", [128, KT], F32)
        r.dma("sp", self.ssmc[:], d["ssmc"][:, :], writes=["ssmc"])
        r.dma("sp", self.ident[:], d["ident"][:, :], writes=["ident"])
        r.dma("sp", self.dskip[:], d["dskip"][:, :], writes=["dskip"])
        r.op("dve", lambda e: e.tensor_copy(out=self.identb[:], in_=self.ident[:]), reads=["ident"], writes=["identb"])

    def cabar_f(self, lamr, lami, lrd, lid, n, pre, rows=None):
        B = lambda k: self.fb(n, pre + k, rows)
        arg, tmp, sn, cs, mg, den, fr, fi = B("arg"), B("tmp"), B("sn"), B("cs"), B("mg"), B("den"), B("fr"), B("fi")
        tmpi = V(self.fc(n).bitcast(mybir.dt.int32) if rows is None else self.fc(n)[0:rows, :].bitcast(mybir.dt.int32),
                 pre + "tmpi")
        self.actv(mg, lrd, AF.Exp)
        self.cpy(arg, lid)
        self.sincos(arg, sn, cs, tmp, tmpi)
        self.tt(cs, cs, mg, ALU.mult)
        self.tt(sn, sn, mg, ALU.mult)
        self.ts(cs, cs, -1.0, ALU.add)
        self.tt(den, lamr, lamr, ALU.mult)
        self.tt(tmp, lami, lami, ALU.mult)
        self.tt(den, den, tmp, ALU.add)
        self.recip(den, den)
        self.tt(fr, cs, lamr, ALU.mult)
        self.tt(tmp, sn, lami, ALU.mult)
        self.tt(fr, fr, tmp, ALU.add)
        self.tt(fr, fr, den, ALU.mult)
        self.tt(fi, sn, lamr, ALU.mult)
        self.tt(tmp, cs, lami, ALU.mult)
        self.tt(fi, fi, tmp, ALU.subtract)
        self.tt(fi, fi, den, ALU.mult)
        return fr, fi

    def ssm_E(self):
        c, r = self.c, self.r
        ps = self.ps
        self.ssm_setup()
        d = self.ssm_in
        KT, JP, NCH = c.KT, self.JP, c.NCH
        self.phase_reset()
        hs8 = self.hs8()
        r.op("pool", lambda e: e.memset(self.hreg[:, 0:KT * 8 * JP], 0.0), writes=["hT"])
        self.rmsnorm(self.xres, "xres", 0, c.NT, 2 * KT, None, "hT", use_flag=True, s_major=True)

        BD = self.gc(8 * 1024).rearrange("p (s g x) -> p s g x", s=8, g=8)
        ssmc = V(self.ssmc[:, :], "ssmc")
        pw = ssmc.v(lambda a: a[:, 8:16].unsqueeze(2).broadcast_to([128, 8, 64]))
        mask4 = ssmc.v(lambda a: a[:, 0:8].unsqueeze(1).unsqueeze(3).broadcast_to([128, 8, 8, 64]))
        dl1 = self.fb(KT, "dl1")
        r.dma("sp", dl1.ap, d["lstep1"][:, :], writes=["dl1"])
        self.actv(dl1, dl1, AF.Exp)
        B64 = lambda k: self.fb(64, "E" + k)
        lamr, lami, bre, bim, lrd, lid = B64("lamr"), B64("lami"), B64("bre"), B64("bim"), B64("lrd"), B64("lid")
        bbr, bbi, t64 = B64("bbr"), B64("bbi"), B64("t64")
        B512 = lambda k: self.fb(512, "E" + k)
        ARG, MAG, SN, CS, W1, W2, TMP = B512("ARG"), B512("MAG"), B512("SN"), B512("CS"), B512("W1"), B512("W2"), B512("TMP")
        TMPI = V(self.fc(512).bitcast(mybir.dt.int32), "ETMPI")
        v3 = lambda a: a.rearrange("p (s x) -> p s x", s=8)
        b3 = lambda a: a.unsqueeze(1).broadcast_to([128, 8, 64])
        foff0 = self.foff
        rows_of = [(0, 64), (64, NCH - 64)]
        for dt in range(KT):
            self.foff = foff0
            for nm, tl in (("lamr1", lamr), ("lami1", lami), ("bre1", bre), ("bim1", bim)):
                r.dma("sp", tl.ap, d[nm][:, dt * 64:(dt + 1) * 64], writes=[tl.key])
            dcol = dl1.v(lambda a, dt=dt: a[:, dt:dt + 1])
            self.ts(lrd, lamr, dcol, ALU.mult)
            self.ts(lid, lami, dcol, ALU.mult)
            self.tt(MAG.v(v3), lrd.v(b3), pw, ALU.mult)
            self.actv(MAG, MAG, AF.Exp)
            fr, fi = self.cabar_f(lamr, lami, lrd, lid, 64, "Ef")
            self.tt(bbr, fr, bre, ALU.mult)
            self.tt(t64, fi, bim, ALU.mult)
            self.tt(bbr, bbr, t64, ALU.subtract)
            self.tt(bbi, fr, bim, ALU.mult)
            self.tt(t64, fi, bre, ALU.mult)
            self.tt(bbi, bbi, t64, ALU.add)
            self.tt(ARG.v(v3), lid.v(b3), pw, ALU.mult)
            self.sincos(ARG, SN, CS, TMP, TMPI)
            self.tt(CS, CS, MAG, ALU.mult)
            self.tt(SN, SN, MAG, ALU.mult)
            self.tt(W1.v(v3), CS.v(v3), bbr.v(b3), ALU.mult)
            self.tt(TMP.v(v3), SN.v(v3), bbi.v(b3), ALU.mult)
            self.tt(W1, W1, TMP, ALU.subtract)
            self.tt(W2.v(v3), CS.v(v3), bbi.v(b3), ALU.mult)
            self.tt(TMP.v(v3), SN.v(v3), bbr.v(b3), ALU.mult)
            self.tt(W2, W2, TMP, ALU.add)
            for comp, W in ((0, W1), (1, W2)):
                self.tt(V(BD[:, :, :, comp * 64:(comp + 1) * 64], "BD"),
                        W.v(lambda a: v3(a).unsqueeze(2).broadcast_to([128, 8, 8, 64])), mask4, ALU.mult, eng="pool")
            for jb, (j0, nr) in enumerate(rows_of):
                est = self.xbuf[jb]
                for s_ in range(8):
                    for half in range(2):
                        bank = jb * 2 + half
                        r.op("pe", lambda e, s_=s_, j0=j0, nr=nr, half=half, bank=bank, dt=dt: e.matmul(
                            ps[bank][0:nr, 0:512], lhsT=hs8[:, dt, s_, j0:j0 + nr],
                            rhs=BD[:, s_, half * 4:(half + 1) * 4, :].rearrange("p g x -> p (g x)"),
                            start=(s_ == 0), stop=(s_ == 7)), reads=["hT", "BD"], writes=[f"ps{bank}"])
                for half in range(2):
                    bank = jb * 2 + half
                    self.cpy(V(est[0:nr, half * 512:(half + 1) * 512], f"xbuf{jb}"),
                             V(ps[bank][0:nr, 0:512], f"ps{bank}"), eng=("act" if half == 0 else "dve"))
                r.dma("sp", self.E_d[j0:j0 + nr, dt * 1024:(dt + 1) * 1024], est[0:nr, 0:1024],
                      reads=[f"xbuf{jb}"], writes=["E_d"])
        r.barrier()

    def ssm_scan(self, passB, fused=False):
        c, r = self.c, self.r
        self.ssm_setup()
        d = self.ssm_in
        G, NCH = c.G, c.NCH
        Q = G // 2
        self.phase_reset()
        B = lambda n, k: self.fb(n, "S" + k, Q)
        lamr, lami, dl, lrd, lid = B(128, "lamr"), B(128, "lami"), B(2, "dl"), B(128, "lrd"), B(128, "lid")
        ARG, MAG, SN, CS, TMP = B(128, "ARG"), B(128, "MAG"), B(128, "SN"), B(128, "CS"), B(128, "TMP")
        TMPI = V(self.fc(128)[0:Q, :].bitcast(mybir.dt.int32), "STMPI")
        X, t1, t2 = B(256, "X"), B(256, "t1"), B(256, "t2")
        r.dma("sp", lamr.ap, d["lamrS"][:, :], writes=[lamr.key])
        r.dma("sp", lami.ap, d["lamiS"][:, :], writes=[lami.key])
        r.dma("sp", dl.ap, d["lstepS"][:, :], writes=[dl.key])
        self.actv(dl, dl, AF.Exp)
        for gi in range(2):
            sl = lambda a, gi=gi: a[:, gi * 64:(gi + 1) * 64]
            dcol = dl.v(lambda a, gi=gi: a[:, gi:gi + 1])
            self.ts(lrd.v(sl), lamr.v(sl), dcol, ALU.mult)
            self.ts(lid.v(sl), lami.v(sl), dcol, ALU.mult)

        def power(n, tag):
            oR, oI, oIn = B(128, tag + "R"), B(128, tag + "I"), B(128, tag + "In")
            self.ts(MAG, lrd, float(n), ALU.mult)
            self.actv(MAG, MAG, AF.Exp)
            self.ts(ARG, lid, float(n), ALU.mult)
            self.sincos(ARG, SN, CS, TMP, TMPI)
            self.tt(oR, CS, MAG, ALU.mult)
            self.tt(oI, SN, MAG, ALU.mult)
            self.ts(oIn, oI, -1.0, ALU.mult)
            return oR, oI, oIn

        x4 = lambda a: a.rearrange("q (g k p) -> q g k p", g=2, k=2)
        a4 = lambda a: a.rearrange("q (g p) -> q g p", g=2)

        def cstep(Xs, Ein, A, outX):
            aR, aI, aIn = A
            self.tt(t1.v(x4), Xs.v(x4), aR.v(lambda a: a4(a).unsqueeze(2).broadcast_to([Q, 2, 2, 64])), ALU.mult)
            self.tt(t2.v(lambda a: x4(a)[:, :, 0, :]), Xs.v(lambda a: x4(a)[:, :, 1, :]), aIn.v(a4), ALU.mult)
            self.tt(t2.v(lambda a: x4(a)[:, :, 1, :]), Xs.v(lambda a: x4(a)[:, :, 0, :]), aI.v(a4), ALU.mult)
            self.tt(t1, t1, t2, ALU.add)
            self.tt(outX, t1, Ein, ALU.add)

        A8 = power(8, "A8")
        self.mset(X, 0.0)
        if passB:
            if fused:
                nsrc = 4
                Lsrc = [self.Lall_d[rr * Q:(rr + 1) * Q, :] for rr in range(nsrc)]
                lkey = "Lall_d"
            else:
                nsrc = 8
                Lall_in = self.inp("Lall", [8, Q, 256])
                Lsrc = [Lall_in[rr, :, :] for rr in range(nsrc)]
                lkey = None
            wsel_in = self.inp("wsel", [Q, 8])
            A1k = power(1024, "A1k")
            wsel, Lt, Xn = B(8, "wsel"), B(256, "Lt"), B(256, "Xn")
            r.dma("sp", wsel.ap, wsel_in[:, :], writes=[wsel.key])
            for rr in range(nsrc):
                r.dma("sp", Lt.ap, Lsrc[rr], reads=([lkey] if lkey else []), writes=[Lt.key])
                cstep(X, Lt, A1k, Xn)
                self.tt(t2, Xn, X, ALU.subtract)
                self.stt(X, t2, wsel.v(lambda a, rr=rr: a[:, rr:rr + 1]), X, ALU.mult, ALU.add)
        JB = 5
        last = NCH - 1 if passB else NCH - 2
        xst = [V(self.gc(JB * 256)[0:Q, :], f"xst{i}") for i in range(2)]
        if passB:
            z = xst[0]
            self.mset(z.v(lambda a: a[:, 0:256]), 0.0, eng="pool")
            r.dma("sp", self.X_d[0:1, :].rearrange("j (q f) -> q j f", f=256),
                  z.ap[:, 0:256].rearrange("q (j f) -> q j f", j=1), reads=[z.key], writes=["X_d"])
        bi = 0
        for j0 in range(1, last + 1, JB):
            nj = min(JB, last + 1 - j0)
            eb = V(self.xbuf[bi % 2][0:Q, 0:JB * 256], f"xbuf{bi%2}")
            xs = xst[bi % 2]
            bi += 1
            r.dma("sp", eb.ap[:, 0:nj * 256].rearrange("q (j f) -> q j f", j=nj),
                  self.E_d[j0:j0 + nj, :].rearrange("j (q f) -> q j f", f=256), reads=["E_d"], writes=[eb.key])
            for jj in range(nj):
                sl = lambda a, jj=jj: a[:, jj * 256:(jj + 1) * 256]
                if passB:
                    self.cpy(xs.v(sl), X, eng="act")
                cstep(X, eb.v(sl), A8, X)
            if passB:
                r.dma("sp", self.X_d[j0:j0 + nj, :].rearrange("j (q f) -> q j f", f=256),
                      xs.ap[:, 0:nj * 256].rearrange("q (j f) -> q j f", j=nj), reads=[xs.key], writes=["X_d"])
        if not passB and not fused:
            Lout = self.outp("Lout", [Q, 256])
            r.dma("sp", Lout[:, :], X.ap, reads=[X.key], writes=["Lout"])
            self.final_keys = getattr(self, "final_keys", []) + ["Lout"]
        if not passB and fused:
            self.Lsrc_d = self.scratch("Lsrc_d", [Q, 256])
            self.Lall_d = self.scratch("Lall_d", [4 * Q, 256])
            r.dma("sp", self.Lsrc_d[:, :], X.ap, reads=[X.key], writes=["Lsrc_d"])
            src_t, dst_t = self.Lsrc_d, self.Lall_d
            r.op("pool", lambda e: e.collective_compute("AllGather", ALU.bypass,
                                                        replica_groups=[[0, 1, 2, 3], [4, 5, 6, 7]],
                                                        ins=[src_t[:, :]], outs=[dst_t[:, :]]),
                 reads=["Lsrc_d"], writes=["Lall_d"], cc=True)
        r.barrier()

    def ssm_y(self):
        c, r = self.c, self.r
        ps = self.ps
        self.ssm_setup()
        d = self.ssm_in
        G, KT, NCH, JP = c.G, c.KT, c.NCH, self.JP
        self.phase_reset()
        hs8 = self.hs8()
        MD = self.gc(8 * 1024).rearrange("p (s g t x) -> p s g t x", s=8, g=8, t=8)
        Hb = self.gc(8 * 128).rearrange("p (g x) -> p g x", g=8)
        XT = self.gc(8 * 72).rearrange("p (g x) -> p g x", g=8)
        Xtok = [self.gc(1024) for i in range(2)]
        lamr2, lami2, dl2, lrd2, lid2 = (self.fb(G, "Y" + k) for k in ("lamr2", "lami2", "dl2", "lrd2", "lid2"))
        r.dma("sp", lamr2.ap, d["lamr2"][:, :], writes=[lamr2.key])
        r.dma("sp", lami2.ap, d["lami2"][:, :], writes=[lami2.key])
        r.dma("sp", dl2.ap, d["lstep2"][:, :], writes=[dl2.key])
        self.actv(dl2, dl2, AF.Exp)
        self.tt(lrd2, lamr2, dl2, ALU.mult)
        self.tt(lid2, lami2, dl2, ALU.mult)
        ssmc = V(self.ssmc[:, :], "ssmc")
        tau = ssmc.v(lambda a: a[:, 16:25].unsqueeze(1).broadcast_to([128, 8, 9]))
        phA = ssmc.v(lambda a: a[:, 25:26])
        phB = ssmc.v(lambda a: a[:, 26:27])
        sgn = ssmc.v(lambda a: a[:, 27:28])
        B72 = lambda k: self.fb(72, "Y" + k)
        ARG, MG, TA, TB, TM = B72("ARG"), B72("MG"), B72("TA"), B72("TB"), B72("TM")
        TMI = V(self.fc(72).bitcast(mybir.dt.int32), "YTMI")
        B128 = lambda k: self.fb(128, "Y" + k)
        CA, CB, BA, BB, Bst, Bt, Ksb, KTT = (B128(k) for k in ("CA", "CB", "BA", "BB", "Bst", "Bt", "Ksb", "KTT"))
        Wst, Wt = self.fb(1152, "YWst"), self.fb(1152, "YWt")
        ypre = self.fb(4 * 72, "Yypre")
        lr8, li8, lrd8, lid8 = (self.fb(8, "Y" + k) for k in ("lr8", "li8", "lrd8", "lid8"))
        w4 = lambda a: a.rearrange("p (g t x) -> p g t x", g=8, t=9)
        g3 = lambda a: a.rearrange("p (g x) -> p g x", g=8)
        t3 = lambda a: a.rearrange("p (g t) -> p g t", g=8)
        foff0 = self.foff
        rows_of = [(0, 64), (64, NCH - 64)]
        for dt in range(KT):
            self.foff = foff0
            gsl = lambda a, dt=dt: a[:, dt * 8:(dt + 1) * 8]
            for nm, tl in (("CA2", CA), ("CB2", CB), ("BA2", BA), ("BB2", BB)):
                r.dma("sp", tl.ap, d[nm][:, dt * 128:(dt + 1) * 128], writes=[tl.key])
            self.cpy(lr8, lamr2.v(gsl))
            self.cpy(li8, lami2.v(gsl))
            self.cpy(lrd8, lrd2.v(gsl))
            self.cpy(lid8, lid2.v(gsl))
            b9 = lambda a: a.unsqueeze(2).broadcast_to([128, 8, 9])
            self.tt(MG.v(t3), lrd8.v(b9), tau, ALU.mult)
            self.actv(MG, MG, AF.Exp)
            for TX, ph in ((TA, phA), (TB, phB)):
                self.tt(ARG.v(t3), lid8.v(b9), tau, ALU.mult)
                self.sincos(ARG, TX, None, TM, TMI, phase_col=ph)
                self.tt(TX, TX, MG, ALU.mult)
            fr2, fi2 = self.cabar_f(lr8, li8, lrd8, lid8, 8, "Yf")
            self.ts(fi2, fi2, sgn, ALU.mult)
            b16 = lambda a: a.unsqueeze(2).broadcast_to([128, 8, 16])
            self.tt(Bst.v(g3), BA.v(g3), fr2.v(b16), ALU.mult)
            self.tt(Bt.v(g3), BB.v(g3), fi2.v(b16), ALU.mult)
            self.tt(Bst, Bst, Bt, ALU.add)
            cw = lambda a: g3(a).unsqueeze(2).broadcast_to([128, 8, 9, 16])
            tw = lambda a: t3(a).unsqueeze(3).broadcast_to([128, 8, 9, 16])
            self.tt(Wst.v(w4), CA.v(cw), TA.v(tw), ALU.mult)
            self.tt(Wt.v(w4), CB.v(cw), TB.v(tw), ALU.mult)
            self.tt(Wst, Wst, Wt, ALU.add)
            self.cpy(V(Hb.rearrange("p g (t x) -> p g t x", t=8), "Hb"), Wst.v(lambda a: w4(a)[:, :, 1:9, :]), eng="act")
            psK = ps[7]
            for g8 in range(8):
                r.op("pe", lambda e, g8=g8: e.matmul(psK[:, g8 * 16:(g8 + 1) * 16],
                                                     lhsT=w4(Wst.ap)[:, g8, 0:8, :].rearrange("p t x -> p (t x)"),
                                                     rhs=g3(Bst.ap)[:, g8, :], start=True, stop=True),
                     reads=[Wst.key, Bst.key], writes=["ps7"])
            self.cpy(Ksb, V(psK[:, 0:128], "ps7"), eng="act")
            r.op("pe", lambda e: e.transpose(out=psK[:, 128:256], in_=Ksb.ap, identity=self.ident[:, :]),
                 reads=[Ksb.key, "ident"], writes=["ps7"])
            self.cpy(KTT, V(psK[:, 128:256], "ps7"), eng="act")
            self.mset(V(MD.rearrange("p s g t x -> p (s g t x)"), "MD"), 0.0, eng="pool")
            for s_ in range(8):
                nt_ = 8 - s_
                self.tt(V(MD[:, s_, :, s_:8, :], "MD"),
                        KTT.v(lambda a, nt_=nt_: a[:, 0:nt_ * 16].rearrange("p (t x) -> p t x", t=nt_).unsqueeze(1)
                              .broadcast_to([128, 8, nt_, 16])),
                        ssmc.v(lambda a, nt_=nt_: a[:, 0:8].unsqueeze(2).unsqueeze(3).broadcast_to([128, 8, nt_, 16])),
                        ALU.mult, eng="pool")
            for jb, (j0, nr) in enumerate(rows_of):
                xt_ = Xtok[jb]
                r.dma("sp", xt_[0:nr, :], self.X_d[j0:j0 + nr, dt * 1024:(dt + 1) * 1024], reads=["X_d"],
                      writes=[f"Xtok{jb}"])
                psX = ps[4].bitcast(BF16)
                for g8 in range(8):
                    r.op("pe", lambda e, g8=g8, nr=nr, xt_=xt_: e.transpose(
                        out=psX[:, g8 * 72:g8 * 72 + nr], in_=xt_[0:nr, g8 * 128:(g8 + 1) * 128],
                        identity=self.identb[0:nr, 0:nr]), reads=[f"Xtok{jb}", "identb"], writes=["ps4"])
                self.cpy(V(XT[:, :, 0:nr], "XT"),
                         V(psX[:, 0:576].rearrange("p (g x) -> p g x", g=8)[:, :, 0:nr], "ps4"), eng="act")
                ytok = self.xbuf[jb]
                for half in range(2):
                    bank = jb * 2 + half
                    for s_ in range(8):
                        r.op("pe", lambda e, s_=s_, j0=j0, nr=nr, half=half, bank=bank, dt=dt: e.matmul(
                            ps[bank][0:nr, 0:512], lhsT=hs8[:, dt, s_, j0:j0 + nr],
                            rhs=MD[:, s_, half * 4:(half + 1) * 4, :, :].rearrange("p g t x -> p (g t x)"),
                            start=(s_ == 0), stop=False), reads=["hT", "MD"], writes=[f"ps{bank}"])
                    for gq in range(4):
                        g8 = half * 4 + gq
                        r.op("pe", lambda e, g8=g8, gq=gq, nr=nr, bank=bank: e.matmul(
                            ps[bank][0:nr, gq * 128:(gq + 1) * 128], lhsT=XT[:, g8, 0:nr], rhs=Hb[:, g8, :],
                            start=False, stop=(gq == 3)), reads=["XT", "Hb"], writes=[f"ps{bank}"])
                    self.cpy(V(ytok[0:nr, 0:1024].rearrange("j (t g x) -> j g t x", t=8, g=8)[:, half * 4:(half + 1) * 4, :, :],
                               f"xbuf{jb}"),
                             V(ps[bank][0:nr, 0:512].rearrange("j (g t x) -> j g t x", g=4, t=8), f"ps{bank}"),
                             eng=("act" if half == 0 else "dve"))
                for th in range(2):
                    bank = 5 + th
                    for tq in range(4):
                        t_ = th * 4 + tq
                        r.op("pe", lambda e, t_=t_, tq=tq, nr=nr, ytok=ytok, bank=bank: e.transpose(
                            out=ps[bank][:, tq * 72:tq * 72 + nr], in_=ytok[0:nr, t_ * 128:(t_ + 1) * 128],
                            identity=self.ident[0:nr, 0:nr]), reads=[f"xbuf{jb}", "ident"], writes=[f"ps{bank}"])
                    yp = ypre.v(lambda a, nr=nr: a.rearrange("p (t x) -> p t x", t=4)[:, :, 0:nr])
                    hsl = V(hs8[:, dt, th * 4:(th + 1) * 4, j0:j0 + nr], "hT")
                    self.stt(yp, hsl, V(self.dskip[:, dt:dt + 1], "dskip"),
                             V(ps[bank][:, 0:288].rearrange("p (t x) -> p t x", t=4)[:, :, 0:nr], f"ps{bank}"),
                             ALU.mult, ALU.add)
                    self.actv(hsl, yp, AF.Gelu_apprx_tanh)
        r.barrier()

    def glu(self):
        c, r = self.c, self.r
        ps = self.ps
        KT, JP, NCH = c.KT, self.JP, c.NCH
        w_glu = self.inp("w_glu", [c.D, 2 * c.D])
        self.phase_reset()
        gy = self.hreg[:, 0:KT * 8 * JP].rearrange("p (k n) -> p k n", k=KT)
        sig = self.fc(8 * JP)
        ysts = [self.fc(c.NT) for i in range(3)]
        ntl = [(0, 3), (3, 3), (6, 2)]
        gtiles = {}
        for m in range(KT):
            for which in range(2):
                if m % 2 == 0:
                    b = self.nextw()
                    self.load_w(b, w_glu, which * c.D + m * 128, 256, 0)
                    gtiles[which] = b
                b = gtiles[which]
                wb = self.wst[b]
                wc0 = (m % 2) * 128
                for i, (t0, ntt) in enumerate(ntl):
                    bank = which * 3 + i
                    for k in range(KT):
                        r.op("pe", lambda e, k=k, wb=wb, t0=t0, ntt=ntt, bank=bank, wc0=wc0: e.matmul(
                            ps[bank][:, 0:ntt * JP], lhsT=wb[:, k, wc0:wc0 + 128], rhs=gy[:, k, t0 * JP:(t0 + ntt) * JP],
                            start=(k == 0), stop=(k == KT - 1)), reads=[f"wst{b}", "hT"], writes=[f"ps{bank}"])
            yst = ysts[m % 3]
            ystk = f"gyst{m%3}"
            for i, (t0, ntt) in enumerate(ntl):
                r.op("act", lambda e, i=i, t0=t0, ntt=ntt: e.activation(
                    out=sig[:, t0 * JP:(t0 + ntt) * JP], in_=ps[3 + i][:, 0:ntt * JP], func=AF.Sigmoid),
                    reads=[f"ps{3+i}"], writes=["sig"])
                r.op("dve", lambda e, i=i, t0=t0, ntt=ntt, yst=yst: e.tensor_tensor(
                    out=yst[:, 0:c.NT].rearrange("p (j t) -> p t j", t=8)[:, t0:t0 + ntt, :],
                    in0=ps[i][:, 0:ntt * JP].rearrange("p (t j) -> p t j", j=JP)[:, :, 0:NCH],
                    in1=sig[:, t0 * JP:(t0 + ntt) * JP].rearrange("p (t j) -> p t j", j=JP)[:, :, 0:NCH], op=ALU.mult),
                    reads=[f"ps{i}", "sig"], writes=[ystk])
            self.defer(lambda m=m, yst=yst, ystk=ystk: r.dma(
                "pool", self.xres[m * 128:(m + 1) * 128, :], yst[:, :], reads=[ystk], writes=[f"xres{m}"],
                accum_op=ALU.add))
        self.flush()
        r.barrier()

    def load_x(self, name):
        c, r = self.c, self.r
        xin = self.inp(name, [c.D, c.NT])
        for m in range(c.KT):
            xb = self.xbuf[m % 2]
            r.dma("sp", xb[:, 0:c.NT], xin[m * 128:(m + 1) * 128, :], writes=[f"xbuf{m%2}"])
            r.dma("sp", self.xres[m * 128:(m + 1) * 128, :], xb[:, 0:c.NT], reads=[f"xbuf{m%2}"], writes=[f"xres{m}"])
        r.barrier()

    def copy_x_to_xres(self):
        c, r = self.c, self.r
        xT0 = self.inp("xT0", [c.D, c.NKV])
        for m in range(c.KT):
            xb = self.xbuf[m % 2]
            r.dma("sp", xb[:, 0:c.NT], xT0[m * 128:(m + 1) * 128, c.NKV - c.NT:c.NKV], writes=[f"xbuf{m%2}"])
            r.dma("sp", self.xres[m * 128:(m + 1) * 128, :], xb[:, 0:c.NT], reads=[f"xbuf{m%2}"], writes=[f"xres{m}"])
        r.barrier()

    def dump_xres(self, name):
        c, r = self.c, self.r
        o = self.outp(name, [c.D, c.NT])
        for m in range(c.KT):
            xb = self.xbuf[m % 2]
            r.dma("sp", xb[:, 0:c.NT], self.xres[m * 128:(m + 1) * 128, :], reads=[f"xres{m}"], writes=[f"xbuf{m%2}"])
            r.dma("sp", o[m * 128:(m + 1) * 128, :], xb[:, 0:c.NT], reads=[f"xbuf{m%2}"], writes=[name])
        self.final_keys = getattr(self, "final_keys", []) + [name]

    def final_norm(self):
        c, r = self.c, self.r
        ps = self.ps
        o = self.outp("outT", [c.D, c.NOWN])
        ncols = c.NT
        nts = ntiles(ncols)
        src = self.xres
        for k in range(c.KT):
            xb = self.xbuf[k % 2]
            sq = self.sq[k % 2]
            r.dma("sp", xb[:, 0:ncols], src[k * 128:(k + 1) * 128, :], reads=[f"xres{k}"], writes=[f"xbuf{k%2}"])
            r.op("act", lambda e, xb=xb, sq=sq: e.activation(out=sq[:, 0:ncols], in_=xb[:, 0:ncols], func=AF.Square),
                 reads=[f"xbuf{k%2}"], writes=[f"sq{k%2}"])
            for i, (s, n) in enumerate(nts):
                r.op("pe", lambda e, i=i, s=s, n=n, sq=sq, k=k: e.matmul(
                    ps[i][:, 0:n], lhsT=self.ones_bf[:, :], rhs=sq[:, s:s + n],
                    start=(k == 0), stop=(k == c.KT - 1)), reads=[f"sq{k%2}", "ones"], writes=[f"ps{i}"])
        for i, (s, n) in enumerate(nts):
            r.op("dve", lambda e, i=i, s=s, n=n: e.tensor_scalar(
                out=self.rstd[:, s:s + n], in0=ps[i][:, 0:n], scalar1=1.0 / c.D, scalar2=EPS,
                op0=ALU.mult, op1=ALU.add), reads=[f"ps{i}"], writes=["rstd"])
        r.op("act", lambda e: e.sqrt(out=self.rstd[:, 0:ncols], in_=self.rstd[:, 0:ncols]),
             reads=["rstd"], writes=["rstd"])
        r.op("dve", lambda e: e.reciprocal(out=self.rstd[:, 0:ncols], in_=self.rstd[:, 0:ncols]),
             reads=["rstd"], writes=["rstd"])
        gcol = 4 * c.KT
        for k in range(c.KT):
            xb = self.xbuf[k % 2]
            r.dma("sp", xb[:, 0:ncols], src[k * 128:(k + 1) * 128, :], reads=[f"xres{k}"], writes=[f"xbuf{k%2}"])
            r.op("dve", lambda e, xb=xb, k=k: e.scalar_tensor_tensor(
                out=xb[:, 0:ncols], in0=xb[:, 0:ncols], scalar=self.gains[:, gcol + k:gcol + k + 1],
                in1=self.rstd[:, 0:ncols], op0=ALU.mult, op1=ALU.mult),
                reads=[f"xbuf{k%2}", "rstd", "gains"], writes=[f"xbuf{k%2}"])
            r.dma("sp", o[k * 128:(k + 1) * 128, :], xb[:, c.HALO:c.HALO + c.NOWN], reads=[f"xbuf{k%2}"],
                  writes=["outT"])
        self.final_keys = getattr(self, "final_keys", []) + ["outT"]

    def finish(self):
        nc, r = self.nc, self.r
        r.wait_keys("sp", getattr(self, "final_keys", []))
        with contextlib.ExitStack() as st:
            sems = {e: st.enter_context(nc.semaphore("s_" + e)) for e in ENGS}
            dsems = {(e, s): st.enter_context(nc.semaphore(f"d_{e}{s}")) for e in ("sp", "pool", "act")
                     for s in range(NSLOT)}
            dsems[("cc", 0)] = st.enter_context(nc.semaphore("cc_sem"))
            r.emit(nc, sems, dsems)
        return nc


def build(cfg, phases):
    p = Prog(cfg, phases)
    p.setup_common()
    for ph in phases:
        if ph == "attn":
            p.attention()
        elif ph == "copyx":
            p.copy_x_to_xres()
        elif ph == "ffn0":
            p.ffn(0)
        elif ph == "ffn1":
            p.ffn(1)
        elif ph == "final":
            p.final_norm()
        elif ph == "ssm":
            p.ssm_E()
            p.ssm_scan(False, fused=True)
            p.ssm_scan(True, fused=True)
            p.ssm_y()
            p.glu()
        elif ph == "ssmA":
            p.ssm_E()
            p.ssm_scan(False)
        elif ph == "ssmB":
            p.ssm_E()
            p.ssm_scan(True)
            p.ssm_y()
            p.glu()
        elif ph.startswith("loadx:"):
            p.load_x(ph[6:])
        elif ph.startswith("dump:"):
            p.dump_xres(ph[5:])
        else:
            raise ValueError(ph)
    nc = p.finish()
    return p, nc


MASKV = -30000.0


def _t5_buckets(dist):
    n = np.maximum(dist, 0)
    is_small = n < 16
    large = 16 + (np.log(np.maximum(n, 1) / 16) / np.log(128 / 16) * (32 - 16)).astype(np.int32)
    large = np.minimum(large, 31)
    return np.where(is_small, n, large).astype(np.int32)


def _prep_shared(cfg, I):
    D, F, KT, FT, G, NH = cfg.D, cfg.F, cfg.KT, cfg.FT, cfg.G, cfg.NH
    m = {}
    vecs = [I['attn_norm'][0], I['ffn_norm'][0], I['ssm_norm'][0], I['ffn_norm'][1], I['final_norm']]
    m['gains'] = np.ascontiguousarray(np.concatenate([np.asarray(v).reshape(KT, 128).T for v in vecs], axis=1),
                                      dtype=np.float32)
    s = np.arange(128)[:, None]
    q = np.arange(128)[None, :]
    rb = np.asarray(I['rel_bias'])
    bt = np.empty((NH // 2, 128, 4, 128), np.float32)
    for kb in range(2):
        dist = q - s + (128 if kb == 0 else 0)
        valid = (dist >= 0) & (dist < 128)
        bk = _t5_buckets(dist)
        for j in range(NH // 2):
            for hh in range(2):
                bt[j, :, hh * 2 + kb, :] = np.where(valid, rb[bk, 2 * j + hh], np.float32(MASKV))
    m['bias_tab'] = bt.reshape(NH // 2, 128, 512)
    sk = np.asarray(I['sinks'][0])
    s2 = np.empty((128, NH // 2), np.float32)
    s2[:64, :] = sk[0::2][None, :]
    s2[64:, :] = sk[1::2][None, :]
    m['sinks2'] = s2
    m['w_qkv'] = I['w_qkv'][0]
    m['w_o'] = I['w_o'][0]
    for li in range(2):
        cw = np.asarray(I['conv_w'][li])
        cb = np.asarray(I['conv_b'][li])
        rows = [cw[0], cw[1], cw[2], cb]
        cp = np.stack([r_.reshape(2 * FT, 128).T for r_ in rows], axis=1)
        m[f'convp{li}'] = np.ascontiguousarray(cp.reshape(128, 4 * 2 * FT), dtype=np.float32)
        m[f'w_up{li}'] = I['w_up'][li]
        m[f'w_down{li}'] = I['w_down'][li]
    lr = np.asarray(I['lambda_re'][0]); li_ = np.asarray(I['lambda_im'][0]); ls = np.asarray(I['log_step'][0])
    br = np.asarray(I['b_re'][0]); bi = np.asarray(I['b_im'][0])
    cr = np.asarray(I['c_re'][0]); ci = np.asarray(I['c_im'][0])
    OFF = 64.0 * math.pi
    sc = np.zeros((128, 32), np.float32)
    p = np.arange(128)
    sc[:, 0:8] = (p[:, None] // 16 == np.arange(8)[None, :])
    sc[:, 8:16] = 7 - np.arange(8)[None, :]
    sc[:, 16:25] = np.arange(9)[None, :]
    sc[:, 25] = np.where(p < 64, 0.5 * math.pi, math.pi) + OFF
    sc[:, 26] = np.where(p < 64, math.pi, 1.5 * math.pi) + OFF
    sc[:, 27] = np.where(p < 64, -1.0, 1.0)
    m['ssmc'] = sc
    m['ident'] = np.eye(128, dtype=np.float32)
    g_of = (np.arange(KT)[None, :] * 8 + (p[:, None] // 16))
    cidx = p % 16
    m['lamr1'] = np.ascontiguousarray(lr[g_of].reshape(128, KT * 64))
    m['lami1'] = np.ascontiguousarray(li_[g_of].reshape(128, KT * 64))
    m['bre1'] = np.ascontiguousarray(br[g_of, :, cidx[:, None]].reshape(128, KT * 64))
    m['bim1'] = np.ascontiguousarray(bi[g_of, :, cidx[:, None]].reshape(128, KT * 64))
    m['lstep1'] = np.ascontiguousarray(ls[g_of])
    pp = p % 64
    m['lamr2'] = np.ascontiguousarray(lr[:, pp].T)
    m['lami2'] = np.ascontiguousarray(li_[:, pp].T)
    m['lstep2'] = np.ascontiguousarray(np.broadcast_to(ls[None, :], (128, G)), dtype=np.float32)
    crT = cr.transpose(2, 0, 1)
    ciT = ci.transpose(2, 0, 1)
    m['CA2'] = np.ascontiguousarray(np.concatenate([crT, crT], 0).reshape(128, G * 16))
    m['CB2'] = np.ascontiguousarray(np.concatenate([ciT, ciT], 0).reshape(128, G * 16))
    brT = br.transpose(1, 0, 2)
    biT = bi.transpose(1, 0, 2)
    m['BA2'] = np.ascontiguousarray(np.concatenate([brT, biT], 0).reshape(128, G * 16))
    m['BB2'] = np.ascontiguousarray(np.concatenate([biT, brT], 0).reshape(128, G * 16))
    m['lamrS'] = np.ascontiguousarray(lr.reshape(G // 2, 128))
    m['lamiS'] = np.ascontiguousarray(li_.reshape(G // 2, 128))
    m['lstepS'] = np.ascontiguousarray(ls.reshape(G // 2, 2))
    m['dskip'] = np.ascontiguousarray(np.asarray(I['d_skip'][0]).reshape(KT, 128).T)
    m['w_glu'] = I['w_glu'][0]
    return m


def _prep_core(c, cfg, I):
    b = c // 4
    ch = c % 4
    t0 = ch * cfg.NOWN
    m = {}
    xs = np.zeros((cfg.NKV, cfg.D), np.float32)
    lo = t0 - cfg.KVH
    src_lo = max(lo, 0)
    xs[src_lo - lo:, :] = I['x'][b, src_lo:t0 + cfg.NOWN, :]
    m['xT0'] = np.ascontiguousarray(xs.T)
    m['flag'] = np.full((128, 1), 0.0 if ch == 0 else 1.0, np.float32)
    m['negmask'] = np.full((128, 128), MASKV if ch == 0 else 0.0, np.float32)
    w = np.zeros((cfg.G // 2, 8), np.float32)
    if FUSED:
        w[:, 0:ch] = 1.0
    else:
        for r_ in range(8):
            if r_ // 4 == b and r_ % 4 < ch:
                w[:, r_] = 1.0
    m['wsel'] = w
    return m


FUSED = True
PHASES = ["attn", "ffn0", "ssm", "ffn1", "final"]
PHASES1 = ["attn", "ffn0", "ssmA", "dump:x2T"]
PHASES2 = ["loadx:x2T", "ssmB", "ffn1", "final"]


def run_model(cfg, I):
    shared = _prep_shared(cfg, I)
    cores = [_prep_core(c, cfg, I) for c in range(8)]
    if FUSED:
        p, nc = build(cfg, PHASES)
        maps = [{k: (cores[c][k] if k in cores[c] else shared[k]) for k in p.din} for c in range(8)]
        res = run_bass_kernel_spmd(nc, maps, core_ids=list(range(8))).results
        out = np.empty(I['x'].shape, np.float32)
        for c in range(8):
            out[c // 4, (c % 4) * cfg.NOWN:(c % 4 + 1) * cfg.NOWN, :] = np.asarray(res[c]['outT']).T
        return out
    p1, nc1 = build(cfg, PHASES1)
    maps1 = [{k: (cores[c][k] if k in cores[c] else shared[k]) for k in p1.din} for c in range(8)]
    r1 = run_bass_kernel_spmd(nc1, maps1, core_ids=list(range(8))).results
    del maps1
    Lall = np.stack([np.asarray(r1[c]['Lout']) for c in range(8)])
    p2, nc2 = build(cfg, PHASES2)
    maps2 = []
    for c in range(8):
        mm = {}
        for k in p2.din:
            if k == 'x2T':
                mm[k] = np.asarray(r1[c]['x2T'])
            elif k == 'Lall':
                mm[k] = Lall
            elif k in cores[c]:
                mm[k] = cores[c][k]
            else:
                mm[k] = shared[k]
        maps2.append(mm)
    r2 = run_bass_kernel_spmd(nc2, maps2, core_ids=list(range(8))).results
    B = I['x'].shape[0]
    out = np.empty(I['x'].shape, np.float32)
    for c in range(8):
        b = c // 4
        ch = c % 4
        out[b, ch * cfg.NOWN:(ch + 1) * cfg.NOWN, :] = np.asarray(r2[c]['outT']).T
    return out


def kernel(**inputs):
    I = {k: np.asarray(v) for k, v in inputs.items()}
    cfg = Cfg(D=I['x'].shape[2], F=I['w_down'].shape[1])
    return run_model(cfg, I)
```
